# Optimizing a Trainium2 kernel written in Bass

```python
import math
import jax, jax.numpy as jnp
from jax import lax
import numpy as np

D_MODEL = 2048
BATCH = 1
SEQ = 16384
DEPTH = 1
DEC_BATCH = 4
DEC_SEQ = 4096
PAST_LEN = 128

GRID_W = 64
NA_HEADS = 8
NA_HEAD_DIM = 128
NA_WIDTH = NA_HEADS * NA_HEAD_DIM
NA_MAX_KH = 8
NA_KW = 16
NA_QB = 16
NA_KB = NA_QB + NA_KW
NEG_INF = -1e30
LRU_WIDTH = 1024
LRU_BLOCKS = 8
LRU_BLOCK = LRU_WIDTH // LRU_BLOCKS
CONV_W = 4
LRU_C = 8.0
PEER_HEADS = 8
PEER_NKEYS = 128
PEER_N = PEER_NKEYS * PEER_NKEYS
PEER_DK = 256
PEER_TOPK = 16
PEER_CHUNK = 128
PLE_DIM = 256
RMS_EPS = 1e-6

IN_SPLITS = [NA_WIDTH, 2 * NA_WIDTH, 3 * NA_WIDTH, 3 * NA_WIDTH + LRU_WIDTH,
             3 * NA_WIDTH + 2 * LRU_WIDTH, 3 * NA_WIDTH + 2 * LRU_WIDTH + D_MODEL]
IN_WIDTH = 3 * NA_WIDTH + 2 * LRU_WIDTH + 2 * D_MODEL

kernel_name = 'hybrid_na_rglru_peer_encoder'


def rms_norm(x, g):
    xf = x.astype(jnp.float32)
    y = xf * lax.rsqrt(jnp.mean(xf * xf, axis=-1, keepdims=True) + RMS_EPS)
    return (y * g.astype(jnp.float32)).astype(x.dtype)


def neighbourhood_attention(q, k, v, rpb):
    B, S = q.shape[0], q.shape[1]
    rows = S // GRID_W
    kh = min(NA_MAX_KH, rows)
    to_grid = lambda t: t.reshape(B, rows, GRID_W, NA_HEADS, NA_HEAD_DIM)
    qg, kg, vg = to_grid(q), to_grid(k), to_grid(v)
    r = jnp.arange(rows)
    row_start = jnp.clip(r - kh // 2, 0, rows - kh)
    key_rows = row_start[:, None] + jnp.arange(kh)[None, :]
    dr_idx = key_rows - r[:, None] + (NA_MAX_KH - 1)
    scale = NA_HEAD_DIM ** -0.5
    outs = []
    for j in range(GRID_W // NA_QB):
        q0 = j * NA_QB
        k0 = min(max(q0 - NA_KW // 2, 0), GRID_W - NA_KB)
        qc = np.arange(q0, q0 + NA_QB)
        kc = np.arange(k0, k0 + NA_KB)
        col_start = np.clip(qc - NA_KW // 2, 0, GRID_W - NA_KW)
        col_ok = (kc[None, :] >= col_start[:, None]) & (kc[None, :] < col_start[:, None] + NA_KW)
        dc_idx = np.clip(kc[None, :] - qc[:, None], -(NA_KW - 1), NA_KW - 1) + (NA_KW - 1)
        qb = qg[:, :, q0:q0 + NA_QB]
        kb = kg[:, :, k0:k0 + NA_KB][:, key_rows]
        vb = vg[:, :, k0:k0 + NA_KB][:, key_rows]
        s = jnp.einsum('brqhd,brikhd->bhrqik', qb, kb, preferred_element_type=jnp.float32) * scale
        bias = rpb[:, dr_idx[:, None, :, None], jnp.asarray(dc_idx)[None, :, None, :]]
        s = s + bias.astype(jnp.float32)[None]
        s = jnp.where(jnp.asarray(col_ok)[None, None, None, :, None, :], s, NEG_INF)
        probs = jax.nn.softmax(s.reshape(B, NA_HEADS, rows, NA_QB, kh * NA_KB), axis=-1).reshape(s.shape)
        outs.append(jnp.einsum('bhrqik,brikhd->brqhd', probs.astype(v.dtype), vb))
    out = jnp.concatenate(outs, axis=2)
    return out.reshape(B, S, NA_WIDTH)


def depthwise_conv_centred(x, w, b):
    C = x.shape[-1]
    y = lax.conv_general_dilated(
        x, w[:, None, :].astype(x.dtype), window_strides=(1,),
        padding=[(CONV_W // 2, CONV_W - 1 - CONV_W // 2)],
        dimension_numbers=('NWC', 'WIO', 'NWC'), feature_group_count=C)
    return y + b.astype(x.dtype)


def rg_lru_scan(xf, wa, ba, wx, bx, lam, reverse):
    B, S, C = xf.shape
    xb = xf.reshape(B, S, LRU_BLOCKS, LRU_BLOCK)
    rec = jax.nn.sigmoid(jnp.einsum('bsnc,ncd->bsnd', xb, wa.astype(jnp.float32)).reshape(B, S, C) + ba.astype(jnp.float32))
    inp = jax.nn.sigmoid(jnp.einsum('bsnc,ncd->bsnd', xb, wx.astype(jnp.float32)).reshape(B, S, C) + bx.astype(jnp.float32))
    log_a = -LRU_C * rec * jax.nn.softplus(-lam.astype(jnp.float32))
    a = jnp.exp(log_a)
    u = jnp.sqrt(-jnp.expm1(2.0 * log_a)) * (inp * xf)

    def combine(e1, e2):
        a1, b1 = e1
        a2, b2 = e2
        return a1 * a2, a2 * b1 + b2

    _, h = lax.associative_scan(combine, (a, u), axis=1, reverse=reverse)
    return h


def bidir_rg_lru(x, wa, ba, wx, bx, lam):
    xf = x.astype(jnp.float32)
    fwd = rg_lru_scan(xf, wa[0], ba[0], wx[0], bx[0], lam[0], False)
    bwd = rg_lru_scan(xf, wa[1], ba[1], wx[1], bx[1], lam[1], True)
    return (fwd + bwd).astype(x.dtype)


def peer(x, wq, subkeys, u, v):
    B, S, D = x.shape
    T = B * S
    xt = x.reshape(T, D)
    q = (xt @ wq).reshape(T, PEER_HEADS, 2, PEER_DK // 2)
    s = jnp.einsum('thpd,hpkd->thpk', q, subkeys, preferred_element_type=jnp.float32)
    sv, si = lax.top_k(s, PEER_TOPK)
    cand = sv[:, :, 0, :, None] + sv[:, :, 1, None, :]
    cv, ci = lax.top_k(cand.reshape(T, PEER_HEADS, PEER_TOPK * PEER_TOPK), PEER_TOPK)
    i1 = jnp.take_along_axis(si[:, :, 0], ci // PEER_TOPK, axis=-1)
    i2 = jnp.take_along_axis(si[:, :, 1], ci % PEER_TOPK, axis=-1)
    ids = i1 * PEER_NKEYS + i2
    g = jax.nn.softmax(cv, axis=-1)
    n_sel = PEER_HEADS * PEER_TOPK
    n_chunk = T // PEER_CHUNK

    def chunk(args):
        xc, idc, gc = args
        hc = jax.nn.gelu(jnp.einsum('cnd,cd->cn', u[idc], xc, preferred_element_type=jnp.float32), approximate=False)
        return jnp.einsum('cn,cnd->cd', (gc * hc).astype(x.dtype), v[idc]).astype(x.dtype)

    y = lax.map(chunk, (xt.reshape(n_chunk, PEER_CHUNK, D),
                        ids.reshape(n_chunk, PEER_CHUNK, n_sel),
                        g.reshape(n_chunk, PEER_CHUNK, n_sel)))
    return y.reshape(B, S, D)


def trunk(x, p, w):
    B, S = x.shape[0], x.shape[1]
    h = x
    for l in range(DEPTH):
        xn = rms_norm(h, w['norm_mix'][l])
        z = xn @ w['w_in'][l]
        q, k, vv, xr, gr, ga, gb = jnp.split(z, IN_SPLITS, axis=-1)
        heads = lambda t: t.reshape(B, S, NA_HEADS, NA_HEAD_DIM)
        y_a = neighbourhood_attention(heads(q), heads(k), heads(vv), w['na_rpb'][l])
        xc = depthwise_conv_centred(xr, w['conv_w'][l], w['conv_b'][l])
        y_r = bidir_rg_lru(xc, w['lru_wa'][l], w['lru_ba'][l], w['lru_wx'][l], w['lru_bx'][l], w['lru_lambda'][l]) * jax.nn.gelu(gr)
        merged = jax.nn.sigmoid(ga) * (y_a @ w['w_branch_a'][l]) + jax.nn.sigmoid(gb) * (y_r @ w['w_branch_r'][l])
        h = h + (merged @ w['w_out'][l]).astype(h.dtype)
        h = h + peer(rms_norm(h, w['norm_ffn'][l]), w['peer_wq'][l], w['peer_subkeys'][l], w['peer_u'][l], w['peer_v'][l]).astype(h.dtype)
        gate = jax.nn.sigmoid(rms_norm(h, w['norm_ple'][l]) @ w['ple_gate_w'][l])
        h = h + (gate * (p[l] @ w['ple_proj_w'][l])).astype(h.dtype)
    return rms_norm(h, w['final_norm'])


def setup_inputs(seed: int = 0) -> dict:
    key = jax.random.key(seed)
    ks = jax.random.split(key, 32)
    f32 = jnp.float32
    nrm = lambda kk, shape, sc: jax.random.normal(kk, shape, f32) * sc
    gain = lambda kk, shape: 1.0 + 0.05 * jax.random.normal(kk, shape, f32)
    a_pow = jax.random.uniform(ks[10], (DEPTH, 2, LRU_WIDTH), f32, 0.9, 0.999)
    a_base = a_pow ** (1.0 / LRU_C)
    lru_lambda = jnp.log(a_base) - jnp.log1p(-a_base)
    return {
        'x_prompt': nrm(ks[0], (BATCH, SEQ, D_MODEL), 1.0),
        'x_sample': nrm(ks[1], (DEC_BATCH, DEC_SEQ, D_MODEL), 1.0),
        'p_prompt': nrm(ks[2], (DEPTH, BATCH, SEQ, PLE_DIM), 1.0),
        'p_sample': nrm(ks[3], (DEPTH, DEC_BATCH, DEC_SEQ, PLE_DIM), 1.0),
        'norm_mix': gain(ks[4], (DEPTH, D_MODEL)),
        'w_in': nrm(ks[5], (DEPTH, D_MODEL, IN_WIDTH), D_MODEL ** -0.5),
        'na_rpb': nrm(ks[6], (DEPTH, NA_HEADS, 2 * NA_MAX_KH - 1, 2 * NA_KW - 1), 0.1),
        'conv_w': nrm(ks[7], (DEPTH, CONV_W, LRU_WIDTH), CONV_W ** -0.5),
        'conv_b': nrm(ks[8], (DEPTH, LRU_WIDTH), 0.01),
        'lru_wa': nrm(ks[9], (DEPTH, 2, LRU_BLOCKS, LRU_BLOCK, LRU_BLOCK), LRU_BLOCK ** -0.5),
        'lru_ba': nrm(ks[11], (DEPTH, 2, LRU_WIDTH), 0.01),
        'lru_wx': nrm(ks[12], (DEPTH, 2, LRU_BLOCKS, LRU_BLOCK, LRU_BLOCK), LRU_BLOCK ** -0.5),
        'lru_bx': nrm(ks[13], (DEPTH, 2, LRU_WIDTH), 0.01),
        'lru_lambda': lru_lambda,
        'w_branch_a': nrm(ks[14], (DEPTH, NA_WIDTH, D_MODEL), NA_WIDTH ** -0.5),
        'w_branch_r': nrm(ks[15], (DEPTH, LRU_WIDTH, D_MODEL), LRU_WIDTH ** -0.5),
        'w_out': nrm(ks[16], (DEPTH, D_MODEL, D_MODEL), D_MODEL ** -0.5),
        'norm_ffn': gain(ks[17], (DEPTH, D_MODEL)),
        'peer_wq': nrm(ks[18], (DEPTH, D_MODEL, PEER_HEADS * PEER_DK), D_MODEL ** -0.5),
        'peer_subkeys': nrm(ks[19], (DEPTH, PEER_HEADS, 2, PEER_NKEYS, PEER_DK // 2), (PEER_DK // 2) ** -0.5),
        'peer_u': nrm(ks[20], (DEPTH, PEER_N, D_MODEL), D_MODEL ** -0.5),
        'peer_v': nrm(ks[21], (DEPTH, PEER_N, D_MODEL), (PEER_HEADS * PEER_TOPK) ** -0.5),
        'norm_ple': gain(ks[22], (DEPTH, D_MODEL)),
        'ple_gate_w': nrm(ks[23], (DEPTH, D_MODEL, D_MODEL), D_MODEL ** -0.5),
        'ple_proj_w': nrm(ks[24], (DEPTH, PLE_DIM, D_MODEL), PLE_DIM ** -0.5),
        'final_norm': gain(ks[25], (D_MODEL,)),
    }


def reference(x_prompt, x_sample, p_prompt, p_sample, norm_mix, w_in, na_rpb, conv_w, conv_b,
              lru_wa, lru_ba, lru_wx, lru_bx, lru_lambda, w_branch_a, w_branch_r, w_out,
              norm_ffn, peer_wq, peer_subkeys, peer_u, peer_v, norm_ple, ple_gate_w, ple_proj_w,
              final_norm):
    w = dict(norm_mix=norm_mix, w_in=w_in, na_rpb=na_rpb, conv_w=conv_w, conv_b=conv_b,
             lru_wa=lru_wa, lru_ba=lru_ba, lru_wx=lru_wx, lru_bx=lru_bx, lru_lambda=lru_lambda,
             w_branch_a=w_branch_a, w_branch_r=w_branch_r, w_out=w_out, norm_ffn=norm_ffn,
             peer_wq=peer_wq, peer_subkeys=peer_subkeys, peer_u=peer_u, peer_v=peer_v,
             norm_ple=norm_ple, ple_gate_w=ple_gate_w, ple_proj_w=ple_proj_w, final_norm=final_norm)
    y_prompt = trunk(x_prompt, p_prompt, w)
    y_sample = trunk(x_sample, p_sample, w)
    return (y_prompt, y_sample)
```

```python
import numpy as np
import ml_dtypes
from contextlib import ExitStack
import concourse.bass as bass
import concourse.mybir as mybir
from concourse.bass_utils import run_bass_kernel_spmd

F32 = mybir.dt.float32
BF16 = mybir.dt.bfloat16
U32 = mybir.dt.uint32
I32 = mybir.dt.int32
AF = mybir.ActivationFunctionType
ALU = mybir.AluOpType
AX = mybir.AxisListType


class Sched:
    ENG = ['pe', 'act', 'dve', 'pool', 'sp']

    def __init__(self, nc, es, n_dsem=40):
        self.nc = nc
        self.q = {e: [] for e in self.ENG}
        self.csem = {e: es.enter_context(nc.semaphore(f"c_{e}")) for e in ['pe', 'act', 'dve', 'pool']}
        self.cnt = {e: 0 for e in self.csem}
        self.dsem = [es.enter_context(nc.semaphore(f"d_{i}")) for i in range(n_dsem)]
        self.dval = [0] * n_dsem
        self.dnext = 0
        self.waited = {e: {} for e in self.ENG}
        self.lastw = {}
        self.readers = {}
        self.ninstr = 0

    def _semobj(self, semkey):
        return self.csem[semkey] if isinstance(semkey, str) else self.dsem[semkey]

    def _wait(self, eng, tok):
        semkey, val = tok
        if eng == 'pe' and semkey == 'pe':
            return
        if self.waited[eng].get(semkey, 0) >= val:
            return
        self.waited[eng][semkey] = val
        sem = self._semobj(semkey)
        self.q[eng].append(lambda e, sem=sem, val=val: e.wait_ge(sem, val))
        self.ninstr += 1

    def _deps(self, eng, reads, writes):
        toks = []
        for k in list(reads) + list(writes):
            if k in self.lastw:
                toks.append(self.lastw[k])
        for k in writes:
            for sk, v in self.readers.get(k, {}).items():
                toks.append((sk, v))
        for t in toks:
            self._wait(eng, t)

    def _commit(self, tok, reads, writes):
        for k in writes:
            self.lastw[k] = tok
            self.readers[k] = {}
        for k in reads:
            d = self.readers.setdefault(k, {})
            d[tok[0]] = max(d.get(tok[0], 0), tok[1])

    def op(self, eng, fn, reads=(), writes=()):
        self._deps(eng, reads, writes)
        self.cnt[eng] += 1
        tok = (eng, self.cnt[eng])
        sem = self.csem[eng]
        self.q[eng].append(lambda e, fn=fn, sem=sem: fn(e).then_inc(sem, 1))
        self.ninstr += 1
        self._commit(tok, reads, writes)

    def dma(self, eng, fn, reads=(), writes=()):
        self._deps(eng, reads, writes)
        k = self.dnext
        self.dnext = (self.dnext + 1) % len(self.dsem)
        if self.dval[k] > 0:
            self._wait(eng, (k, self.dval[k]))
        self.dval[k] += 16
        tok = (k, self.dval[k])
        sem = self.dsem[k]
        self.q[eng].append(lambda e, fn=fn, sem=sem: fn(e).then_inc(sem, 16))
        self.ninstr += 1
        self._commit(tok, reads, writes)

    def barrier(self):
        for e in self.ENG:
            for e2 in self.csem:
                if self.cnt[e2] > 0:
                    self._wait(e, (e2, self.cnt[e2]))
            for k in range(len(self.dsem)):
                if self.dval[k] > 0:
                    self._wait(e, (k, self.dval[k]))
        self.lastw = {}
        self.readers = {}

    def flush(self):
        nc = self.nc
        q = self.q
        with nc.Block() as block:
            @block.tensor
            def _(e):
                for f in q['pe']:
                    f(e)

            @block.scalar
            def _(e):
                for f in q['act']:
                    f(e)

            @block.vector
            def _(e):
                for f in q['dve']:
                    f(e)

            @block.gpsimd
            def _(e):
                for f in q['pool']:
                    f(e)

            @block.sync
            def _(e):
                for f in q['sp']:
                    f(e)
        self.q = {e: [] for e in self.ENG}


D = 2048
NTOK = 4096
TS = 512
NT = NTOK // TS
EPS = 1e-6
QSCALE = 128 ** -0.5


def build_nc(debug=False):
    nc = bass.Bass("TRN2", target_bir_lowering=False)

    def din(name, shape, dt=F32):
        return nc.dram_tensor(name, shape, dt, kind="ExternalInput").ap()

    def dscr(name, shape, dt=F32):
        kind = "ExternalOutput" if debug in (1, 2, 3) else "Internal"
        return nc.dram_tensor(name, shape, dt, kind=kind).ap()

    x_own = din("x_own", [NTOK, D])
    x_halo = din("x_halo", [512, D])
    x_ext = din("x_ext", [3, 4100, D])
    pT = din("pT", [256, NTOK])
    gains = din("gains", [4, D])
    w_in = din("w_in", [D, 9216])
    w_a = din("w_a", [1024, D])
    w_r = din("w_r", [1024, D])
    w_out = din("w_out", [D, D])
    w_q = din("w_q", [D, D])
    w_pg = din("w_pg", [D, D])
    w_pp = din("w_pp", [256, D])
    skT = din("skT", [16, 128, 128])
    peer_u = din("peer_u", [16384, D])
    peer_v = din("peer_v", [16384, D])
    natab = din("natab", [5, 8, 128, 640])
    cwb = din("cwb", [4, 128, 8, 6])
    lwa = din("lwa", [5, 8, 128, 128])
    lwx = din("lwx", [5, 8, 128, 128])
    lsm = din("lsm", [5, 128, 8, 3])
    flags = din("flags", [128, 16])
    ident = din("ident", [128, 128])
    y_own = nc.dram_tensor("y_own", [NTOK, D], F32, kind="ExternalOutput").ap()

    QT = dscr("QT", [1024, NTOK], BF16)
    KT = dscr("KT", [1024, NTOK + 512], BF16)
    VV = dscr("VV", [NTOK + 512, 1024], BF16)
    XR = dscr("XR", [1024, 4100])
    XRE = dscr("XRE", [3, 1024, 4100])
    GG = dscr("GG", [1024, NTOK])
    HF = dscr("HF", [1024, NTOK])
    YA = dscr("YA", [1024, NTOK], BF16)
    YR = dscr("YR", [1024, NTOK], BF16)
    UBF = nc.dram_tensor("UBF", [16384, D], BF16).ap()
    WIN1 = nc.dram_tensor("WIN1", [10, 128, 16, 512], BF16).ap()
    WIN2 = nc.dram_tensor("WIN2", [16, 128, 16, 256], BF16).ap()
    WAT = nc.dram_tensor("WAT", [8, 128, 8, 256], BF16).ap()
    WRT = nc.dram_tensor("WRT", [8, 128, 8, 256], BF16).ap()
    WOT = nc.dram_tensor("WOT", [8, 128, 16, 256], BF16).ap()
    WQT = nc.dram_tensor("WQT", [8, 128, 16, 256], BF16).ap()
    WPGT = nc.dram_tensor("WPGT", [8, 128, 16, 256], BF16).ap()
    WPPT = nc.dram_tensor("WPPT", [8, 128, 2, 256], BF16).ap()
    VBF = nc.dram_tensor("VBF", [16384, D], BF16).ap()

    with ExitStack() as es:
        S = Sched(nc, es)

        def sbuf(st, name, shape, dt=F32):
            return st.enter_context(nc.sbuf_tensor(name, shape, dt))

        def psum(st, name, shape, dt=F32):
            return st.enter_context(nc.psum_tensor(name, shape, dt))

        identf = sbuf(es, "identf", [128, 128])
        identb = sbuf(es, "identb", [128, 128], BF16)
        S.dma('sp', lambda e: e.dma_start(out=identf[:], in_=ident), writes=['identf'])
        S.op('dve', lambda e: e.tensor_copy(out=identb[:], in_=identf[:]), reads=['identf'], writes=['identb'])

        cnt = {'evac': 0}

        def evac_copy(out_ap, in_ap, reads, writes, scale=None, func=None):
            cnt['evac'] += 1
            if func is not None:
                S.op('act', lambda e: e.activation(out=out_ap, in_=in_ap, func=func), reads=reads, writes=writes)
            elif scale is not None:
                if cnt['evac'] % 2:
                    S.op('act', lambda e: e.mul(out=out_ap, in_=in_ap, mul=scale) if False else e.activation(out=out_ap, in_=in_ap, func=AF.Copy, scale=scale), reads=reads, writes=writes)
                else:
                    S.op('dve', lambda e: e.tensor_scalar(out=out_ap, in0=in_ap, scalar1=scale, scalar2=None, op0=ALU.mult), reads=reads, writes=writes)
            else:
                if cnt['evac'] % 2:
                    S.op('act', lambda e: e.copy(out=out_ap, in_=in_ap), reads=reads, writes=writes)
                else:
                    S.op('dve', lambda e: e.tensor_copy(out=out_ap, in_=in_ap), reads=reads, writes=writes)

        def load_gain(G, k):
            S.dma('sp', lambda e: e.dma_start(out=G[:], in_=gains[k:k + 1, :].partition_broadcast(128)), writes=['G', 'J0', 'J1'])

        def rms_norm_blocks(XT, XB, G, SM, nblk=4, stats_only=False, xbkey='XB'):
            for b in range(nblk):
                S.op('act', lambda e, b=b: e.activation(out=XB[:, b, :], in_=XT[:, b, :], func=AF.Square, accum_out=SM[:, b:b + 1]),
                     reads=['XT'], writes=[xbkey, 'SM'])
            S.op('dve', lambda e: e.tensor_scalar(out=SM[:, 4:4 + nblk], in0=SM[:, 0:nblk], scalar1=1.0 / D, scalar2=EPS, op0=ALU.mult, op1=ALU.add),
                 reads=['SM'], writes=['SM'])
            S.op('act', lambda e: e.activation(out=SM[:, 8:8 + nblk], in_=SM[:, 4:4 + nblk], func=AF.Sqrt), reads=['SM'], writes=['SM'])
            S.op('dve', lambda e: e.reciprocal(out=SM[:, 12:12 + nblk], in_=SM[:, 8:8 + nblk]), reads=['SM'], writes=['SM'])
            for b in range(0 if stats_only else nblk):
                S.op('dve', lambda e, b=b: e.scalar_tensor_tensor(out=XB[:, b, :], in0=XT[:, b, :], scalar=SM[:, 12 + b:13 + b], in1=G[:],
                                                                  op0=ALU.mult, op1=ALU.mult),
                     reads=['XT', 'SM', 'G'], writes=[xbkey])

        def transpose_blocks(XB, XNT, PTs, nblk=4, xbkey='XB'):
            for c in range(16):
                pt = PTs[c % 2]
                key = f'pt{c % 2}'
                for b in range(nblk):
                    S.op('pe', lambda e, b=b, c=c, pt=pt: e.transpose(pt[:, b * 128:(b + 1) * 128], XB[:, b, c * 128:(c + 1) * 128], identb[:]),
                         reads=[xbkey, 'identb'], writes=[key])
                evac_copy(XNT[:, c, 0:nblk * 128], pt[:, 0:nblk * 128], reads=[key], writes=['XNT'])


        with ExitStack() as ph:
            CV0 = [sbuf(ph, f"p0_CV{i}", [128, 8192], BF16) for i in range(2)]
            p0 = {'k': 0}

            def conv_w(w, c_lo, nr, cchunk, T, cw, g_off=0):
                K = w.shape[0] // 128
                ncols = T.shape[0] * cw - g_off * cw
                wv = w.rearrange("(r p) c -> p r c", p=128)
                Tv = T.rearrange("g p kc j -> p kc g j")
                for r0 in range(0, K, nr):
                    for c0 in range(0, ncols, cchunk):
                        j = p0['k'] % 2
                        p0['k'] += 1
                        ng = cchunk // cw
                        stg = CV0[j][:, 0:nr * cchunk]
                        S.dma('pool', lambda e, r0=r0, c0=c0, stg=stg: e.dma_start(out=stg.rearrange("p (r c) -> p r c", r=nr),
                                                                                 in_=wv[:, r0:r0 + nr, c_lo + c0:c_lo + c0 + cchunk]), writes=[f'cv{j}'])
                        g0 = g_off + c0 // cw
                        for gi in range(ng):
                            S.dma('sp', lambda e, r0=r0, g0=g0, gi=gi, ng=ng, stg=stg: e.dma_start(out=Tv[:, r0:r0 + nr, g0 + gi, :],
                                                                                               in_=stg.rearrange("p (r g j) -> p r g j", r=nr, g=ng)[:, :, gi, :]), reads=[f'cv{j}'])

            conv_w(w_in, 0, 8, 1024, WIN1, 512)
            conv_w(w_in, 5120, 8, 1024, WIN2, 256)
            conv_w(w_a, 0, 8, 1024, WAT, 256)
            conv_w(w_r, 0, 8, 1024, WRT, 256)
            conv_w(w_out, 0, 8, 1024, WOT, 256)
            conv_w(w_q, 0, 8, 1024, WQT, 256)
            conv_w(w_pg, 0, 8, 1024, WPGT, 256)
            conv_w(w_pp, 0, 2, 2048, WPPT, 256)
            S.barrier()
            S.flush()

        with ExitStack() as ph:
            XT = sbuf(ph, "p1_XT", [128, 4, D])
            XB = sbuf(ph, "p1_XB", [128, 4, D], BF16)
            XNT = sbuf(ph, "p1_XNT", [128, 16, TS], BF16)
            WG = [sbuf(ph, f"p1_WG{i}", [128, 16, 512], BF16) for i in range(2)]
            WGX = sbuf(ph, "p1_WGX", [128, 16, 1024], BF16)
            STB = [sbuf(ph, f"p1_STB{i}", [128, 4, 512], BF16) for i in range(2)]
            STF = [sbuf(ph, f"p1_STF{i}", [128, 4, 512]) for i in range(2)]
            G = sbuf(ph, "p1_G", [128, D])
            SM = sbuf(ph, "p1_SM", [128, 16])
            PSB = [psum(ph, f"p1_ps{i}", [128, 512]) for i in range(4)]
            PTf = [psum(ph, f"p1_pt{i}", [128, 512]) for i in range(2)]
            PTs = [p[:, :].bitcast(BF16) for p in PTf]
            load_gain(G, 0)
            st = {'wg': 0, 'ps': 0, 'stb': 0, 'stf': 0, 'cv': 0}
            CVB = [sbuf(ph, f"p1_CV{i}", [128, 2, D], BF16) for i in range(2)]

            def conv_steps(n):
                for _ in range(n):
                    k = st['cv']
                    if k >= 128:
                        return
                    st['cv'] += 1
                    src, dst = (peer_u, UBF) if k < 64 else (peer_v, VBF)
                    kk = k % 64
                    j = k % 2
                    sv = src.rearrange("(r p) d -> p r d", p=128)
                    dv = dst.rearrange("(r p) d -> p r d", p=128)
                    S.dma('pool', lambda e, sv=sv, kk=kk, j=j: e.dma_start(out=CVB[j][:, :, :], in_=sv[:, kk * 2:(kk + 1) * 2, :]), writes=[f'cv{j}'])
                    S.dma('sp', lambda e, dv=dv, kk=kk, j=j: e.dma_start(out=dv[:, kk * 2:(kk + 1) * 2, :], in_=CVB[j][:, :, :]), reads=[f'cv{j}'])

            def load_x(src_ap, nrows):
                nb = (nrows + 127) // 128
                if nrows % 128 == 0:
                    S.dma('sp', lambda e: e.dma_start(out=XT[:, 0:nb, :], in_=src_ap.rearrange("(b p) d -> p b d", p=128)), writes=['XT'])
                else:
                    S.op('dve', lambda e: e.memset(XT[:, 0, :], 0.0), writes=['XT'])
                    S.dma('sp', lambda e: e.dma_start(out=XT[0:nrows, 0, :], in_=src_ap), writes=['XT'])
                return nb

            def fm_group(wt, wkey, wc0, nb, kind, dst_fn):
                ntok = nb * 128
                if kind in ('xr', 'gg'):
                    stg = STF[st['stf'] % 2]; skey = f"stf{st['stf'] % 2}"; st['stf'] += 1
                else:
                    stg = STB[st['stb'] % 2]; skey = f"stb{st['stb'] % 2}"; st['stb'] += 1
                for oc in range(4):
                    ps = PSB[st['ps'] % 4]; pkey = f"ps{st['ps'] % 4}"; st['ps'] += 1
                    for kc in range(16):
                        S.op('pe', lambda e, ps=ps, kc=kc, oc=oc: e.matmul(ps[:, 0:ntok], wt[:, kc, wc0 + oc * 128: wc0 + (oc + 1) * 128], XNT[:, kc, 0:ntok],
                                                                         start=(kc == 0), stop=(kc == 15)),
                             reads=[wkey, 'XNT'], writes=[pkey])
                    if kind == 'q':
                        evac_copy(stg[:, oc, 0:ntok], ps[:, 0:ntok], [pkey], [skey], scale=QSCALE)
                    elif kind == 'gg':
                        evac_copy(stg[:, oc, 0:ntok], ps[:, 0:ntok], [pkey], [skey], func=AF.Gelu)
                    else:
                        evac_copy(stg[:, oc, 0:ntok], ps[:, 0:ntok], [pkey], [skey])
                dst_fn(stg, skey)

            def tm_group(wt, wkey, nb, dst_fn):
                stg = STB[st['stb'] % 2]; skey = f"stb{st['stb'] % 2}"; st['stb'] += 1
                for b in range(nb):
                    ps = PSB[st['ps'] % 4]; pkey = f"ps{st['ps'] % 4}"; st['ps'] += 1
                    for kc in range(16):
                        S.op('pe', lambda e, ps=ps, kc=kc, b=b: e.matmul(ps[:, :], XNT[:, kc, b * 128:(b + 1) * 128], wt[:, kc, 0:512],
                                                                       start=(kc == 0), stop=(kc == 15)),
                             reads=[wkey, 'XNT'], writes=[pkey])
                    evac_copy(stg[:, b, :], ps[:, :], [pkey], [skey])
                dst_fn(stg, skey)

            def load_wg(col0):
                i = st['wg'] % 2; st['wg'] += 1
                S.dma('sp', lambda e: e.dma_start(out=WG[i][:, :, :], in_=WIN1[col0 // 512]), writes=[f'wg{i}'])
                return WG[i], f'wg{i}'

            def fm_store(dst, r0, t0, ntok):
                def f(stg, skey):
                    S.dma('sp', lambda e: e.dma_start(out=dst[r0:r0 + 512, t0:t0 + ntok].rearrange("(c p) t -> p c t", p=128), in_=stg[:, :, 0:ntok]),
                          reads=[skey])
                return f

            def tm_store(dst, t0, c0, nb):
                def f(stg, skey):
                    S.dma('sp', lambda e: e.dma_start(out=dst[t0:t0 + nb * 128, c0:c0 + 512].rearrange("(b p) c -> p b c", p=128), in_=stg[:, 0:nb, :]),
                          reads=[skey])
                return f

            def halo_k_store(half):
                def f(stg, skey):
                    S.dma('sp', lambda e: e.dma_start(out=KT[half:half + 512, 0:256].rearrange("(c p) t -> p c t", p=128), in_=stg[:, :, 0:256]),
                          reads=[skey])
                    S.dma('sp', lambda e: e.dma_start(out=KT[half:half + 512, 256 + NTOK:512 + NTOK].rearrange("(c p) t -> p c t", p=128), in_=stg[:, :, 256:512]),
                          reads=[skey])
                return f

            def halo_v_store(half):
                def f(stg, skey):
                    S.dma('sp', lambda e: e.dma_start(out=VV[0:256, half:half + 512].rearrange("(b p) c -> p b c", p=128), in_=stg[:, 0:2, :]),
                          reads=[skey])
                    S.dma('sp', lambda e: e.dma_start(out=VV[256 + NTOK:512 + NTOK, half:half + 512].rearrange("(b p) c -> p b c", p=128), in_=stg[:, 2:4, :]),
                          reads=[skey])
                return f

            def halo_xr_store(half):
                def f(stg, skey):
                    S.dma('sp', lambda e: e.dma_start(out=XR[half:half + 512, 0:2].rearrange("(c p) t -> p c t", p=128), in_=stg[:, :, 254:256]),
                          reads=[skey])
                    S.dma('sp', lambda e: e.dma_start(out=XR[half:half + 512, 4098:4100].rearrange("(c p) t -> p c t", p=128), in_=stg[:, :, 256:258]),
                          reads=[skey])
                return f


            XB2 = [XB, sbuf(ph, "p1_XBb", [128, 4, D], BF16)]
            tiles = [('own', i) for i in range(NT)] + [('halo', 0)] + [('ext', s, i) for s in range(3) for i in range(9)]

            def tile_src(tl):
                if tl[0] == 'own':
                    return x_own[tl[1] * TS:(tl[1] + 1) * TS, :], TS
                if tl[0] == 'halo':
                    return x_halo[:, :], 512
                s_, i_ = tl[1], tl[2]
                nrows = TS if i_ < 8 else 4
                return x_ext[s_, i_ * TS:i_ * TS + nrows, :], nrows

            def prep(k):
                src, nrows = tile_src(tiles[k])
                conv_steps(4)
                nb = load_x(src, nrows)
                rms_norm_blocks(XT, XB2[k % 2], G, SM, nblk=nb, xbkey=f'XB{k % 2}')
                return nb

            nbs = {0: prep(0)}
            for k, tl in enumerate(tiles):
                nb = nbs[k]
                transpose_blocks(XB2[k % 2], XNT, PTs, nblk=nb, xbkey=f'XB{k % 2}')
                if k + 1 < len(tiles):
                    nbs[k + 1] = prep(k + 1)
                if tl[0] == 'own':
                    t0 = tl[1] * TS
                    for g in range(10):
                        wt, wkey = load_wg(g * 512)
                        half = (g % 2) * 512
                        if g < 2:
                            fm_group(wt, wkey, 0, 4, 'q', fm_store(QT, half, t0, TS))
                        elif g < 4:
                            fm_group(wt, wkey, 0, 4, 'k', fm_store(KT, half, 256 + t0, TS))
                        elif g < 6:
                            tm_group(wt, wkey, 4, tm_store(VV, 256 + t0, half, 4))
                        elif g < 8:
                            fm_group(wt, wkey, 0, 4, 'xr', fm_store(XR, half, 2 + t0, TS))
                        else:
                            fm_group(wt, wkey, 0, 4, 'gg', fm_store(GG, half, t0, TS))
                elif tl[0] == 'halo':
                    for g in (2, 3, 4, 5, 6, 7):
                        wt, wkey = load_wg(g * 512)
                        half = (g % 2) * 512
                        if g < 4:
                            fm_group(wt, wkey, 0, 4, 'k', halo_k_store(half))
                        elif g < 6:
                            tm_group(wt, wkey, 4, halo_v_store(half))
                        else:
                            fm_group(wt, wkey, 0, 4, 'xr', halo_xr_store(half))
                    S.dma('sp', lambda e: e.dma_start(out=WGX[:, :, 0:512], in_=WIN1[6]), writes=['wgx'])
                    S.dma('sp', lambda e: e.dma_start(out=WGX[:, :, 512:1024], in_=WIN1[7]), writes=['wgx'])
                else:
                    s_, i_ = tl[1], tl[2]
                    t0 = i_ * TS
                    for hh in range(2):
                        if i_ < 8:
                            fm_group(WGX, 'wgx', hh * 512, 4, 'xr', fm_store(XRE[s_], hh * 512, t0, TS))
                        else:
                            fm_group(WGX, 'wgx', hh * 512, 1, 'xr', fm_store(XRE[s_], hh * 512, t0, 4))
            conv_steps(128)
            S.barrier()
            S.flush()
        if debug == 1:
            return nc
        with ExitStack() as ph:
            TAB = sbuf(ph, "na_TAB", [128, 40, 640], BF16)
            for ty in range(5):
                S.dma('pool', lambda e, ty=ty: e.dma_start(out=TAB[:, ty * 8:(ty + 1) * 8, :], in_=natab[ty].rearrange("h q k -> q h k")), writes=['TAB'])
            Qt = [sbuf(ph, f"na_Q{i}", [128, 8, 128], BF16) for i in range(2)]
            Kt = [sbuf(ph, f"na_K{i}", [128, 8, 640], BF16) for i in range(2)]
            Vt = [sbuf(ph, f"na_V{i}", [128, 5, 1024], BF16) for i in range(2)]
            YAo = [sbuf(ph, f"na_YA{i}", [128, 8, 128], BF16) for i in range(2)]
            Pf = [sbuf(ph, f"na_Pf{i}", [128, 640], BF16) for i in range(2)]
            Pn = [sbuf(ph, f"na_Pn{i}", [128, 640], BF16) for i in range(2)]
            PTs_ = [sbuf(ph, f"na_PT{i}", [128, 5, 128], BF16) for i in range(2)]
            NM = sbuf(ph, "na_NM", [128, 8])
            psS = [psum(ph, f"na_psS{i}", [128, 1024]) for i in range(2)]
            psPTf = [psum(ph, f"na_psPT{i}", [128, 512]) for i in range(2)]
            psPT = [p[:, :].bitcast(BF16) for p in psPTf]
            psO = [psum(ph, f"na_psO{i}", [128, 512]) for i in range(2)]
            nau = []
            for P in range(32):
                for h in range(8):
                    nau.append((P, h))

            def naA(k):
                P, h = nau[k]
                j, i = P % 2, k % 2
                ty = {0: 1, 1: 2, 30: 3, 31: 4}.get(P, 0)
                if h == 0:
                    S.dma('sp', lambda e: e.dma_start(out=Qt[j][:, :, :], in_=QT[:, 128 * P:128 * P + 128].rearrange("(h d) t -> d h t", d=128)), writes=[f'Q{j}'])
                    S.dma('sp', lambda e: e.dma_start(out=Kt[j][:, :, :], in_=KT[:, 128 * P:128 * P + 640].rearrange("(h d) t -> d h t", d=128)), writes=[f'K{j}'])
                    S.dma('sp', lambda e: e.dma_start(out=Vt[j][:, :, :], in_=VV[128 * P:128 * P + 640, :].rearrange("(c p) f -> p c f", p=128)), writes=[f'V{j}'])
                sp_, pk = psS[i], f'psS{i}'
                S.op('pe', lambda e: e.matmul(sp_[:, 0:512], Qt[j][:, h, :], Kt[j][:, h, 0:512], start=True, stop=False), reads=[f'Q{j}', f'K{j}'], writes=[pk])
                S.op('pe', lambda e: e.matmul(sp_[:, 0:512], identb[:], TAB[:, ty * 8 + h, 0:512], start=False, stop=True), reads=['TAB', 'identb'], writes=[pk])
                S.op('pe', lambda e: e.matmul(sp_[:, 512:640], Qt[j][:, h, :], Kt[j][:, h, 512:640], start=True, stop=False), reads=[f'Q{j}', f'K{j}'], writes=[pk])
                S.op('pe', lambda e: e.matmul(sp_[:, 512:640], identb[:], TAB[:, ty * 8 + h, 512:640], start=False, stop=True), reads=['TAB', 'identb'], writes=[pk])

            def naB(k):
                i = k % 2
                sp_, pk, nk = psS[i], f'psS{i}', f'NM{i}'
                S.op('dve', lambda e: e.reduce_max(out=NM[:, i:i + 1], in_=sp_[:, 0:640], axis=AX.X, negate=True), reads=[pk], writes=[nk])
                S.op('act', lambda e: e.activation(out=Pf[i][:], in_=sp_[:, 0:640], func=AF.Exp, bias=NM[:, i:i + 1], scale=1.0, accum_out=NM[:, 2 + i:3 + i]),
                     reads=[pk, nk], writes=[f'Pf{i}', f'NS{i}'])
                S.op('dve', lambda e: e.reciprocal(out=NM[:, 4 + i:5 + i], in_=NM[:, 2 + i:3 + i]), reads=[f'NS{i}'], writes=[f'NR{i}'])
                S.op('dve', lambda e: e.tensor_scalar(out=Pn[i][:], in0=Pf[i][:], scalar1=NM[:, 4 + i:5 + i], scalar2=None, op0=ALU.mult),
                     reads=[f'Pf{i}', f'NR{i}'], writes=[f'Pn{i}'])

            def naC(k):
                i = k % 2
                for c in range(5):
                    S.op('pe', lambda e, c=c: e.transpose(psPT[i][:, c * 128:(c + 1) * 128], Pn[i][:, c * 128:(c + 1) * 128], identb[:]),
                         reads=[f'Pn{i}', 'identb'], writes=[f'psPT{i}'])
                evac_copy(PTs_[i][:, :, :].rearrange("p c q -> p (c q)"), psPT[i][:, 0:640], [f'psPT{i}'], [f'PT{i}'])

            def naD(k):
                P, h = nau[k]
                j, i = P % 2, k % 2
                for c in range(5):
                    S.op('pe', lambda e, c=c: e.matmul(psO[i][:, 0:128], Vt[j][:, c, h * 128:(h + 1) * 128], PTs_[i][:, c, :], start=(c == 0), stop=(c == 4)),
                         reads=[f'V{j}', f'PT{i}'], writes=[f'psO{i}'])
                evac_copy(YAo[j][:, h, :], psO[i][:, 0:128], [f'psO{i}'], [f'YA{j}'])
                if h == 7:
                    S.dma('sp', lambda e: e.dma_start(out=YA[:, 128 * P:128 * P + 128].rearrange("(h d) t -> d h t", d=128), in_=YAo[j][:, :, :]), reads=[f'YA{j}'])

            nn = len(nau)
            for kk in range(nn + 3):
                if kk < nn:
                    naA(kk)
                if 0 <= kk - 1 < nn:
                    naB(kk - 1)
                if 0 <= kk - 2 < nn:
                    naC(kk - 2)
                if 0 <= kk - 3 < nn:
                    naD(kk - 3)
            S.barrier()
            S.flush()
        if debug == 2:
            return nc
        with ExitStack() as ph:
            WA = sbuf(ph, "l_WA", [128, 5, 8, 128], BF16)
            WX = sbuf(ph, "l_WX", [128, 5, 8, 128], BF16)
            S.dma('pool', lambda e: e.dma_start(out=WA[:, :, :, :], in_=lwa.rearrange("u n c d -> c u n d")), writes=['WA'])
            S.dma('pool', lambda e: e.dma_start(out=WX[:, :, :, :], in_=lwx.rearrange("u n c d -> c u n d")), writes=['WX'])
            LSM = sbuf(ph, "l_LSM", [128, 5, 8, 3])
            S.dma('sp', lambda e: e.dma_start(out=LSM[:, :, :, :], in_=lsm.rearrange("u p n k -> p u n k")), writes=['LSM'])
            CWB = sbuf(ph, "l_CWB", [128, 4, 8, 6])
            S.dma('sp', lambda e: e.dma_start(out=CWB[:, :, :, :], in_=cwb.rearrange("u p n k -> p u n k")), writes=['CWB'])
            FL = sbuf(ph, "l_FL", [128, 16])
            S.dma('sp', lambda e: e.dma_start(out=FL[:, :], in_=flags), writes=['FL'])
            C8 = sbuf(ph, "l_C8", [128, 5, 8])
            E1 = sbuf(ph, "l_E1", [128, 5, 8])
            S.op('act', lambda e: e.activation(out=E1[:, :, :], in_=LSM[:, :, :, 2], func=AF.Exp, scale=-1.0), reads=['LSM'], writes=['E1'])
            S.op('act', lambda e: e.activation(out=E1[:, :, :], in_=E1[:, :, :], func=AF.Ln, bias=1.0, scale=1.0), reads=['E1'], writes=['E1'])
            S.op('dve', lambda e: e.tensor_scalar(out=C8[:, :, :], in0=E1[:, :, :], scalar1=-8.0, scalar2=None, op0=ALU.mult), reads=['E1'], writes=['C8'])
            STt = sbuf(ph, "l_ST", [128, 8])
            ACCF = sbuf(ph, "l_ACCF", [128, 8])
            ACCB = sbuf(ph, "l_ACCB", [128, 8])
            for t_, k_ in ((STt, 'ST'), (ACCF, 'ACCF'), (ACCB, 'ACCB')):
                S.op('dve', lambda e, t_=t_: e.memset(t_[:, :], 0.0), writes=[k_])
            NB = 6
            XRt = [sbuf(ph, f"l_XR{i}", [128, 516]) for i in range(NB)]
            XC = [sbuf(ph, f"l_XC{i}", [128, 512]) for i in range(NB)]
            XCB = [sbuf(ph, f"l_XCB{i}", [128, 512], BF16) for i in range(NB)]
            REC = [sbuf(ph, f"l_REC{i}", [128, 512]) for i in range(NB)]
            INP = [sbuf(ph, f"l_INP{i}", [128, 512]) for i in range(NB)]
            AA = [sbuf(ph, f"l_A{i}", [128, 512]) for i in range(NB)]
            T1 = [sbuf(ph, f"l_T1{i}", [128, 512]) for i in range(NB)]
            UU = [sbuf(ph, f"l_U{i}", [128, 512]) for i in range(NB)]
            HH = [sbuf(ph, f"l_H{i}", [128, 512]) for i in range(NB)]
            HFt = [sbuf(ph, f"l_HF{i}", [128, 512]) for i in range(NB)]
            GGt = [sbuf(ph, f"l_GG{i}", [128, 512]) for i in range(NB)]
            YRt = [sbuf(ph, f"l_YR{i}", [128, 512], BF16) for i in range(NB)]
            psA = [psum(ph, f"l_psA{i}", [128, 512]) for i in range(4)]
            psX = [psum(ph, f"l_psX{i}", [128, 512]) for i in range(4)]
            units = []

            def mk(uc, ug, scr, i, n, state, skey, reverse, valid_ap, vkeys, kind, pre=None, fin=None):
                k = len(units)
                units.append(dict(uc=uc, ug=ug, scr=scr, i=i, n=n, state=state, skey=skey, reverse=reverse, valid_ap=valid_ap, vkeys=vkeys,
                                  kind=kind, pre=pre, fin=fin, j=k % NB, pj=k % 4))

            def stA(u):
                j, pj, n, i, uc, ug = u['j'], u['pj'], u['n'], u['i'], u['uc'], u['ug']
                k = lambda s_: f'{s_}{j}'
                scr = u['scr']
                S.dma('sp', lambda e: e.dma_start(out=XRt[j][:, :], in_=scr[n * 128:(n + 1) * 128, 512 * i:512 * i + 516]), writes=[k('XR')])
                if u['kind'] == 'bwd':
                    S.dma('sp', lambda e: e.dma_start(out=HFt[j][:, :], in_=HF[n * 128:(n + 1) * 128, 512 * i:512 * i + 512]), reads=[f'HF_{i}_{n}'], writes=[k('HFt')])
                    S.dma('sp', lambda e: e.dma_start(out=GGt[j][:, :], in_=GG[n * 128:(n + 1) * 128, 512 * i:512 * i + 512]), writes=[k('GGt')])
                S.op('dve', lambda e: e.tensor_scalar(out=XC[j][:, :], in0=XRt[j][:, 0:512], scalar1=CWB[:, uc, n, 0:1], scalar2=CWB[:, uc, n, 5:6],
                                                      op0=ALU.mult, op1=ALU.add), reads=[k('XR'), 'CWB'], writes=[k('XC')])
                for t in range(1, 5):
                    S.op('dve', lambda e, t=t: e.scalar_tensor_tensor(out=XC[j][:, :], in0=XRt[j][:, t:t + 512], scalar=CWB[:, uc, n, t:t + 1], in1=XC[j][:, :],
                                                                      op0=ALU.mult, op1=ALU.add), reads=[k('XR'), k('XC'), 'CWB'], writes=[k('XC')])
                S.op('act', lambda e: e.copy(out=XCB[j][:, :], in_=XC[j][:, :]), reads=[k('XC')], writes=[k('XCB')])
                S.op('pe', lambda e: e.matmul(psA[pj][:, :], WA[:, ug, n, :], XCB[j][:, :], start=True, stop=True), reads=['WA', k('XCB')], writes=[f'psA{pj}'])
                S.op('pe', lambda e: e.matmul(psX[pj][:, :], WX[:, ug, n, :], XCB[j][:, :], start=True, stop=True), reads=['WX', k('XCB')], writes=[f'psX{pj}'])

            def stB1(u):
                j, pj, n, ug = u['j'], u['pj'], u['n'], u['ug']
                k = lambda s_: f'{s_}{j}'
                S.op('act', lambda e: e.activation(out=REC[j][:, :], in_=psA[pj][:, :], func=AF.Sigmoid, bias=LSM[:, ug, n, 0:1], scale=1.0),
                     reads=[f'psA{pj}', 'LSM'], writes=[k('REC')])
                S.op('act', lambda e: e.activation(out=INP[j][:, :], in_=psX[pj][:, :], func=AF.Sigmoid, bias=LSM[:, ug, n, 1:2], scale=1.0),
                     reads=[f'psX{pj}', 'LSM'], writes=[k('INP')])
                S.op('act', lambda e: e.activation(out=AA[j][:, :], in_=REC[j][:, :], func=AF.Exp, scale=C8[:, ug, n:n + 1]),
                     reads=[k('REC'), 'C8'], writes=[k('A')])
                S.op('pool', lambda e: e.tensor_tensor(out=T1[j][:, :], in0=AA[j][:, :], in1=AA[j][:, :], op=ALU.mult), reads=[k('A')], writes=[k('T1')])
                S.op('pool', lambda e: e.tensor_tensor(out=UU[j][:, :], in0=INP[j][:, :], in1=XC[j][:, :], op=ALU.mult), reads=[k('INP'), k('XC')], writes=[k('U')])

            def stB2(u):
                j = u['j']
                k = lambda s_: f'{s_}{j}'
                S.op('act', lambda e: e.activation(out=T1[j][:, :], in_=T1[j][:, :], func=AF.Sqrt, scale=-1.0, bias=1.0), reads=[k('T1')], writes=[k('T1')])
                S.op('pool', lambda e: e.tensor_tensor(out=UU[j][:, :], in0=T1[j][:, :], in1=UU[j][:, :], op=ALU.mult), reads=[k('T1'), k('U')], writes=[k('U')])

            def stC(u):
                j, n, i = u['j'], u['n'], u['i']
                k = lambda s_: f'{s_}{j}'
                state, skey = u['state'], u['skey']
                if u['pre'] is not None:
                    u['pre']()
                sk = f'{skey}{n}'
                if not u['reverse']:
                    S.op('dve', lambda e: e.tensor_tensor_scan(out=HH[j][:, :], data0=AA[j][:, :], data1=UU[j][:, :], initial=state[:, n:n + 1],
                                                               op0=ALU.mult, op1=ALU.add), reads=[k('A'), k('U'), sk, skey], writes=[k('H')])
                    S.op('act', lambda e: e.copy(out=state[:, n:n + 1], in_=HH[j][:, 511:512]), reads=[k('H')], writes=[sk])
                else:
                    S.op('dve', lambda e: e.tensor_tensor_scan(out=HH[j][:, ::-1], data0=AA[j][:, ::-1], data1=UU[j][:, ::-1], initial=state[:, n:n + 1],
                                                               op0=ALU.mult, op1=ALU.add), reads=[k('A'), k('U'), sk, skey], writes=[k('H')])
                    S.op('act', lambda e: e.copy(out=state[:, n:n + 1], in_=HH[j][:, 0:1]), reads=[k('H')], writes=[sk])
                if u['kind'] == 'fwd':
                    S.dma('sp', lambda e: e.dma_start(out=HF[n * 128:(n + 1) * 128, 512 * i:512 * i + 512], in_=HH[j][:, :]), reads=[k('H')], writes=[f'HF_{i}_{n}'])
                elif u['kind'] == 'bwd':
                    S.op('pool', lambda e: e.tensor_tensor(out=HFt[j][:, :], in0=HFt[j][:, :], in1=HH[j][:, :], op=ALU.add), reads=[k('HFt'), k('H')], writes=[k('HFt')])
                    S.op('pool', lambda e: e.tensor_tensor(out=YRt[j][:, :], in0=HFt[j][:, :], in1=GGt[j][:, :], op=ALU.mult), reads=[k('HFt'), k('GGt')], writes=[k('YR')])
                    S.dma('sp', lambda e: e.dma_start(out=YR[n * 128:(n + 1) * 128, 512 * i:512 * i + 512], in_=YRt[j][:, :]), reads=[k('YR')])
                if u['fin'] is not None:
                    u['fin']()

            ST_ALL = ['ST'] + [f'ST{n}' for n in range(8)]

            def slot_pre(s):
                def f():
                    S.op('dve', lambda e: e.tensor_scalar(out=STt[:, :], in0=STt[:, :], scalar1=FL[:, s:s + 1], scalar2=None, op0=ALU.mult),
                         reads=ST_ALL + ['FL'], writes=ST_ALL)
                return f

            def slot_fin(s):
                def f():
                    S.op('dve', lambda e: e.scalar_tensor_tensor(out=ACCF[:, :], in0=STt[:, :], scalar=FL[:, 6 + s:7 + s], in1=ACCF[:, :], op0=ALU.mult, op1=ALU.add),
                         reads=ST_ALL + ['FL', 'ACCF'], writes=['ACCF'])
                    S.op('dve', lambda e: e.scalar_tensor_tensor(out=ACCB[:, :], in0=STt[:, :], scalar=FL[:, 9 + s:10 + s], in1=ACCB[:, :], op0=ALU.mult, op1=ALU.add),
                         reads=ST_ALL + ['FL', 'ACCB'], writes=['ACCB'])
                return f

            for s in range(3):
                for i in range(8):
                    for n in range(8):
                        mk(s, s, XRE[s], i, n, STt, 'ST', False, FL[:, 3 + s:4 + s], ['FL'], 'ext',
                           pre=slot_pre(s) if (i == 0 and n == 0) else None, fin=slot_fin(s) if (i == 7 and n == 7) else None)
            for i in range(8):
                for n in range(8):
                    mk(3, 3, XR, i, n, ACCF, 'ACCF', False, 1.0, [], 'fwd')
            for i in range(7, -1, -1):
                for n in range(8):
                    mk(3, 4, XR, i, n, ACCB, 'ACCB', True, 1.0, [], 'bwd')
            nu = len(units)
            for kk in range(nu + 3):
                if kk < nu:
                    stA(units[kk])
                if 0 <= kk - 1 < nu:
                    stB1(units[kk - 1])
                if 0 <= kk - 2 < nu:
                    stB2(units[kk - 2])
                if 0 <= kk - 3 < nu:
                    stC(units[kk - 3])
            S.barrier()
            S.flush()
        if debug == 3:
            return nc
        with ExitStack() as ph:
            XT = sbuf(ph, "f_XT", [128, 4, D])
            XB = sbuf(ph, "f_XB", [128, 4, D], BF16)
            XNT = sbuf(ph, "f_XNT", [128, 16, TS], BF16)
            WG = [sbuf(ph, f"f_WG{i}", [128, 16, 256], BF16) for i in range(2)]
            WGA = sbuf(ph, "f_WGA", [128, 8, 256], BF16)
            WGR = sbuf(ph, "f_WGR", [128, 8, 256], BF16)
            WPP = sbuf(ph, "f_WPP", [128, 2, 256], BF16)
            MT = sbuf(ph, "f_MT", [128, 16, TS], BF16)
            NG = 6
            BIG = sbuf(ph, "f_BIG", [128, 2 * NG * D], BF16)
            UB = [BIG[:, j * D:(j + 1) * D] for j in range(NG)]
            VB = [BIG[:, (NG + j) * D:(NG + 1 + j) * D] for j in range(NG)]
            YAT = BIG[:, 0:2 * D].rearrange("p (c t) -> p c t", c=8)
            YRT = BIG[:, 2 * D:4 * D].rearrange("p (c t) -> p c t", c=8)
            OUTB = [BIG[:, (NG + 2 * j) * D:(NG + 2 + 2 * j) * D].bitcast(F32) for j in range(2)]
            G = sbuf(ph, "f_G", [128, D])
            JUNK = G[:, 0:1024].bitcast(BF16)
            SGA = sbuf(ph, "f_SGA", [128, 512])
            SGB = sbuf(ph, "f_SGB", [128, 512])
            TT = sbuf(ph, "f_TT", [128, 512])
            PTt = sbuf(ph, "f_PTt", [128, 2, TS], BF16)
            SM = sbuf(ph, "f_SM", [128, 16])
            SKT = sbuf(ph, "f_SKT", [128, 16, 128], BF16)
            S.dma('pool', lambda e: e.dma_start(out=SKT[:, :, :], in_=skT.rearrange("c d k -> d c k")), writes=['SKT'])
            S_ALL = sbuf(ph, "f_SALL", [128, 16, 128])
            S_TMP = sbuf(ph, "f_STMP", [128, 256])
            SV = sbuf(ph, "f_SV", [128, 16, 16])
            SI = sbuf(ph, "f_SI", [128, 16, 16], U32)
            SIF = sbuf(ph, "f_SIF", [128, 16, 16])
            CAND = sbuf(ph, "f_CAND", [128, 16, 16])
            CV = sbuf(ph, "f_CV", [128, 8, 16])
            CI = sbuf(ph, "f_CI", [128, 8, 16], U32)
            HI = sbuf(ph, "f_HI", [128, 128], U32)
            LO = sbuf(ph, "f_LO", [128, 128], U32)
            HIF = sbuf(ph, "f_HIF", [128, 128])
            LOF = sbuf(ph, "f_LOF", [128, 128])
            EQ = sbuf(ph, "f_EQ", [128, 128, 16])
            I1 = sbuf(ph, "f_I1", [128, 128])
            I2 = sbuf(ph, "f_I2", [128, 128])
            IDS = sbuf(ph, "f_IDS", [128, 128])
            NEG = sbuf(ph, "f_NEG", [128, 8])
            GE = sbuf(ph, "f_GE", [128, 8, 16])
            GS = sbuf(ph, "f_GS", [128, 8])
            RS = sbuf(ph, "f_RS", [128, 8])
            GM = sbuf(ph, "f_GM", [128, 8, 16])
            IDST = sbuf(ph, "f_IDST", [128, 128], U32)
            GT = sbuf(ph, "f_GT", [128, 128])
            IOT = sbuf(ph, "f_IOT", [128, 16])
            S.op('pool', lambda e: e.iota(IOT[:], pattern=[[1, 16]], base=0, channel_multiplier=0, allow_small_or_imprecise_dtypes=True), writes=['IOT'])
            HUH = [sbuf(ph, f"f_HU{i}", [128, 4]) for i in range(2)]
            GL = sbuf(ph, "f_GL", [128, 4])
            ZB = [sbuf(ph, f"f_Z{i}", [128, 256], BF16) for i in range(4)]
            for zi in range(4):
                S.op('dve', lambda e, zi=zi: e.memset(ZB[zi][:, :], 0.0), writes=[f'Z{zi}'])
            Xps = psum(ph, "f_Xps", [128, D])
            Yps = psum(ph, "f_Yps", [128, D])
            BANK = [(Xps[:, j * 512:(j + 1) * 512], f'bX{j}') for j in range(4)] + [(Yps[:, j * 512:(j + 1) * 512], f'bY{j}') for j in range(4)]
            XK = [f'bX{j}' for j in range(4)]
            YK = [f'bY{j}' for j in range(4)]
            PTs4 = [Xps[:, 0:512].bitcast(BF16), Xps[:, 512:1024].bitcast(BF16)]
            rot = {'b': 0, 'wg': 0}

            def nbank():
                r = BANK[rot['b'] % 8]
                rot['b'] += 1
                return r

            def load_w(dst, key, T, g):
                S.dma('sp', lambda e: e.dma_start(out=dst, in_=T[g]), writes=[key])

            def load_wg(T, g):
                i = rot['wg'] % 2
                rot['wg'] += 1
                load_w(WG[i][:, :, :], f'wg{i}', T, g)
                return WG[i], f'wg{i}'

            def transpose4(XB_, XNT_):
                for c in range(16):
                    pt = PTs4[c % 2]; key = f'bX{c % 2}'
                    for b in range(4):
                        S.op('pe', lambda e, b=b, c=c, pt=pt: e.transpose(pt[:, b * 128:(b + 1) * 128], XB_[:, b, c * 128:(c + 1) * 128], identb[:]),
                             reads=['XB', 'identb'], writes=[key])
                    evac_copy(XNT_[:, c, :], pt[:, 0:512], reads=[key], writes=['XNT'])

            def top16(src, skeys, sv, si, okeys, tmp):
                S.op('dve', lambda e: e.max(out=sv[:, 0:8], in_=src), reads=skeys, writes=okeys[:1])
                S.op('dve', lambda e: e.max_index(out=si[:, 0:8], in_max=sv[:, 0:8], in_values=src), reads=skeys + okeys[:1], writes=okeys[1:])
                S.op('dve', lambda e: e.match_replace(out=tmp, in_to_replace=sv[:, 0:8], in_values=src, imm_value=-1e30), reads=skeys + okeys[:1], writes=['STMP'])
                S.op('dve', lambda e: e.max(out=sv[:, 8:16], in_=tmp), reads=['STMP'], writes=okeys[:1])
                S.op('dve', lambda e: e.max_index(out=si[:, 8:16], in_max=sv[:, 8:16], in_values=tmp), reads=['STMP'] + okeys[:1], writes=okeys[1:])

            def peer_block(b):
                for c in range(16):
                    S.op('pe', lambda e, c=c: e.matmul(Xps[:, c * 128:(c + 1) * 128], MT[:, c, b * 128:(b + 1) * 128], SKT[:, c, :], start=True, stop=True),
                         reads=['MT', 'SKT'], writes=[f'bX{c // 4}'])
                S.op('act', lambda e: e.copy(out=S_ALL[:, :, :].rearrange("p c k -> p (c k)"), in_=Xps[:, :]), reads=XK, writes=['SALL'])
                for c in range(16):
                    top16(S_ALL[:, c, :], ['SALL'], SV[:, c, :], SI[:, c, :], ['SV', 'SI'], S_TMP[:, 0:128])
                for h in range(8):
                    S.op('dve', lambda e, h=h: e.tensor_tensor(out=CAND[:, :, :], in0=SV[:, 2 * h, :].unsqueeze(2).to_broadcast([128, 16, 16]),
                                                               in1=SV[:, 2 * h + 1, :].unsqueeze(1).to_broadcast([128, 16, 16]), op=ALU.add),
                         reads=['SV'], writes=['CAND'])
                    top16(CAND[:, :, :].rearrange("p a b -> p (a b)"), ['CAND'], CV[:, h, :], CI[:, h, :], ['CV', 'CI'], S_TMP[:, 0:256])
                CIf = CI[:, :, :].rearrange("p h k -> p (h k)")
                S.op('dve', lambda e: e.tensor_single_scalar(out=HI[:, :], in_=CIf, scalar=4, op=ALU.logical_shift_right), reads=['CI'], writes=['HI'])
                S.op('dve', lambda e: e.tensor_single_scalar(out=LO[:, :], in_=CIf, scalar=15, op=ALU.bitwise_and), reads=['CI'], writes=['LO'])
                S.op('dve', lambda e: e.tensor_copy(out=HIF[:, :], in_=HI[:, :]), reads=['HI'], writes=['HIF'])
                S.op('dve', lambda e: e.tensor_copy(out=LOF[:, :], in_=LO[:, :]), reads=['LO'], writes=['LOF'])
                S.op('dve', lambda e: e.tensor_copy(out=SIF[:, :, :], in_=SI[:, :, :]), reads=['SI'], writes=['SIF'])
                SIF4 = SIF[:, :, :].rearrange("p (h two) a -> p h two a", two=2)
                EQ4 = EQ[:, :, :].rearrange("p (h k) a -> p h k a", h=8)
                for which, (XF, IX, ikey) in enumerate(((HIF, I1, 'I1'), (LOF, I2, 'I2'))):
                    xkey = 'HIF' if which == 0 else 'LOF'
                    S.op('dve', lambda e, XF=XF: e.tensor_tensor(out=EQ[:, :, :], in0=XF[:, :].unsqueeze(2).to_broadcast([128, 128, 16]),
                                                                 in1=IOT[:, :].unsqueeze(1).to_broadcast([128, 128, 16]), op=ALU.is_equal),
                         reads=[xkey, 'IOT'], writes=['EQ'])
                    S.op('dve', lambda e, which=which: e.tensor_tensor(out=EQ4, in0=EQ4, in1=SIF4[:, :, which, :].unsqueeze(2).to_broadcast([128, 8, 16, 16]), op=ALU.mult),
                         reads=['EQ', 'SIF'], writes=['EQ'])
                    S.op('dve', lambda e, IX=IX: e.reduce_sum(out=IX[:, :], in_=EQ[:, :, :], axis=AX.X), reads=['EQ'], writes=[ikey])
                S.op('dve', lambda e: e.scalar_tensor_tensor(out=IDS[:, :], in0=I1[:, :], scalar=128.0, in1=I2[:, :], op0=ALU.mult, op1=ALU.add),
                     reads=['I1', 'I2'], writes=['IDS'])
                S.op('dve', lambda e: e.tensor_scalar(out=NEG[:, :], in0=CV[:, :, 0], scalar1=-1.0, scalar2=None, op0=ALU.mult), reads=['CV'], writes=['NEG'])
                for h in range(8):
                    S.op('act', lambda e, h=h: e.activation(out=GE[:, h, :], in_=CV[:, h, :], func=AF.Exp, bias=NEG[:, h:h + 1], scale=1.0, accum_out=GS[:, h:h + 1]),
                         reads=['CV', 'NEG'], writes=['GE', 'GS'])
                S.op('dve', lambda e: e.reciprocal(out=RS[:, :], in_=GS[:, :]), reads=['GS'], writes=['RS'])
                S.op('dve', lambda e: e.tensor_tensor(out=GM[:, :, :], in0=GE[:, :, :], in1=RS[:, :].unsqueeze(2).to_broadcast([128, 8, 16]), op=ALU.mult),
                     reads=['GE', 'RS'], writes=['GM'])
                S.op('pe', lambda e: e.transpose(Xps[:, 0:128], IDS[:, :], identf[:]), reads=['IDS', 'identf'], writes=['bX0'])
                S.op('pe', lambda e: e.transpose(Xps[:, 512:640], GM[:, :, :].rearrange("p h k -> p (h k)"), identf[:]), reads=['GM', 'identf'], writes=['bX1'])
                S.op('dve', lambda e: e.tensor_copy(out=IDST[:, :], in_=Xps[:, 0:128]), reads=['bX0'], writes=['IDST'])
                S.op('act', lambda e: e.copy(out=GT[:, :], in_=Xps[:, 512:640]), reads=['bX1'], writes=['GT'])
                def gather(tl):
                    ui = tl % NG
                    S.dma('pool', lambda e: e.indirect_dma_start(out=UB[ui], out_offset=None, in_=UBF,
                                                                 in_offset=bass.IndirectOffsetOnAxis(ap=IDST[:, tl:tl + 1], axis=0)),
                          reads=['IDST'], writes=[f'UB{ui}'])
                    S.dma('pool', lambda e: e.indirect_dma_start(out=VB[ui], out_offset=None, in_=VBF,
                                                                 in_offset=bass.IndirectOffsetOnAxis(ap=IDST[:, tl:tl + 1], axis=0)),
                          reads=['IDST'], writes=[f'VB{ui}'])

                def bcast_half(tl, hf):
                    for c in (2 * hf, 2 * hf + 1):
                        S.op('pe', lambda e, c=c: e.matmul(Xps[:, c * 512:(c + 1) * 512], identb[:, tl:tl + 1].to_broadcast([128, 128]),
                                                           XB[:, b, c * 512:(c + 1) * 512], start=True, stop=True),
                             reads=['XB', 'identb'], writes=[f'bX{c}'])

                for tl in range(NG - 1):
                    gather(tl)
                bcast_half(0, 0)
                bcast_half(0, 1)
                for tl in range(128):
                    ui = tl % NG
                    zi = tl % 4
                    if tl + NG - 1 < 128:
                        gather(tl + NG - 1)
                    for hf in range(2):
                        hs = slice(hf * 1024, (hf + 1) * 1024)
                        S.op('dve', lambda e, ui=ui, zi=zi, hf=hf, hs=hs: e.scalar_tensor_tensor(out=JUNK[:, hs], in0=UB[ui][:, hs], scalar=1.0, in1=Xps[:, hs],
                                                                                             op0=ALU.mult, op1=ALU.mult, accum_out=HUH[hf][:, zi:zi + 1]),
                             reads=[f'UB{ui}', f'bX{2 * hf}', f'bX{2 * hf + 1}'], writes=[f'J{hf}', f'HU{hf}{zi}'])
                        if tl + 1 < 128:
                            bcast_half(tl + 1, hf)
                    S.op('act', lambda e, zi=zi: e.activation(out=GL[:, zi:zi + 1], in_=HUH[0][:, zi:zi + 1], func=AF.Gelu, bias=HUH[1][:, zi:zi + 1], scale=1.0),
                         reads=[f'HU0{zi}', f'HU1{zi}'], writes=[f'GL{zi}'])
                    S.op('act', lambda e, zi=zi, tl=tl: e.activation(out=ZB[zi][:, 128:129], in_=GL[:, zi:zi + 1], func=AF.Copy, scale=GT[:, tl:tl + 1]),
                         reads=[f'GL{zi}', 'GT'], writes=[f'Z{zi}'])
                    for c in range(4):
                        S.op('pe', lambda e, tl=tl, c=c, zi=zi, ui=ui: e.matmul(Yps[:, c * 512:(c + 1) * 512], ZB[zi][:, 128 - tl:256 - tl], VB[ui][:, c * 512:(c + 1) * 512],
                                                                             start=(tl == 0), stop=(tl == 127)),
                             reads=[f'Z{zi}', f'VB{ui}'], writes=[f'bY{c}'])
                S.op('dve', lambda e: e.tensor_tensor(out=XT[:, b, :], in0=XT[:, b, :], in1=Yps[:, :], op=ALU.add), reads=['XT'] + YK, writes=['XT'])

            ntiles = NT if debug != 4 else 1
            for i in range(ntiles):
                t0 = i * TS
                S.dma('sp', lambda e, t0=t0: e.dma_start(out=XT[:, :, :], in_=x_own[t0:t0 + TS, :].rearrange("(b p) d -> p b d", p=128)), writes=['XT'])
                load_gain(G, 0)
                rms_norm_blocks(XT, XB, G, SM)
                transpose4(XB, XNT)
                S.dma('sp', lambda e, t0=t0: e.dma_start(out=YAT, in_=YA[:, t0:t0 + TS].rearrange("(c p) t -> p c t", p=128)), writes=['UB0', 'UB1'])
                S.dma('sp', lambda e, t0=t0: e.dma_start(out=YRT, in_=YR[:, t0:t0 + TS].rearrange("(c p) t -> p c t", p=128)), writes=['UB2', 'UB3'])
                for fg in range(8):
                    load_w(WG[0][:, :, :], 'wg0', WIN2, fg)
                    load_w(WG[1][:, :, :], 'wg1', WIN2, 8 + fg)
                    load_w(WGA[:, :, :], 'wga', WAT, fg)
                    load_w(WGR[:, :, :], 'wgr', WRT, fg)
                    for f2 in range(2):
                        f = fg * 2 + f2
                        cs = slice(f2 * 128, (f2 + 1) * 128)
                        (pa, ka), (pb, kb), (pA, kA), (pR, kR) = nbank(), nbank(), nbank(), nbank()
                        for kc in range(16):
                            S.op('pe', lambda e, kc=kc, pa=pa, cs=cs: e.matmul(pa, WG[0][:, kc, cs], XNT[:, kc, :], start=(kc == 0), stop=(kc == 15)),
                                 reads=['wg0', 'XNT'], writes=[ka])
                        for kc in range(16):
                            S.op('pe', lambda e, kc=kc, pb=pb, cs=cs: e.matmul(pb, WG[1][:, kc, cs], XNT[:, kc, :], start=(kc == 0), stop=(kc == 15)),
                                 reads=['wg1', 'XNT'], writes=[kb])
                        for kc in range(8):
                            S.op('pe', lambda e, kc=kc, pA=pA, cs=cs: e.matmul(pA, WGA[:, kc, cs], YAT[:, kc, :], start=(kc == 0), stop=(kc == 7)),
                                 reads=['wga', 'UB0', 'UB1'], writes=[kA])
                        for kc in range(8):
                            S.op('pe', lambda e, kc=kc, pR=pR, cs=cs: e.matmul(pR, WGR[:, kc, cs], YRT[:, kc, :], start=(kc == 0), stop=(kc == 7)),
                                 reads=['wgr', 'UB2', 'UB3'], writes=[kR])
                        S.op('act', lambda e, pa=pa: e.activation(out=SGA[:, :], in_=pa, func=AF.Sigmoid), reads=[ka], writes=['SGA'])
                        S.op('act', lambda e, pb=pb: e.activation(out=SGB[:, :], in_=pb, func=AF.Sigmoid), reads=[kb], writes=['SGB'])
                        S.op('dve', lambda e, pA=pA: e.tensor_tensor(out=TT[:, :], in0=SGA[:, :], in1=pA, op=ALU.mult), reads=['SGA', kA], writes=['TT'])
                        S.op('dve', lambda e, pR=pR: e.tensor_tensor(out=SGB[:, :], in0=SGB[:, :], in1=pR, op=ALU.mult), reads=['SGB', kR], writes=['SGB'])
                        S.op('pool', lambda e, f=f: e.tensor_tensor(out=MT[:, f, :], in0=TT[:, :], in1=SGB[:, :], op=ALU.add), reads=['TT', 'SGB'], writes=['MT'])
                for cg in range(8):
                    wt, wkey = load_wg(WOT, cg)
                    for b in range(4):
                        pb_, kb_ = nbank()
                        for kc in range(16):
                            S.op('pe', lambda e, kc=kc, b=b, pb_=pb_, wt=wt: e.matmul(pb_[:, 0:256], MT[:, kc, b * 128:(b + 1) * 128], wt[:, kc, :], start=(kc == 0), stop=(kc == 15)),
                                 reads=['MT', wkey], writes=[kb_])
                        S.op('dve', lambda e, b=b, cg=cg, pb_=pb_: e.tensor_tensor(out=XT[:, b, cg * 256:(cg + 1) * 256], in0=XT[:, b, cg * 256:(cg + 1) * 256], in1=pb_[:, 0:256], op=ALU.add),
                             reads=['XT', kb_], writes=['XT'])
                load_gain(G, 1)
                rms_norm_blocks(XT, XB, G, SM)
                transpose4(XB, XNT)
                for cg in range(8):
                    wt, wkey = load_wg(WQT, cg)
                    for c2 in range(2):
                        pb_, kb_ = nbank()
                        for kc in range(16):
                            S.op('pe', lambda e, kc=kc, c2=c2, pb_=pb_, wt=wt: e.matmul(pb_, wt[:, kc, c2 * 128:(c2 + 1) * 128], XNT[:, kc, :], start=(kc == 0), stop=(kc == 15)),
                                 reads=['XNT', wkey], writes=[kb_])
                        evac_copy(MT[:, cg * 2 + c2, :], pb_, [kb_], ['MT'])
                for b in range(4):
                    peer_block(b)
                load_gain(G, 2)
                rms_norm_blocks(XT, XB, G, SM)
                transpose4(XB, XNT)
                S.dma('pool', lambda e, t0=t0: e.dma_start(out=PTt[:, :, :], in_=pT[:, t0:t0 + TS].rearrange("(c p) t -> p c t", p=128)), writes=['PTt'])
                for cg in range(8):
                    wt, wkey = load_wg(WPGT, cg)
                    load_w(WPP[:, :, :], 'wpp', WPPT, cg)
                    for b in range(4):
                        (pg, kg), (pp_, kp) = nbank(), nbank()
                        for kc in range(16):
                            S.op('pe', lambda e, kc=kc, b=b, pg=pg, wt=wt: e.matmul(pg[:, 0:256], XNT[:, kc, b * 128:(b + 1) * 128], wt[:, kc, :], start=(kc == 0), stop=(kc == 15)),
                                 reads=['XNT', wkey], writes=[kg])
                        for kc in range(2):
                            S.op('pe', lambda e, kc=kc, b=b, pp_=pp_: e.matmul(pp_[:, 0:256], PTt[:, kc, b * 128:(b + 1) * 128], WPP[:, kc, :], start=(kc == 0), stop=(kc == 1)),
                                 reads=['PTt', 'wpp'], writes=[kp])
                        S.op('act', lambda e, pg=pg: e.activation(out=SGA[:, 0:256], in_=pg[:, 0:256], func=AF.Sigmoid), reads=[kg], writes=['SGA'])
                        S.op('dve', lambda e, pp_=pp_: e.tensor_tensor(out=TT[:, 0:256], in0=SGA[:, 0:256], in1=pp_[:, 0:256], op=ALU.mult), reads=['SGA', kp], writes=['TT'])
                        S.op('pool', lambda e, b=b, cg=cg: e.tensor_tensor(out=XT[:, b, cg * 256:(cg + 1) * 256], in0=XT[:, b, cg * 256:(cg + 1) * 256], in1=TT[:, 0:256], op=ALU.add),
                             reads=['XT', 'TT'], writes=['XT'])
                load_gain(G, 3)
                rms_norm_blocks(XT, XB, G, SM, stats_only=True)
                for b in range(4):
                    ob, okey = OUTB[b % 2], [f'VB{2 * (b % 2)}', f'VB{2 * (b % 2) + 1}']
                    S.op('dve', lambda e, b=b, ob=ob: e.scalar_tensor_tensor(out=ob, in0=XT[:, b, :], scalar=SM[:, 12 + b:13 + b], in1=G[:, :], op0=ALU.mult, op1=ALU.mult),
                         reads=['XT', 'SM', 'G'], writes=okey)
                    S.dma('sp', lambda e, b=b, ob=ob, t0=t0: e.dma_start(out=y_own[t0 + b * 128:t0 + (b + 1) * 128, :], in_=ob), reads=okey)
            S.barrier()
            S.flush()
    return nc


def _na_tables(rpb, base_row, rows, has_prev, has_next):
    H = rpb.shape[0]
    out = np.full((5, H, 128, 640), -1e30, np.float32)
    qc = np.arange(64)
    kc = np.arange(64)
    col_start = np.clip(qc - 8, 0, 48)
    col_ok = (kc[None, :] >= col_start[:, None]) & (kc[None, :] < col_start[:, None] + 16)
    dc_idx = np.clip(kc[None, :] - qc[:, None], -15, 15) + 15
    for ty, P in enumerate((15, 0, 1, 30, 31)):
        for ri in range(2):
            r_seq = base_row + 2 * P + ri
            row_start = int(np.clip(r_seq - 4, 0, rows - 8))
            for c in range(5):
                pair = P - 2 + c
                for rj in range(2):
                    if pair < 0:
                        if has_prev:
                            k_seq = base_row + 2 * pair + rj
                        elif pair == -2:
                            k_seq = base_row + 6 + rj
                        else:
                            continue
                    elif pair > 31:
                        if has_next:
                            k_seq = base_row + 2 * pair + rj
                        elif pair == 33:
                            k_seq = base_row + 56 + rj
                        else:
                            continue
                    else:
                        k_seq = base_row + 2 * pair + rj
                    if not (row_start <= k_seq < row_start + 8):
                        continue
                    dr = k_seq - r_seq + 7
                    blk = rpb[:, dr, :][:, dc_idx]
                    blk = np.where(col_ok[None], blk, np.float32(-1e30))
                    out[ty, :, ri * 64:(ri + 1) * 64, c * 128 + rj * 64: c * 128 + (rj + 1) * 64] = blk
    return out


def prep_inputs(inp):
    f32 = np.float32
    xp = np.asarray(inp['x_prompt'], f32)[0]
    xs = np.asarray(inp['x_sample'], f32)
    pp = np.asarray(inp['p_prompt'], f32)[0, 0]
    psm = np.asarray(inp['p_sample'], f32)[0]
    rpb = np.asarray(inp['na_rpb'], f32)[0]
    conv_w = np.asarray(inp['conv_w'], f32)[0]
    conv_b = np.asarray(inp['conv_b'], f32)[0]
    wa = np.asarray(inp['lru_wa'], f32)[0]; wx = np.asarray(inp['lru_wx'], f32)[0]
    ba = np.asarray(inp['lru_ba'], f32)[0]; bx = np.asarray(inp['lru_bx'], f32)[0]; lam = np.asarray(inp['lru_lambda'], f32)[0]
    shared = {
        'gains': np.ascontiguousarray(np.stack([inp['norm_mix'][0], inp['norm_ffn'][0], inp['norm_ple'][0], inp['final_norm']]).astype(f32)),
        'w_in': np.ascontiguousarray(inp['w_in'][0], f32), 'w_a': np.ascontiguousarray(inp['w_branch_a'][0], f32),
        'w_r': np.ascontiguousarray(inp['w_branch_r'][0], f32), 'w_out': np.ascontiguousarray(inp['w_out'][0], f32),
        'w_q': np.ascontiguousarray(inp['peer_wq'][0], f32), 'w_pg': np.ascontiguousarray(inp['ple_gate_w'][0], f32),
        'w_pp': np.ascontiguousarray(inp['ple_proj_w'][0], f32),
        'skT': np.ascontiguousarray(np.asarray(inp['peer_subkeys'], f32)[0].reshape(16, 128, 128).transpose(0, 2, 1)),
        'peer_u': np.ascontiguousarray(inp['peer_u'][0], f32), 'peer_v': np.ascontiguousarray(inp['peer_v'][0], f32),
        'ident': np.eye(128, dtype=f32),
    }
    zero_tap = np.zeros((1, 1024), f32)
    taps_f = np.concatenate([conv_w, zero_tap], 0)
    taps_b = np.concatenate([zero_tap, conv_w[::-1]], 0)

    def pack_cwb(taps):
        t = np.concatenate([taps, conv_b[None]], 0)
        return t.reshape(6, 8, 128).transpose(2, 1, 0)

    def pack_sm(d):
        t = np.stack([ba[d], bx[d], lam[d]], 0)
        return t.reshape(3, 8, 128).transpose(2, 1, 0)

    maps = []
    for core in range(8):
        m = dict(shared)
        if core < 4:
            c = core; seq = xp; pseq = pp; start = c * NTOK; rows = 256; base_row = 64 * c
            has_prev, has_next = c > 0, c < 3
        else:
            c = None; seq = xs[core - 4]; pseq = psm[core - 4]; start = 0; rows = 64; base_row = 0
            has_prev = has_next = False
        own = seq[start:start + NTOK]
        m['x_own'] = np.ascontiguousarray(own)
        halo = np.zeros((512, D), f32)
        if has_prev:
            halo[0:256] = seq[start - 256:start]
        else:
            halo[0:128] = own[384:512]
        if has_next:
            halo[256:512] = seq[start + NTOK:start + NTOK + 256]
        else:
            halo[384:512] = own[3584:3712]
        m['x_halo'] = halo
        m['pT'] = np.ascontiguousarray(pseq[start:start + NTOK].T)
        m['natab'] = _na_tables(rpb, base_row, rows, has_prev, has_next)
        ext = np.zeros((3, 4100, D), f32)
        fl = np.zeros((128, 16), f32)
        dirs = [0, 0, 0]
        if c is not None:
            padded = np.concatenate([np.zeros((2, D), f32), seq, np.zeros((2, D), f32)], 0)
            slots = [(j, 0) for j in range(c)] + [(j, 1) for j in range(3, c, -1)]
            for s, (j, d) in enumerate(slots):
                a = padded[j * NTOK: j * NTOK + 4100]
                ext[s] = a if d == 0 else a[::-1]
                dirs[s] = d
                fl[:, 3 + s] = 1.0
                if s > 0 and slots[s - 1][1] == d:
                    fl[:, s] = 1.0
            if c > 0:
                fl[:, 6 + c - 1] = 1.0
            if c < 3:
                fl[:, 9 + 2] = 1.0
        m['x_ext'] = ext
        m['flags'] = fl
        udirs = dirs + [0, 1]
        m['cwb'] = np.ascontiguousarray(np.stack([pack_cwb(taps_b if dirs[s] else taps_f) for s in range(3)] + [pack_cwb(taps_f)]))
        m['lwa'] = np.ascontiguousarray(np.stack([wa[d] for d in udirs]))
        m['lwx'] = np.ascontiguousarray(np.stack([wx[d] for d in udirs]))
        m['lsm'] = np.ascontiguousarray(np.stack([pack_sm(d) for d in udirs]))
        maps.append(m)
    return maps


def kernel(**inputs):
    maps = prep_inputs(inputs)
    nc = build_nc()
    res = run_bass_kernel_spmd(nc, maps, core_ids=list(range(8)))
    outs = [np.asarray(r['y_own'], np.float32) for r in res.results]
    y_prompt = np.concatenate(outs[0:4], 0)[None]
    y_sample = np.stack(outs[4:8], 0)
    return (y_prompt, y_sample)
```

```python
import numpy as np
import ml_dtypes
from contextlib import ExitStack
import concourse.bass as bass
import concourse.mybir as mybir
from concourse.bass_utils import run_bass_kernel_spmd

F32 = mybir.dt.float32
BF16 = mybir.dt.bfloat16
U32 = mybir.dt.uint32
I32 = mybir.dt.int32
AF = mybir.ActivationFunctionType
ALU = mybir.AluOpType
AX = mybir.AxisListType


class Sched:
    ENG = ['pe', 'act', 'dve', 'pool', 'sp']

    def __init__(self, nc, es, n_dsem=40):
        self.nc = nc
        self.q = {e: [] for e in self.ENG}
        self.csem = {e: es.enter_context(nc.semaphore(f"c_{e}")) for e in ['pe', 'act', 'dve', 'pool']}
        self.cnt = {e: 0 for e in self.csem}
        self.dsem = [es.enter_context(nc.semaphore(f"d_{i}")) for i in range(n_dsem)]
        self.dval = [0] * n_dsem
        self.dnext = 0
        self.waited = {e: {} for e in self.ENG}
        self.lastw = {}
        self.readers = {}
        self.ninstr = 0

    def _semobj(self, semkey):
        return self.csem[semkey] if isinstance(semkey, str) else self.dsem[semkey]

    def _wait(self, eng, tok):
        semkey, val = tok
        if eng == 'pe' and semkey == 'pe':
            return
        if self.waited[eng].get(semkey, 0) >= val:
            return
        self.waited[eng][semkey] = val
        sem = self._semobj(semkey)
        self.q[eng].append(lambda e, sem=sem, val=val: e.wait_ge(sem, val))
        self.ninstr += 1

    def _deps(self, eng, reads, writes):
        toks = []
        for k in list(reads) + list(writes):
            if k in self.lastw:
                toks.append(self.lastw[k])
        for k in writes:
            for sk, v in self.readers.get(k, {}).items():
                toks.append((sk, v))
        for t in toks:
            self._wait(eng, t)

    def _commit(self, tok, reads, writes):
        for k in writes:
            self.lastw[k] = tok
            self.readers[k] = {}
        for k in reads:
            d = self.readers.setdefault(k, {})
            d[tok[0]] = max(d.get(tok[0], 0), tok[1])

    def op(self, eng, fn, reads=(), writes=()):
        self._deps(eng, reads, writes)
        self.cnt[eng] += 1
        tok = (eng, self.cnt[eng])
        sem = self.csem[eng]
        self.q[eng].append(lambda e, fn=fn, sem=sem: fn(e).then_inc(sem, 1))
        self.ninstr += 1
        self._commit(tok, reads, writes)

    def dma(self, eng, fn, reads=(), writes=()):
        self._deps(eng, reads, writes)
        k = self.dnext
        self.dnext = (self.dnext + 1) % len(self.dsem)
        if self.dval[k] > 0:
            self._wait(eng, (k, self.dval[k]))
        self.dval[k] += 16
        tok = (k, self.dval[k])
        sem = self.dsem[k]
        self.q[eng].append(lambda e, fn=fn, sem=sem: fn(e).then_inc(sem, 16))
        self.ninstr += 1
        self._commit(tok, reads, writes)

    def barrier(self):
        for e in self.ENG:
            for e2 in self.csem:
                if self.cnt[e2] > 0:
                    self._wait(e, (e2, self.cnt[e2]))
            for k in range(len(self.dsem)):
                if self.dval[k] > 0:
                    self._wait(e, (k, self.dval[k]))
        self.lastw = {}
        self.readers = {}

    def flush(self):
        nc = self.nc
        q = self.q
        with nc.Block() as block:
            @block.tensor
            def _(e):
                for f in q['pe']:
                    f(e)

            @block.scalar
            def _(e):
                for f in q['act']:
                    f(e)

            @block.vector
            def _(e):
                for f in q['dve']:
                    f(e)

            @block.gpsimd
            def _(e):
                for f in q['pool']:
                    f(e)

            @block.sync
            def _(e):
                for f in q['sp']:
                    f(e)
        self.q = {e: [] for e in self.ENG}


D = 2048
NTOK = 4096
TS = 512
NT = NTOK // TS
EPS = 1e-6
QSCALE = 128 ** -0.5


def build_nc(debug=False):
    nc = bass.Bass("TRN2", target_bir_lowering=False)

    def din(name, shape, dt=F32):
        return nc.dram_tensor(name, shape, dt, kind="ExternalInput").ap()

    def dscr(name, shape, dt=F32):
        kind = "ExternalOutput" if debug in (1, 2, 3) else "Internal"
        return nc.dram_tensor(name, shape, dt, kind=kind).ap()

    x_own = din("x_own", [NTOK, D])
    x_halo = din("x_halo", [512, D])
    x_ext = din("x_ext", [3, 4100, D])
    pT = din("pT", [256, NTOK])
    gains = din("gains", [4, D])
    w_in = din("w_in", [D, 9216])
    w_a = din("w_a", [1024, D])
    w_r = din("w_r", [1024, D])
    w_out = din("w_out", [D, D])
    w_q = din("w_q", [D, D])
    w_pg = din("w_pg", [D, D])
    w_pp = din("w_pp", [256, D])
    skT = din("skT", [16, 128, 128])
    peer_u = din("peer_u", [16384, D])
    peer_v = din("peer_v", [16384, D])
    natab = din("natab", [5, 8, 128, 640])
    cwb = din("cwb", [4, 128, 8, 6])
    lwa = din("lwa", [5, 8, 128, 128])
    lwx = din("lwx", [5, 8, 128, 128])
    lsm = din("lsm", [5, 128, 8, 3])
    flags = din("flags", [128, 16])
    ident = din("ident", [128, 128])
    y_own = nc.dram_tensor("y_own", [NTOK, D], F32, kind="ExternalOutput").ap()

    QT = dscr("QT", [1024, NTOK], BF16)
    KT = dscr("KT", [1024, NTOK + 512], BF16)
    VV = dscr("VV", [NTOK + 512, 1024], BF16)
    XR = dscr("XR", [1024, 4100])
    XRE = dscr("XRE", [3, 1024, 4100])
    GG = dscr("GG", [1024, NTOK])
    HF = dscr("HF", [1024, NTOK])
    YA = dscr("YA", [1024, NTOK], BF16)
    YR = dscr("YR", [1024, NTOK], BF16)
    UBF = nc.dram_tensor("UBF", [16384, D], BF16).ap()
    WIN1 = nc.dram_tensor("WIN1", [10, 128, 16, 512], BF16).ap()
    WIN2 = nc.dram_tensor("WIN2", [16, 128, 16, 256], BF16).ap()
    WAT = nc.dram_tensor("WAT", [8, 128, 8, 256], BF16).ap()
    WRT = nc.dram_tensor("WRT", [8, 128, 8, 256], BF16).ap()
    WOT = nc.dram_tensor("WOT", [8, 128, 16, 256], BF16).ap()
    WQT = nc.dram_tensor("WQT", [8, 128, 16, 256], BF16).ap()
    WPGT = nc.dram_tensor("WPGT", [8, 128, 16, 256], BF16).ap()
    WPPT = nc.dram_tensor("WPPT", [8, 128, 2, 256], BF16).ap()
    VBF = nc.dram_tensor("VBF", [16384, D], BF16).ap()

    with ExitStack() as es:
        S = Sched(nc, es)

        def sbuf(st, name, shape, dt=F32):
            return st.enter_context(nc.sbuf_tensor(name, shape, dt))

        def psum(st, name, shape, dt=F32):
            return st.enter_context(nc.psum_tensor(name, shape, dt))

        identf = sbuf(es, "identf", [128, 128])
        identb = sbuf(es, "identb", [128, 128], BF16)
        S.dma('sp', lambda e: e.dma_start(out=identf[:], in_=ident), writes=['identf'])
        S.op('dve', lambda e: e.tensor_copy(out=identb[:], in_=identf[:]), reads=['identf'], writes=['identb'])

        cnt = {'evac': 0}

        def evac_copy(out_ap, in_ap, reads, writes, scale=None, func=None):
            cnt['evac'] += 1
            if func is not None:
                S.op('act', lambda e: e.activation(out=out_ap, in_=in_ap, func=func), reads=reads, writes=writes)
            elif scale is not None:
                if cnt['evac'] % 2:
                    S.op('act', lambda e: e.mul(out=out_ap, in_=in_ap, mul=scale) if False else e.activation(out=out_ap, in_=in_ap, func=AF.Copy, scale=scale), reads=reads, writes=writes)
                else:
                    S.op('dve', lambda e: e.tensor_scalar(out=out_ap, in0=in_ap, scalar1=scale, scalar2=None, op0=ALU.mult), reads=reads, writes=writes)
            else:
                if cnt['evac'] % 2:
                    S.op('act', lambda e: e.copy(out=out_ap, in_=in_ap), reads=reads, writes=writes)
                else:
                    S.op('dve', lambda e: e.tensor_copy(out=out_ap, in_=in_ap), reads=reads, writes=writes)

        def load_gain(G, k):
            S.dma('sp', lambda e: e.dma_start(out=G[:], in_=gains[k:k + 1, :].partition_broadcast(128)), writes=['G', 'J0', 'J1'])

        def rms_norm_blocks(XT, XB, G, SM, nblk=4, stats_only=False, xbkey='XB'):
            for b in range(nblk):
                S.op('act', lambda e, b=b: e.activation(out=XB[:, b, :], in_=XT[:, b, :], func=AF.Square, accum_out=SM[:, b:b + 1]),
                     reads=['XT'], writes=[xbkey, 'SM'])
            S.op('dve', lambda e: e.tensor_scalar(out=SM[:, 4:4 + nblk], in0=SM[:, 0:nblk], scalar1=1.0 / D, scalar2=EPS, op0=ALU.mult, op1=ALU.add),
                 reads=['SM'], writes=['SM'])
            S.op('act', lambda e: e.activation(out=SM[:, 8:8 + nblk], in_=SM[:, 4:4 + nblk], func=AF.Sqrt), reads=['SM'], writes=['SM'])
            S.op('dve', lambda e: e.reciprocal(out=SM[:, 12:12 + nblk], in_=SM[:, 8:8 + nblk]), reads=['SM'], writes=['SM'])
            for b in range(0 if stats_only else nblk):
                S.op('dve', lambda e, b=b: e.scalar_tensor_tensor(out=XB[:, b, :], in0=XT[:, b, :], scalar=SM[:, 12 + b:13 + b], in1=G[:],
                                                                  op0=ALU.mult, op1=ALU.mult),
                     reads=['XT', 'SM', 'G'], writes=[xbkey])

        def transpose_blocks(XB, XNT, PTs, nblk=4, xbkey='XB'):
            for c in range(16):
                pt = PTs[c % 2]
                key = f'pt{c % 2}'
                for b in range(nblk):
                    S.op('pe', lambda e, b=b, c=c, pt=pt: e.transpose(pt[:, b * 128:(b + 1) * 128], XB[:, b, c * 128:(c + 1) * 128], identb[:]),
                         reads=[xbkey, 'identb'], writes=[key])
                evac_copy(XNT[:, c, 0:nblk * 128], pt[:, 0:nblk * 128], reads=[key], writes=['XNT'])


        with ExitStack() as ph:
            CV0 = [sbuf(ph, f"p0_CV{i}", [128, 8192], BF16) for i in range(2)]
            p0 = {'k': 0}

            def conv_w(w, c_lo, nr, cchunk, T, cw, g_off=0):
                K = w.shape[0] // 128
                ncols = T.shape[0] * cw - g_off * cw
                wv = w.rearrange("(r p) c -> p r c", p=128)
                Tv = T.rearrange("g p kc j -> p kc g j")
                for r0 in range(0, K, nr):
                    for c0 in range(0, ncols, cchunk):
                        j = p0['k'] % 2
                        p0['k'] += 1
                        ng = cchunk // cw
                        stg = CV0[j][:, 0:nr * cchunk]
                        S.dma('pool', lambda e, r0=r0, c0=c0, stg=stg: e.dma_start(out=stg.rearrange("p (r c) -> p r c", r=nr),
                                                                                 in_=wv[:, r0:r0 + nr, c_lo + c0:c_lo + c0 + cchunk]), writes=[f'cv{j}'])
                        g0 = g_off + c0 // cw
                        for gi in range(ng):
                            S.dma('sp', lambda e, r0=r0, g0=g0, gi=gi, ng=ng, stg=stg: e.dma_start(out=Tv[:, r0:r0 + nr, g0 + gi, :],
                                                                                               in_=stg.rearrange("p (r g j) -> p r g j", r=nr, g=ng)[:, :, gi, :]), reads=[f'cv{j}'])

            conv_w(w_in, 0, 8, 1024, WIN1, 512)
            conv_w(w_in, 5120, 8, 1024, WIN2, 256)
            conv_w(w_a, 0, 8, 1024, WAT, 256)
            conv_w(w_r, 0, 8, 1024, WRT, 256)
            conv_w(w_out, 0, 8, 1024, WOT, 256)
            conv_w(w_q, 0, 8, 1024, WQT, 256)
            conv_w(w_pg, 0, 8, 1024, WPGT, 256)
            conv_w(w_pp, 0, 2, 2048, WPPT, 256)
            S.barrier()
            S.flush()

        with ExitStack() as ph:
            XT = sbuf(ph, "p1_XT", [128, 4, D])
            XB = sbuf(ph, "p1_XB", [128, 4, D], BF16)
            XNT = sbuf(ph, "p1_XNT", [128, 16, TS], BF16)
            WG = [sbuf(ph, f"p1_WG{i}", [128, 16, 512], BF16) for i in range(2)]
            WGX = sbuf(ph, "p1_WGX", [128, 16, 1024], BF16)
            STB = [sbuf(ph, f"p1_STB{i}", [128, 4, 512], BF16) for i in range(2)]
            STF = [sbuf(ph, f"p1_STF{i}", [128, 4, 512]) for i in range(2)]
            G = sbuf(ph, "p1_G", [128, D])
            SM = sbuf(ph, "p1_SM", [128, 16])
            PSB = [psum(ph, f"p1_ps{i}", [128, 512]) for i in range(4)]
            PTf = [psum(ph, f"p1_pt{i}", [128, 512]) for i in range(2)]
            PTs = [p[:, :].bitcast(BF16) for p in PTf]
            load_gain(G, 0)
            st = {'wg': 0, 'ps': 0, 'stb': 0, 'stf': 0, 'cv': 0}
            CVB = [sbuf(ph, f"p1_CV{i}", [128, 2, D], BF16) for i in range(2)]

            def conv_steps(n):
                for _ in range(n):
                    k = st['cv']
                    if k >= 128:
                        return
                    st['cv'] += 1
                    src, dst = (peer_u, UBF) if k < 64 else (peer_v, VBF)
                    kk = k % 64
                    j = k % 2
                    sv = src.rearrange("(r p) d -> p r d", p=128)
                    dv = dst.rearrange("(r p) d -> p r d", p=128)
                    S.dma('pool', lambda e, sv=sv, kk=kk, j=j: e.dma_start(out=CVB[j][:, :, :], in_=sv[:, kk * 2:(kk + 1) * 2, :]), writes=[f'cv{j}'])
                    S.dma('pool', lambda e, dv=dv, kk=kk, j=j: e.dma_start(out=dv[:, kk * 2:(kk + 1) * 2, :], in_=CVB[j][:, :, :]), reads=[f'cv{j}'])

            def load_x(src_ap, nrows):
                nb = (nrows + 127) // 128
                if nrows % 128 == 0:
                    S.dma('sp', lambda e: e.dma_start(out=XT[:, 0:nb, :], in_=src_ap.rearrange("(b p) d -> p b d", p=128)), writes=['XT'])
                else:
                    S.op('dve', lambda e: e.memset(XT[:, 0, :], 0.0), writes=['XT'])
                    S.dma('sp', lambda e: e.dma_start(out=XT[0:nrows, 0, :], in_=src_ap), writes=['XT'])
                return nb

            def fm_group(wt, wkey, wc0, nb, kind, dst_fn):
                ntok = nb * 128
                if kind in ('xr', 'gg'):
                    stg = STF[st['stf'] % 2]; skey = f"stf{st['stf'] % 2}"; st['stf'] += 1
                else:
                    stg = STB[st['stb'] % 2]; skey = f"stb{st['stb'] % 2}"; st['stb'] += 1
                for oc in range(4):
                    ps = PSB[st['ps'] % 4]; pkey = f"ps{st['ps'] % 4}"; st['ps'] += 1
                    for kc in range(16):
                        S.op('pe', lambda e, ps=ps, kc=kc, oc=oc: e.matmul(ps[:, 0:ntok], wt[:, kc, wc0 + oc * 128: wc0 + (oc + 1) * 128], XNT[:, kc, 0:ntok],
                                                                         start=(kc == 0), stop=(kc == 15)),
                             reads=[wkey, 'XNT'], writes=[pkey])
                    if kind == 'q':
                        evac_copy(stg[:, oc, 0:ntok], ps[:, 0:ntok], [pkey], [skey], scale=QSCALE)
                    elif kind == 'gg':
                        evac_copy(stg[:, oc, 0:ntok], ps[:, 0:ntok], [pkey], [skey], func=AF.Gelu)
                    else:
                        evac_copy(stg[:, oc, 0:ntok], ps[:, 0:ntok], [pkey], [skey])
                dst_fn(stg, skey)

            def tm_group(wt, wkey, nb, dst_fn):
                stg = STB[st['stb'] % 2]; skey = f"stb{st['stb'] % 2}"; st['stb'] += 1
                for b in range(nb):
                    ps = PSB[st['ps'] % 4]; pkey = f"ps{st['ps'] % 4}"; st['ps'] += 1
                    for kc in range(16):
                        S.op('pe', lambda e, ps=ps, kc=kc, b=b: e.matmul(ps[:, :], XNT[:, kc, b * 128:(b + 1) * 128], wt[:, kc, 0:512],
                                                                       start=(kc == 0), stop=(kc == 15)),
                             reads=[wkey, 'XNT'], writes=[pkey])
                    evac_copy(stg[:, b, :], ps[:, :], [pkey], [skey])
                dst_fn(stg, skey)

            def load_wg(col0):
                i = st['wg'] % 2; st['wg'] += 1
                S.dma('sp', lambda e: e.dma_start(out=WG[i][:, :, :], in_=WIN1[col0 // 512]), writes=[f'wg{i}'])
                return WG[i], f'wg{i}'

            def fm_store(dst, r0, t0, ntok):
                def f(stg, skey):
                    S.dma('act', lambda e: e.dma_start(out=dst[r0:r0 + 512, t0:t0 + ntok].rearrange("(c p) t -> p c t", p=128), in_=stg[:, :, 0:ntok]),
                          reads=[skey])
                return f

            def tm_store(dst, t0, c0, nb):
                def f(stg, skey):
                    S.dma('act', lambda e: e.dma_start(out=dst[t0:t0 + nb * 128, c0:c0 + 512].rearrange("(b p) c -> p b c", p=128), in_=stg[:, 0:nb, :]),
                          reads=[skey])
                return f

            def halo_k_store(half):
                def f(stg, skey):
                    S.dma('act', lambda e: e.dma_start(out=KT[half:half + 512, 0:256].rearrange("(c p) t -> p c t", p=128), in_=stg[:, :, 0:256]),
                          reads=[skey])
                    S.dma('act', lambda e: e.dma_start(out=KT[half:half + 512, 256 + NTOK:512 + NTOK].rearrange("(c p) t -> p c t", p=128), in_=stg[:, :, 256:512]),
                          reads=[skey])
                return f

            def halo_v_store(half):
                def f(stg, skey):
                    S.dma('act', lambda e: e.dma_start(out=VV[0:256, half:half + 512].rearrange("(b p) c -> p b c", p=128), in_=stg[:, 0:2, :]),
                          reads=[skey])
                    S.dma('act', lambda e: e.dma_start(out=VV[256 + NTOK:512 + NTOK, half:half + 512].rearrange("(b p) c -> p b c", p=128), in_=stg[:, 2:4, :]),
                          reads=[skey])
                return f

            def halo_xr_store(half):
                def f(stg, skey):
                    S.dma('act', lambda e: e.dma_start(out=XR[half:half + 512, 0:2].rearrange("(c p) t -> p c t", p=128), in_=stg[:, :, 254:256]),
                          reads=[skey])
                    S.dma('act', lambda e: e.dma_start(out=XR[half:half + 512, 4098:4100].rearrange("(c p) t -> p c t", p=128), in_=stg[:, :, 256:258]),
                          reads=[skey])
                return f


            XB2 = [XB, sbuf(ph, "p1_XBb", [128, 4, D], BF16)]
            tiles = [('own', i) for i in range(NT)] + [('halo', 0)] + [('ext', s, i) for s in range(3) for i in range(9)]

            def tile_src(tl):
                if tl[0] == 'own':
                    return x_own[tl[1] * TS:(tl[1] + 1) * TS, :], TS
                if tl[0] == 'halo':
                    return x_halo[:, :], 512
                s_, i_ = tl[1], tl[2]
                nrows = TS if i_ < 8 else 4
                return x_ext[s_, i_ * TS:i_ * TS + nrows, :], nrows

            def prep(k):
                src, nrows = tile_src(tiles[k])
                conv_steps(4)
                nb = load_x(src, nrows)
                rms_norm_blocks(XT, XB2[k % 2], G, SM, nblk=nb, xbkey=f'XB{k % 2}')
                return nb

            nbs = {0: prep(0)}
            for k, tl in enumerate(tiles):
                nb = nbs[k]
                transpose_blocks(XB2[k % 2], XNT, PTs, nblk=nb, xbkey=f'XB{k % 2}')
                if k + 1 < len(tiles):
                    nbs[k + 1] = prep(k + 1)
                if tl[0] == 'own':
                    t0 = tl[1] * TS
                    for g in range(10):
                        wt, wkey = load_wg(g * 512)
                        half = (g % 2) * 512
                        if g < 2:
                            fm_group(wt, wkey, 0, 4, 'q', fm_store(QT, half, t0, TS))
                        elif g < 4:
                            fm_group(wt, wkey, 0, 4, 'k', fm_store(KT, half, 256 + t0, TS))
                        elif g < 6:
                            tm_group(wt, wkey, 4, tm_store(VV, 256 + t0, half, 4))
                        elif g < 8:
                            fm_group(wt, wkey, 0, 4, 'xr', fm_store(XR, half, 2 + t0, TS))
                        else:
                            fm_group(wt, wkey, 0, 4, 'gg', fm_store(GG, half, t0, TS))
                elif tl[0] == 'halo':
                    for g in (2, 3, 4, 5, 6, 7):
                        wt, wkey = load_wg(g * 512)
                        half = (g % 2) * 512
                        if g < 4:
                            fm_group(wt, wkey, 0, 4, 'k', halo_k_store(half))
                        elif g < 6:
                            tm_group(wt, wkey, 4, halo_v_store(half))
                        else:
                            fm_group(wt, wkey, 0, 4, 'xr', halo_xr_store(half))
                    S.dma('sp', lambda e: e.dma_start(out=WGX[:, :, 0:512], in_=WIN1[6]), writes=['wgx'])
                    S.dma('sp', lambda e: e.dma_start(out=WGX[:, :, 512:1024], in_=WIN1[7]), writes=['wgx'])
                else:
                    s_, i_ = tl[1], tl[2]
                    t0 = i_ * TS
                    for hh in range(2):
                        if i_ < 8:
                            fm_group(WGX, 'wgx', hh * 512, 4, 'xr', fm_store(XRE[s_], hh * 512, t0, TS))
                        else:
                            fm_group(WGX, 'wgx', hh * 512, 1, 'xr', fm_store(XRE[s_], hh * 512, t0, 4))
            conv_steps(128)
            S.barrier()
            S.flush()
        if debug == 1:
            return nc
        with ExitStack() as ph:
            TAB = sbuf(ph, "na_TAB", [128, 40, 640], BF16)
            for ty in range(5):
                S.dma('pool', lambda e, ty=ty: e.dma_start(out=TAB[:, ty * 8:(ty + 1) * 8, :], in_=natab[ty].rearrange("h q k -> q h k")), writes=['TAB'])
            Qt = [sbuf(ph, f"na_Q{i}", [128, 8, 128], BF16) for i in range(2)]
            Kt = [sbuf(ph, f"na_K{i}", [128, 8, 640], BF16) for i in range(2)]
            Vt = [sbuf(ph, f"na_V{i}", [128, 5, 1024], BF16) for i in range(2)]
            YAo = [sbuf(ph, f"na_YA{i}", [128, 8, 128], BF16) for i in range(2)]
            Pf = [sbuf(ph, f"na_Pf{i}", [128, 640], BF16) for i in range(2)]
            Pn = [sbuf(ph, f"na_Pn{i}", [128, 640], BF16) for i in range(2)]
            PTs_ = [sbuf(ph, f"na_PT{i}", [128, 5, 128], BF16) for i in range(2)]
            NM = sbuf(ph, "na_NM", [128, 8])
            psS = [psum(ph, f"na_psS{i}", [128, 1024]) for i in range(2)]
            psPTf = [psum(ph, f"na_psPT{i}", [128, 512]) for i in range(2)]
            psPT = [p[:, :].bitcast(BF16) for p in psPTf]
            psO = [psum(ph, f"na_psO{i}", [128, 512]) for i in range(2)]
            nau = []
            for P in range(32):
                for h in range(8):
                    nau.append((P, h))

            def naA(k):
                P, h = nau[k]
                j, i = P % 2, k % 2
                ty = {0: 1, 1: 2, 30: 3, 31: 4}.get(P, 0)
                if h == 0:
                    S.dma('sp', lambda e: e.dma_start(out=Qt[j][:, :, :], in_=QT[:, 128 * P:128 * P + 128].rearrange("(h d) t -> d h t", d=128)), writes=[f'Q{j}'])
                    S.dma('sp', lambda e: e.dma_start(out=Kt[j][:, :, :], in_=KT[:, 128 * P:128 * P + 640].rearrange("(h d) t -> d h t", d=128)), writes=[f'K{j}'])
                    S.dma('sp', lambda e: e.dma_start(out=Vt[j][:, :, :], in_=VV[128 * P:128 * P + 640, :].rearrange("(c p) f -> p c f", p=128)), writes=[f'V{j}'])
                sp_, pk = psS[i], f'psS{i}'
                S.op('pe', lambda e: e.matmul(sp_[:, 0:512], Qt[j][:, h, :], Kt[j][:, h, 0:512], start=True, stop=False), reads=[f'Q{j}', f'K{j}'], writes=[pk])
                S.op('pe', lambda e: e.matmul(sp_[:, 0:512], identb[:], TAB[:, ty * 8 + h, 0:512], start=False, stop=True), reads=['TAB', 'identb'], writes=[pk])
                S.op('pe', lambda e: e.matmul(sp_[:, 512:640], Qt[j][:, h, :], Kt[j][:, h, 512:640], start=True, stop=False), reads=[f'Q{j}', f'K{j}'], writes=[pk])
                S.op('pe', lambda e: e.matmul(sp_[:, 512:640], identb[:], TAB[:, ty * 8 + h, 512:640], start=False, stop=True), reads=['TAB', 'identb'], writes=[pk])

            def naB(k):
                i = k % 2
                sp_, pk, nk = psS[i], f'psS{i}', f'NM{i}'
                S.op('dve', lambda e: e.reduce_max(out=NM[:, i:i + 1], in_=sp_[:, 0:640], axis=AX.X, negate=True), reads=[pk], writes=[nk])
                S.op('act', lambda e: e.activation(out=Pf[i][:], in_=sp_[:, 0:640], func=AF.Exp, bias=NM[:, i:i + 1], scale=1.0, accum_out=NM[:, 2 + i:3 + i]),
                     reads=[pk, nk], writes=[f'Pf{i}', f'NS{i}'])
                S.op('dve', lambda e: e.reciprocal(out=NM[:, 4 + i:5 + i], in_=NM[:, 2 + i:3 + i]), reads=[f'NS{i}'], writes=[f'NR{i}'])
                S.op('dve', lambda e: e.tensor_scalar(out=Pn[i][:], in0=Pf[i][:], scalar1=NM[:, 4 + i:5 + i], scalar2=None, op0=ALU.mult),
                     reads=[f'Pf{i}', f'NR{i}'], writes=[f'Pn{i}'])

            def naC(k):
                i = k % 2
                for c in range(5):
                    S.op('pe', lambda e, c=c: e.transpose(psPT[i][:, c * 128:(c + 1) * 128], Pn[i][:, c * 128:(c + 1) * 128], identb[:]),
                         reads=[f'Pn{i}', 'identb'], writes=[f'psPT{i}'])
                evac_copy(PTs_[i][:, :, :].rearrange("p c q -> p (c q)"), psPT[i][:, 0:640], [f'psPT{i}'], [f'PT{i}'])

            def naD(k):
                P, h = nau[k]
                j, i = P % 2, k % 2
                for c in range(5):
                    S.op('pe', lambda e, c=c: e.matmul(psO[i][:, 0:128], Vt[j][:, c, h * 128:(h + 1) * 128], PTs_[i][:, c, :], start=(c == 0), stop=(c == 4)),
                         reads=[f'V{j}', f'PT{i}'], writes=[f'psO{i}'])
                evac_copy(YAo[j][:, h, :], psO[i][:, 0:128], [f'psO{i}'], [f'YA{j}'])
                if h == 7:
                    S.dma('act', lambda e: e.dma_start(out=YA[:, 128 * P:128 * P + 128].rearrange("(h d) t -> d h t", d=128), in_=YAo[j][:, :, :]), reads=[f'YA{j}'])

            nn = len(nau)
            for kk in range(nn + 3):
                if kk < nn:
                    naA(kk)
                if 0 <= kk - 1 < nn:
                    naB(kk - 1)
                if 0 <= kk - 2 < nn:
                    naC(kk - 2)
                if 0 <= kk - 3 < nn:
                    naD(kk - 3)
            S.barrier()
            S.flush()
        if debug == 2:
            return nc
        with ExitStack() as ph:
            WA = sbuf(ph, "l_WA", [128, 5, 8, 128], BF16)
            WX = sbuf(ph, "l_WX", [128, 5, 8, 128], BF16)
            S.dma('pool', lambda e: e.dma_start(out=WA[:, :, :, :], in_=lwa.rearrange("u n c d -> c u n d")), writes=['WA'])
            S.dma('pool', lambda e: e.dma_start(out=WX[:, :, :, :], in_=lwx.rearrange("u n c d -> c u n d")), writes=['WX'])
            LSM = sbuf(ph, "l_LSM", [128, 5, 8, 3])
            S.dma('sp', lambda e: e.dma_start(out=LSM[:, :, :, :], in_=lsm.rearrange("u p n k -> p u n k")), writes=['LSM'])
            CWB = sbuf(ph, "l_CWB", [128, 4, 8, 6])
            S.dma('sp', lambda e: e.dma_start(out=CWB[:, :, :, :], in_=cwb.rearrange("u p n k -> p u n k")), writes=['CWB'])
            FL = sbuf(ph, "l_FL", [128, 16])
            S.dma('sp', lambda e: e.dma_start(out=FL[:, :], in_=flags), writes=['FL'])
            C8 = sbuf(ph, "l_C8", [128, 5, 8])
            E1 = sbuf(ph, "l_E1", [128, 5, 8])
            S.op('act', lambda e: e.activation(out=E1[:, :, :], in_=LSM[:, :, :, 2], func=AF.Exp, scale=-1.0), reads=['LSM'], writes=['E1'])
            S.op('act', lambda e: e.activation(out=E1[:, :, :], in_=E1[:, :, :], func=AF.Ln, bias=1.0, scale=1.0), reads=['E1'], writes=['E1'])
            S.op('dve', lambda e: e.tensor_scalar(out=C8[:, :, :], in0=E1[:, :, :], scalar1=-8.0, scalar2=None, op0=ALU.mult), reads=['E1'], writes=['C8'])
            STt = sbuf(ph, "l_ST", [128, 8])
            ACCF = sbuf(ph, "l_ACCF", [128, 8])
            ACCB = sbuf(ph, "l_ACCB", [128, 8])
            for t_, k_ in ((STt, 'ST'), (ACCF, 'ACCF'), (ACCB, 'ACCB')):
                S.op('dve', lambda e, t_=t_: e.memset(t_[:, :], 0.0), writes=[k_])
            NB = 6
            XRt = [sbuf(ph, f"l_XR{i}", [128, 516]) for i in range(NB)]
            XC = [sbuf(ph, f"l_XC{i}", [128, 512]) for i in range(NB)]
            XCB = [sbuf(ph, f"l_XCB{i}", [128, 512], BF16) for i in range(NB)]
            REC = [sbuf(ph, f"l_REC{i}", [128, 512]) for i in range(NB)]
            INP = [sbuf(ph, f"l_INP{i}", [128, 512]) for i in range(NB)]
            AA = [sbuf(ph, f"l_A{i}", [128, 512]) for i in range(NB)]
            T1 = [sbuf(ph, f"l_T1{i}", [128, 512]) for i in range(NB)]
            UU = [sbuf(ph, f"l_U{i}", [128, 512]) for i in range(NB)]
            HH = [sbuf(ph, f"l_H{i}", [128, 512]) for i in range(NB)]
            HFt = [sbuf(ph, f"l_HF{i}", [128, 512]) for i in range(NB)]
            GGt = [sbuf(ph, f"l_GG{i}", [128, 512]) for i in range(NB)]
            YRt = [sbuf(ph, f"l_YR{i}", [128, 512], BF16) for i in range(NB)]
            psA = [psum(ph, f"l_psA{i}", [128, 512]) for i in range(4)]
            psX = [psum(ph, f"l_psX{i}", [128, 512]) for i in range(4)]
            units = []

            def mk(uc, ug, scr, i, n, state, skey, reverse, valid_ap, vkeys, kind, pre=None, fin=None):
                k = len(units)
                units.append(dict(uc=uc, ug=ug, scr=scr, i=i, n=n, state=state, skey=skey, reverse=reverse, valid_ap=valid_ap, vkeys=vkeys,
                                  kind=kind, pre=pre, fin=fin, j=k % NB, pj=k % 4))

            def stA(u):
                j, pj, n, i, uc, ug = u['j'], u['pj'], u['n'], u['i'], u['uc'], u['ug']
                k = lambda s_: f'{s_}{j}'
                scr = u['scr']
                S.dma('sp', lambda e: e.dma_start(out=XRt[j][:, :], in_=scr[n * 128:(n + 1) * 128, 512 * i:512 * i + 516]), writes=[k('XR')])
                if u['kind'] == 'bwd':
                    S.dma('sp', lambda e: e.dma_start(out=HFt[j][:, :], in_=HF[n * 128:(n + 1) * 128, 512 * i:512 * i + 512]), reads=[f'HF_{i}_{n}'], writes=[k('HFt')])
                    S.dma('sp', lambda e: e.dma_start(out=GGt[j][:, :], in_=GG[n * 128:(n + 1) * 128, 512 * i:512 * i + 512]), writes=[k('GGt')])
                S.op('dve', lambda e: e.tensor_scalar(out=XC[j][:, :], in0=XRt[j][:, 0:512], scalar1=CWB[:, uc, n, 0:1], scalar2=CWB[:, uc, n, 5:6],
                                                      op0=ALU.mult, op1=ALU.add), reads=[k('XR'), 'CWB'], writes=[k('XC')])
                for t in range(1, 5):
                    S.op('dve', lambda e, t=t: e.scalar_tensor_tensor(out=XC[j][:, :], in0=XRt[j][:, t:t + 512], scalar=CWB[:, uc, n, t:t + 1], in1=XC[j][:, :],
                                                                      op0=ALU.mult, op1=ALU.add), reads=[k('XR'), k('XC'), 'CWB'], writes=[k('XC')])
                S.op('act', lambda e: e.copy(out=XCB[j][:, :], in_=XC[j][:, :]), reads=[k('XC')], writes=[k('XCB')])
                S.op('pe', lambda e: e.matmul(psA[pj][:, :], WA[:, ug, n, :], XCB[j][:, :], start=True, stop=True), reads=['WA', k('XCB')], writes=[f'psA{pj}'])
                S.op('pe', lambda e: e.matmul(psX[pj][:, :], WX[:, ug, n, :], XCB[j][:, :], start=True, stop=True), reads=['WX', k('XCB')], writes=[f'psX{pj}'])

            def stB1(u):
                j, pj, n, ug = u['j'], u['pj'], u['n'], u['ug']
                k = lambda s_: f'{s_}{j}'
                S.op('act', lambda e: e.activation(out=REC[j][:, :], in_=psA[pj][:, :], func=AF.Sigmoid, bias=LSM[:, ug, n, 0:1], scale=1.0),
                     reads=[f'psA{pj}', 'LSM'], writes=[k('REC')])
                S.op('act', lambda e: e.activation(out=INP[j][:, :], in_=psX[pj][:, :], func=AF.Sigmoid, bias=LSM[:, ug, n, 1:2], scale=1.0),
                     reads=[f'psX{pj}', 'LSM'], writes=[k('INP')])
                S.op('act', lambda e: e.activation(out=AA[j][:, :], in_=REC[j][:, :], func=AF.Exp, scale=C8[:, ug, n:n + 1]),
                     reads=[k('REC'), 'C8'], writes=[k('A')])
                S.op('pool', lambda e: e.tensor_tensor(out=T1[j][:, :], in0=AA[j][:, :], in1=AA[j][:, :], op=ALU.mult), reads=[k('A')], writes=[k('T1')])
                S.op('pool', lambda e: e.tensor_tensor(out=UU[j][:, :], in0=INP[j][:, :], in1=XC[j][:, :], op=ALU.mult), reads=[k('INP'), k('XC')], writes=[k('U')])

            def stB2(u):
                j = u['j']
                k = lambda s_: f'{s_}{j}'
                S.op('dve', lambda e: e.tensor_scalar(out=T1[j][:, :], in0=T1[j][:, :], scalar1=-1.0, scalar2=1.0, op0=ALU.mult, op1=ALU.add),
                     reads=[k('T1')], writes=[k('T1')])
                S.op('act', lambda e: e.activation(out=T1[j][:, :], in_=T1[j][:, :], func=AF.Sqrt), reads=[k('T1')], writes=[k('T1')])

            def stC(u):
                j, n, i = u['j'], u['n'], u['i']
                k = lambda s_: f'{s_}{j}'
                state, skey = u['state'], u['skey']
                if u['pre'] is not None:
                    u['pre']()
                S.op('dve', lambda e: e.scalar_tensor_tensor(out=UU[j][:, :], in0=T1[j][:, :], scalar=u['valid_ap'], in1=UU[j][:, :], op0=ALU.mult, op1=ALU.mult),
                     reads=[k('T1'), k('U')] + u['vkeys'], writes=[k('U')])
                sk = f'{skey}{n}'
                if not u['reverse']:
                    S.op('dve', lambda e: e.tensor_tensor_scan(out=HH[j][:, :], data0=AA[j][:, :], data1=UU[j][:, :], initial=state[:, n:n + 1],
                                                               op0=ALU.mult, op1=ALU.add), reads=[k('A'), k('U'), sk, skey], writes=[k('H')])
                    S.op('act', lambda e: e.copy(out=state[:, n:n + 1], in_=HH[j][:, 511:512]), reads=[k('H')], writes=[sk])
                else:
                    S.op('dve', lambda e: e.tensor_tensor_scan(out=HH[j][:, ::-1], data0=AA[j][:, ::-1], data1=UU[j][:, ::-1], initial=state[:, n:n + 1],
                                                               op0=ALU.mult, op1=ALU.add), reads=[k('A'), k('U'), sk, skey], writes=[k('H')])
                    S.op('act', lambda e: e.copy(out=state[:, n:n + 1], in_=HH[j][:, 0:1]), reads=[k('H')], writes=[sk])
                if u['kind'] == 'fwd':
                    S.dma('sp', lambda e: e.dma_start(out=HF[n * 128:(n + 1) * 128, 512 * i:512 * i + 512], in_=HH[j][:, :]), reads=[k('H')], writes=[f'HF_{i}_{n}'])
                elif u['kind'] == 'bwd':
                    S.op('pool', lambda e: e.tensor_tensor(out=HFt[j][:, :], in0=HFt[j][:, :], in1=HH[j][:, :], op=ALU.add), reads=[k('HFt'), k('H')], writes=[k('HFt')])
                    S.op('pool', lambda e: e.tensor_tensor(out=YRt[j][:, :], in0=HFt[j][:, :], in1=GGt[j][:, :], op=ALU.mult), reads=[k('HFt'), k('GGt')], writes=[k('YR')])
                    S.dma('sp', lambda e: e.dma_start(out=YR[n * 128:(n + 1) * 128, 512 * i:512 * i + 512], in_=YRt[j][:, :]), reads=[k('YR')])
                if u['fin'] is not None:
                    u['fin']()

            ST_ALL = ['ST'] + [f'ST{n}' for n in range(8)]

            def slot_pre(s):
                def f():
                    S.op('dve', lambda e: e.tensor_scalar(out=STt[:, :], in0=STt[:, :], scalar1=FL[:, s:s + 1], scalar2=None, op0=ALU.mult),
                         reads=ST_ALL + ['FL'], writes=ST_ALL)
                return f

            def slot_fin(s):
                def f():
                    S.op('dve', lambda e: e.scalar_tensor_tensor(out=ACCF[:, :], in0=STt[:, :], scalar=FL[:, 6 + s:7 + s], in1=ACCF[:, :], op0=ALU.mult, op1=ALU.add),
                         reads=ST_ALL + ['FL', 'ACCF'], writes=['ACCF'])
                    S.op('dve', lambda e: e.scalar_tensor_tensor(out=ACCB[:, :], in0=STt[:, :], scalar=FL[:, 9 + s:10 + s], in1=ACCB[:, :], op0=ALU.mult, op1=ALU.add),
                         reads=ST_ALL + ['FL', 'ACCB'], writes=['ACCB'])
                return f

            for s in range(3):
                for i in range(8):
                    for n in range(8):
                        mk(s, s, XRE[s], i, n, STt, 'ST', False, FL[:, 3 + s:4 + s], ['FL'], 'ext',
                           pre=slot_pre(s) if (i == 0 and n == 0) else None, fin=slot_fin(s) if (i == 7 and n == 7) else None)
            for i in range(8):
                for n in range(8):
                    mk(3, 3, XR, i, n, ACCF, 'ACCF', False, 1.0, [], 'fwd')
            for i in range(7, -1, -1):
                for n in range(8):
                    mk(3, 4, XR, i, n, ACCB, 'ACCB', True, 1.0, [], 'bwd')
            nu = len(units)
            for kk in range(nu + 3):
                if kk < nu:
                    stA(units[kk])
                if 0 <= kk - 1 < nu:
                    stB1(units[kk - 1])
                if 0 <= kk - 2 < nu:
                    stB2(units[kk - 2])
                if 0 <= kk - 3 < nu:
                    stC(units[kk - 3])
            S.barrier()
            S.flush()
        if debug == 3:
            return nc
        with ExitStack() as ph:
            XT = sbuf(ph, "f_XT", [128, 4, D])
            XB = sbuf(ph, "f_XB", [128, 4, D], BF16)
            XNT = sbuf(ph, "f_XNT", [128, 16, TS], BF16)
            WG = [sbuf(ph, f"f_WG{i}", [128, 16, 256], BF16) for i in range(2)]
            WGA = sbuf(ph, "f_WGA", [128, 8, 256], BF16)
            WGR = sbuf(ph, "f_WGR", [128, 8, 256], BF16)
            WPP = sbuf(ph, "f_WPP", [128, 2, 256], BF16)
            MT = sbuf(ph, "f_MT", [128, 16, TS], BF16)
            NG = 6
            BIG = sbuf(ph, "f_BIG", [128, 2 * NG * D], BF16)
            UB = [BIG[:, j * D:(j + 1) * D] for j in range(NG)]
            VB = [BIG[:, (NG + j) * D:(NG + 1 + j) * D] for j in range(NG)]
            YAT = BIG[:, 0:2 * D].rearrange("p (c t) -> p c t", c=8)
            YRT = BIG[:, 2 * D:4 * D].rearrange("p (c t) -> p c t", c=8)
            OUTB = [BIG[:, (NG + 2 * j) * D:(NG + 2 + 2 * j) * D].bitcast(F32) for j in range(2)]
            G = sbuf(ph, "f_G", [128, D])
            JUNK = G[:, 0:1024].bitcast(BF16)
            SGA = sbuf(ph, "f_SGA", [128, 512])
            SGB = sbuf(ph, "f_SGB", [128, 512])
            TT = sbuf(ph, "f_TT", [128, 512])
            PTt = sbuf(ph, "f_PTt", [128, 2, TS], BF16)
            SM = sbuf(ph, "f_SM", [128, 16])
            SKT = sbuf(ph, "f_SKT", [128, 16, 128], BF16)
            S.dma('pool', lambda e: e.dma_start(out=SKT[:, :, :], in_=skT.rearrange("c d k -> d c k")), writes=['SKT'])
            S_ALL = sbuf(ph, "f_SALL", [128, 16, 128])
            S_TMP = sbuf(ph, "f_STMP", [128, 256])
            SV = sbuf(ph, "f_SV", [128, 16, 16])
            SI = sbuf(ph, "f_SI", [128, 16, 16], U32)
            SIF = sbuf(ph, "f_SIF", [128, 16, 16])
            CAND = sbuf(ph, "f_CAND", [128, 16, 16])
            CV = sbuf(ph, "f_CV", [128, 8, 16])
            CI = sbuf(ph, "f_CI", [128, 8, 16], U32)
            HI = sbuf(ph, "f_HI", [128, 128], U32)
            LO = sbuf(ph, "f_LO", [128, 128], U32)
            HIF = sbuf(ph, "f_HIF", [128, 128])
            LOF = sbuf(ph, "f_LOF", [128, 128])
            EQ = sbuf(ph, "f_EQ", [128, 128, 16])
            I1 = sbuf(ph, "f_I1", [128, 128])
            I2 = sbuf(ph, "f_I2", [128, 128])
            IDS = sbuf(ph, "f_IDS", [128, 128])
            NEG = sbuf(ph, "f_NEG", [128, 8])
            GE = sbuf(ph, "f_GE", [128, 8, 16])
            GS = sbuf(ph, "f_GS", [128, 8])
            RS = sbuf(ph, "f_RS", [128, 8])
            GM = sbuf(ph, "f_GM", [128, 8, 16])
            IDST = sbuf(ph, "f_IDST", [128, 128], U32)
            GT = sbuf(ph, "f_GT", [128, 128])
            IOT = sbuf(ph, "f_IOT", [128, 16])
            S.op('pool', lambda e: e.iota(IOT[:], pattern=[[1, 16]], base=0, channel_multiplier=0, allow_small_or_imprecise_dtypes=True), writes=['IOT'])
            HUH = [sbuf(ph, f"f_HU{i}", [128, 4]) for i in range(2)]
            GL = sbuf(ph, "f_GL", [128, 4])
            ZB = [sbuf(ph, f"f_Z{i}", [128, 256], BF16) for i in range(4)]
            for zi in range(4):
                S.op('dve', lambda e, zi=zi: e.memset(ZB[zi][:, :], 0.0), writes=[f'Z{zi}'])
            Xps = psum(ph, "f_Xps", [128, D])
            Yps = psum(ph, "f_Yps", [128, D])
            BANK = [(Xps[:, j * 512:(j + 1) * 512], f'bX{j}') for j in range(4)] + [(Yps[:, j * 512:(j + 1) * 512], f'bY{j}') for j in range(4)]
            XK = [f'bX{j}' for j in range(4)]
            YK = [f'bY{j}' for j in range(4)]
            PTs4 = [Xps[:, 0:512].bitcast(BF16), Xps[:, 512:1024].bitcast(BF16)]
            rot = {'b': 0, 'wg': 0}

            def nbank():
                r = BANK[rot['b'] % 8]
                rot['b'] += 1
                return r

            def load_w(dst, key, T, g):
                S.dma('sp', lambda e: e.dma_start(out=dst, in_=T[g]), writes=[key])

            def load_wg(T, g):
                i = rot['wg'] % 2
                rot['wg'] += 1
                load_w(WG[i][:, :, :], f'wg{i}', T, g)
                return WG[i], f'wg{i}'

            def transpose4(XB_, XNT_):
                for c in range(16):
                    pt = PTs4[c % 2]; key = f'bX{c % 2}'
                    for b in range(4):
                        S.op('pe', lambda e, b=b, c=c, pt=pt: e.transpose(pt[:, b * 128:(b + 1) * 128], XB_[:, b, c * 128:(c + 1) * 128], identb[:]),
                             reads=['XB', 'identb'], writes=[key])
                    evac_copy(XNT_[:, c, :], pt[:, 0:512], reads=[key], writes=['XNT'])

            def top16(src, skeys, sv, si, okeys, tmp):
                S.op('dve', lambda e: e.max(out=sv[:, 0:8], in_=src), reads=skeys, writes=okeys[:1])
                S.op('dve', lambda e: e.max_index(out=si[:, 0:8], in_max=sv[:, 0:8], in_values=src), reads=skeys + okeys[:1], writes=okeys[1:])
                S.op('dve', lambda e: e.match_replace(out=tmp, in_to_replace=sv[:, 0:8], in_values=src, imm_value=-1e30), reads=skeys + okeys[:1], writes=['STMP'])
                S.op('dve', lambda e: e.max(out=sv[:, 8:16], in_=tmp), reads=['STMP'], writes=okeys[:1])
                S.op('dve', lambda e: e.max_index(out=si[:, 8:16], in_max=sv[:, 8:16], in_values=tmp), reads=['STMP'] + okeys[:1], writes=okeys[1:])

            def peer_block(b):
                for c in range(16):
                    S.op('pe', lambda e, c=c: e.matmul(Xps[:, c * 128:(c + 1) * 128], MT[:, c, b * 128:(b + 1) * 128], SKT[:, c, :], start=True, stop=True),
                         reads=['MT', 'SKT'], writes=[f'bX{c // 4}'])
                S.op('act', lambda e: e.copy(out=S_ALL[:, :, :].rearrange("p c k -> p (c k)"), in_=Xps[:, :]), reads=XK, writes=['SALL'])
                for c in range(16):
                    top16(S_ALL[:, c, :], ['SALL'], SV[:, c, :], SI[:, c, :], ['SV', 'SI'], S_TMP[:, 0:128])
                for h in range(8):
                    S.op('dve', lambda e, h=h: e.tensor_tensor(out=CAND[:, :, :], in0=SV[:, 2 * h, :].unsqueeze(2).to_broadcast([128, 16, 16]),
                                                               in1=SV[:, 2 * h + 1, :].unsqueeze(1).to_broadcast([128, 16, 16]), op=ALU.add),
                         reads=['SV'], writes=['CAND'])
                    top16(CAND[:, :, :].rearrange("p a b -> p (a b)"), ['CAND'], CV[:, h, :], CI[:, h, :], ['CV', 'CI'], S_TMP[:, 0:256])
                CIf = CI[:, :, :].rearrange("p h k -> p (h k)")
                S.op('dve', lambda e: e.tensor_single_scalar(out=HI[:, :], in_=CIf, scalar=4, op=ALU.logical_shift_right), reads=['CI'], writes=['HI'])
                S.op('dve', lambda e: e.tensor_single_scalar(out=LO[:, :], in_=CIf, scalar=15, op=ALU.bitwise_and), reads=['CI'], writes=['LO'])
                S.op('dve', lambda e: e.tensor_copy(out=HIF[:, :], in_=HI[:, :]), reads=['HI'], writes=['HIF'])
                S.op('dve', lambda e: e.tensor_copy(out=LOF[:, :], in_=LO[:, :]), reads=['LO'], writes=['LOF'])
                S.op('dve', lambda e: e.tensor_copy(out=SIF[:, :, :], in_=SI[:, :, :]), reads=['SI'], writes=['SIF'])
                SIF4 = SIF[:, :, :].rearrange("p (h two) a -> p h two a", two=2)
                EQ4 = EQ[:, :, :].rearrange("p (h k) a -> p h k a", h=8)
                for which, (XF, IX, ikey) in enumerate(((HIF, I1, 'I1'), (LOF, I2, 'I2'))):
                    xkey = 'HIF' if which == 0 else 'LOF'
                    S.op('dve', lambda e, XF=XF: e.tensor_tensor(out=EQ[:, :, :], in0=XF[:, :].unsqueeze(2).to_broadcast([128, 128, 16]),
                                                                 in1=IOT[:, :].unsqueeze(1).to_broadcast([128, 128, 16]), op=ALU.is_equal),
                         reads=[xkey, 'IOT'], writes=['EQ'])
                    S.op('dve', lambda e, which=which: e.tensor_tensor(out=EQ4, in0=EQ4, in1=SIF4[:, :, which, :].unsqueeze(2).to_broadcast([128, 8, 16, 16]), op=ALU.mult),
                         reads=['EQ', 'SIF'], writes=['EQ'])
                    S.op('dve', lambda e, IX=IX: e.reduce_sum(out=IX[:, :], in_=EQ[:, :, :], axis=AX.X), reads=['EQ'], writes=[ikey])
                S.op('dve', lambda e: e.scalar_tensor_tensor(out=IDS[:, :], in0=I1[:, :], scalar=128.0, in1=I2[:, :], op0=ALU.mult, op1=ALU.add),
                     reads=['I1', 'I2'], writes=['IDS'])
                S.op('dve', lambda e: e.tensor_scalar(out=NEG[:, :], in0=CV[:, :, 0], scalar1=-1.0, scalar2=None, op0=ALU.mult), reads=['CV'], writes=['NEG'])
                for h in range(8):
                    S.op('act', lambda e, h=h: e.activation(out=GE[:, h, :], in_=CV[:, h, :], func=AF.Exp, bias=NEG[:, h:h + 1], scale=1.0, accum_out=GS[:, h:h + 1]),
                         reads=['CV', 'NEG'], writes=['GE', 'GS'])
                S.op('dve', lambda e: e.reciprocal(out=RS[:, :], in_=GS[:, :]), reads=['GS'], writes=['RS'])
                S.op('dve', lambda e: e.tensor_tensor(out=GM[:, :, :], in0=GE[:, :, :], in1=RS[:, :].unsqueeze(2).to_broadcast([128, 8, 16]), op=ALU.mult),
                     reads=['GE', 'RS'], writes=['GM'])
                S.op('pe', lambda e: e.transpose(Xps[:, 0:128], IDS[:, :], identf[:]), reads=['IDS', 'identf'], writes=['bX0'])
                S.op('pe', lambda e: e.transpose(Xps[:, 512:640], GM[:, :, :].rearrange("p h k -> p (h k)"), identf[:]), reads=['GM', 'identf'], writes=['bX1'])
                S.op('dve', lambda e: e.tensor_copy(out=IDST[:, :], in_=Xps[:, 0:128]), reads=['bX0'], writes=['IDST'])
                S.op('act', lambda e: e.copy(out=GT[:, :], in_=Xps[:, 512:640]), reads=['bX1'], writes=['GT'])
                def gather(tl):
                    ui = tl % NG
                    S.dma('pool', lambda e: e.indirect_dma_start(out=UB[ui], out_offset=None, in_=UBF,
                                                                 in_offset=bass.IndirectOffsetOnAxis(ap=IDST[:, tl:tl + 1], axis=0)),
                          reads=['IDST'], writes=[f'UB{ui}'])
                    S.dma('pool', lambda e: e.indirect_dma_start(out=VB[ui], out_offset=None, in_=VBF,
                                                                 in_offset=bass.IndirectOffsetOnAxis(ap=IDST[:, tl:tl + 1], axis=0)),
                          reads=['IDST'], writes=[f'VB{ui}'])

                def bcast_half(tl, hf):
                    for c in (2 * hf, 2 * hf + 1):
                        S.op('pe', lambda e, c=c: e.matmul(Xps[:, c * 512:(c + 1) * 512], identb[:, tl:tl + 1].to_broadcast([128, 128]),
                                                           XB[:, b, c * 512:(c + 1) * 512], start=True, stop=True),
                             reads=['XB', 'identb'], writes=[f'bX{c}'])

                for tl in range(NG - 1):
                    gather(tl)
                bcast_half(0, 0)
                bcast_half(0, 1)
                for tl in range(128):
                    ui = tl % NG
                    zi = tl % 4
                    if tl + NG - 1 < 128:
                        gather(tl + NG - 1)
                    for hf in range(2):
                        hs = slice(hf * 1024, (hf + 1) * 1024)
                        S.op('dve', lambda e, ui=ui, zi=zi, hf=hf, hs=hs: e.scalar_tensor_tensor(out=JUNK[:, hs], in0=UB[ui][:, hs], scalar=1.0, in1=Xps[:, hs],
                                                                                             op0=ALU.mult, op1=ALU.mult, accum_out=HUH[hf][:, zi:zi + 1]),
                             reads=[f'UB{ui}', f'bX{2 * hf}', f'bX{2 * hf + 1}'], writes=[f'J{hf}', f'HU{hf}{zi}'])
                        if tl + 1 < 128:
                            bcast_half(tl + 1, hf)
                    S.op('act', lambda e, zi=zi: e.activation(out=GL[:, zi:zi + 1], in_=HUH[0][:, zi:zi + 1], func=AF.Gelu, bias=HUH[1][:, zi:zi + 1], scale=1.0),
                         reads=[f'HU0{zi}', f'HU1{zi}'], writes=[f'GL{zi}'])
                    S.op('act', lambda e, zi=zi, tl=tl: e.activation(out=ZB[zi][:, 128:129], in_=GL[:, zi:zi + 1], func=AF.Copy, scale=GT[:, tl:tl + 1]),
                         reads=[f'GL{zi}', 'GT'], writes=[f'Z{zi}'])
                    for c in range(4):
                        S.op('pe', lambda e, tl=tl, c=c, zi=zi, ui=ui: e.matmul(Yps[:, c * 512:(c + 1) * 512], ZB[zi][:, 128 - tl:256 - tl], VB[ui][:, c * 512:(c + 1) * 512],
                                                                             start=(tl == 0), stop=(tl == 127)),
                             reads=[f'Z{zi}', f'VB{ui}'], writes=[f'bY{c}'])
                S.op('dve', lambda e: e.tensor_tensor(out=XT[:, b, :], in0=XT[:, b, :], in1=Yps[:, :], op=ALU.add), reads=['XT'] + YK, writes=['XT'])

            ntiles = NT if debug != 4 else 1
            for i in range(ntiles):
                t0 = i * TS
                S.dma('sp', lambda e, t0=t0: e.dma_start(out=XT[:, :, :], in_=x_own[t0:t0 + TS, :].rearrange("(b p) d -> p b d", p=128)), writes=['XT'])
                load_gain(G, 0)
                rms_norm_blocks(XT, XB, G, SM)
                transpose4(XB, XNT)
                S.dma('sp', lambda e, t0=t0: e.dma_start(out=YAT, in_=YA[:, t0:t0 + TS].rearrange("(c p) t -> p c t", p=128)), writes=['UB0', 'UB1'])
                S.dma('sp', lambda e, t0=t0: e.dma_start(out=YRT, in_=YR[:, t0:t0 + TS].rearrange("(c p) t -> p c t", p=128)), writes=['UB2', 'UB3'])
                for fg in range(8):
                    load_w(WG[0][:, :, :], 'wg0', WIN2, fg)
                    load_w(WG[1][:, :, :], 'wg1', WIN2, 8 + fg)
                    load_w(WGA[:, :, :], 'wga', WAT, fg)
                    load_w(WGR[:, :, :], 'wgr', WRT, fg)
                    for f2 in range(2):
                        f = fg * 2 + f2
                        cs = slice(f2 * 128, (f2 + 1) * 128)
                        (pa, ka), (pb, kb), (pA, kA), (pR, kR) = nbank(), nbank(), nbank(), nbank()
                        for kc in range(16):
                            S.op('pe', lambda e, kc=kc, pa=pa, cs=cs: e.matmul(pa, WG[0][:, kc, cs], XNT[:, kc, :], start=(kc == 0), stop=(kc == 15)),
                                 reads=['wg0', 'XNT'], writes=[ka])
                        for kc in range(16):
                            S.op('pe', lambda e, kc=kc, pb=pb, cs=cs: e.matmul(pb, WG[1][:, kc, cs], XNT[:, kc, :], start=(kc == 0), stop=(kc == 15)),
                                 reads=['wg1', 'XNT'], writes=[kb])
                        for kc in range(8):
                            S.op('pe', lambda e, kc=kc, pA=pA, cs=cs: e.matmul(pA, WGA[:, kc, cs], YAT[:, kc, :], start=(kc == 0), stop=(kc == 7)),
                                 reads=['wga', 'UB0', 'UB1'], writes=[kA])
                        for kc in range(8):
                            S.op('pe', lambda e, kc=kc, pR=pR, cs=cs: e.matmul(pR, WGR[:, kc, cs], YRT[:, kc, :], start=(kc == 0), stop=(kc == 7)),
                                 reads=['wgr', 'UB2', 'UB3'], writes=[kR])
                        S.op('act', lambda e, pa=pa: e.activation(out=SGA[:, :], in_=pa, func=AF.Sigmoid), reads=[ka], writes=['SGA'])
                        S.op('act', lambda e, pb=pb: e.activation(out=SGB[:, :], in_=pb, func=AF.Sigmoid), reads=[kb], writes=['SGB'])
                        S.op('dve', lambda e, pA=pA: e.tensor_tensor(out=TT[:, :], in0=SGA[:, :], in1=pA, op=ALU.mult), reads=['SGA', kA], writes=['TT'])
                        S.op('dve', lambda e, pR=pR: e.tensor_tensor(out=SGB[:, :], in0=SGB[:, :], in1=pR, op=ALU.mult), reads=['SGB', kR], writes=['SGB'])
                        S.op('pool', lambda e, f=f: e.tensor_tensor(out=MT[:, f, :], in0=TT[:, :], in1=SGB[:, :], op=ALU.add), reads=['TT', 'SGB'], writes=['MT'])
                for cg in range(8):
                    wt, wkey = load_wg(WOT, cg)
                    for b in range(4):
                        pb_, kb_ = nbank()
                        for kc in range(16):
                            S.op('pe', lambda e, kc=kc, b=b, pb_=pb_, wt=wt: e.matmul(pb_[:, 0:256], MT[:, kc, b * 128:(b + 1) * 128], wt[:, kc, :], start=(kc == 0), stop=(kc == 15)),
                                 reads=['MT', wkey], writes=[kb_])
                        S.op('dve', lambda e, b=b, cg=cg, pb_=pb_: e.tensor_tensor(out=XT[:, b, cg * 256:(cg + 1) * 256], in0=XT[:, b, cg * 256:(cg + 1) * 256], in1=pb_[:, 0:256], op=ALU.add),
                             reads=['XT', kb_], writes=['XT'])
                load_gain(G, 1)
                rms_norm_blocks(XT, XB, G, SM)
                transpose4(XB, XNT)
                for cg in range(8):
                    wt, wkey = load_wg(WQT, cg)
                    for c2 in range(2):
                        pb_, kb_ = nbank()
                        for kc in range(16):
                            S.op('pe', lambda e, kc=kc, c2=c2, pb_=pb_, wt=wt: e.matmul(pb_, wt[:, kc, c2 * 128:(c2 + 1) * 128], XNT[:, kc, :], start=(kc == 0), stop=(kc == 15)),
                                 reads=['XNT', wkey], writes=[kb_])
                        evac_copy(MT[:, cg * 2 + c2, :], pb_, [kb_], ['MT'])
                for b in range(4):
                    peer_block(b)
                load_gain(G, 2)
                rms_norm_blocks(XT, XB, G, SM)
                transpose4(XB, XNT)
                S.dma('pool', lambda e, t0=t0: e.dma_start(out=PTt[:, :, :], in_=pT[:, t0:t0 + TS].rearrange("(c p) t -> p c t", p=128)), writes=['PTt'])
                for cg in range(8):
                    wt, wkey = load_wg(WPGT, cg)
                    load_w(WPP[:, :, :], 'wpp', WPPT, cg)
                    for b in range(4):
                        (pg, kg), (pp_, kp) = nbank(), nbank()
                        for kc in range(16):
                            S.op('pe', lambda e, kc=kc, b=b, pg=pg, wt=wt: e.matmul(pg[:, 0:256], XNT[:, kc, b * 128:(b + 1) * 128], wt[:, kc, :], start=(kc == 0), stop=(kc == 15)),
                                 reads=['XNT', wkey], writes=[kg])
                        for kc in range(2):
                            S.op('pe', lambda e, kc=kc, b=b, pp_=pp_: e.matmul(pp_[:, 0:256], PTt[:, kc, b * 128:(b + 1) * 128], WPP[:, kc, :], start=(kc == 0), stop=(kc == 1)),
                                 reads=['PTt', 'wpp'], writes=[kp])
                        S.op('act', lambda e, pg=pg: e.activation(out=SGA[:, 0:256], in_=pg[:, 0:256], func=AF.Sigmoid), reads=[kg], writes=['SGA'])
                        S.op('dve', lambda e, pp_=pp_: e.tensor_tensor(out=TT[:, 0:256], in0=SGA[:, 0:256], in1=pp_[:, 0:256], op=ALU.mult), reads=['SGA', kp], writes=['TT'])
                        S.op('pool', lambda e, b=b, cg=cg: e.tensor_tensor(out=XT[:, b, cg * 256:(cg + 1) * 256], in0=XT[:, b, cg * 256:(cg + 1) * 256], in1=TT[:, 0:256], op=ALU.add),
                             reads=['XT', 'TT'], writes=['XT'])
                load_gain(G, 3)
                rms_norm_blocks(XT, XB, G, SM, stats_only=True)
                for b in range(4):
                    ob, okey = OUTB[b % 2], [f'VB{2 * (b % 2)}', f'VB{2 * (b % 2) + 1}']
                    S.op('dve', lambda e, b=b, ob=ob: e.scalar_tensor_tensor(out=ob, in0=XT[:, b, :], scalar=SM[:, 12 + b:13 + b], in1=G[:, :], op0=ALU.mult, op1=ALU.mult),
                         reads=['XT', 'SM', 'G'], writes=okey)
                    S.dma('act', lambda e, b=b, ob=ob, t0=t0: e.dma_start(out=y_own[t0 + b * 128:t0 + (b + 1) * 128, :], in_=ob), reads=okey)
            S.barrier()
            S.flush()
    return nc


def _na_tables(rpb, base_row, rows, has_prev, has_next):
    H = rpb.shape[0]
    out = np.full((5, H, 128, 640), -1e30, np.float32)
    qc = np.arange(64)
    kc = np.arange(64)
    col_start = np.clip(qc - 8, 0, 48)
    col_ok = (kc[None, :] >= col_start[:, None]) & (kc[None, :] < col_start[:, None] + 16)
    dc_idx = np.clip(kc[None, :] - qc[:, None], -15, 15) + 15
    for ty, P in enumerate((15, 0, 1, 30, 31)):
        for ri in range(2):
            r_seq = base_row + 2 * P + ri
            row_start = int(np.clip(r_seq - 4, 0, rows - 8))
            for c in range(5):
                pair = P - 2 + c
                for rj in range(2):
                    if pair < 0:
                        if has_prev:
                            k_seq = base_row + 2 * pair + rj
                        elif pair == -2:
                            k_seq = base_row + 6 + rj
                        else:
                            continue
                    elif pair > 31:
                        if has_next:
                            k_seq = base_row + 2 * pair + rj
                        elif pair == 33:
                            k_seq = base_row + 56 + rj
                        else:
                            continue
                    else:
                        k_seq = base_row + 2 * pair + rj
                    if not (row_start <= k_seq < row_start + 8):
                        continue
                    dr = k_seq - r_seq + 7
                    blk = rpb[:, dr, :][:, dc_idx]
                    blk = np.where(col_ok[None], blk, np.float32(-1e30))
                    out[ty, :, ri * 64:(ri + 1) * 64, c * 128 + rj * 64: c * 128 + (rj + 1) * 64] = blk
    return out


def prep_inputs(inp):
    f32 = np.float32
    xp = np.asarray(inp['x_prompt'], f32)[0]
    xs = np.asarray(inp['x_sample'], f32)
    pp = np.asarray(inp['p_prompt'], f32)[0, 0]
    psm = np.asarray(inp['p_sample'], f32)[0]
    rpb = np.asarray(inp['na_rpb'], f32)[0]
    conv_w = np.asarray(inp['conv_w'], f32)[0]
    conv_b = np.asarray(inp['conv_b'], f32)[0]
    wa = np.asarray(inp['lru_wa'], f32)[0]; wx = np.asarray(inp['lru_wx'], f32)[0]
    ba = np.asarray(inp['lru_ba'], f32)[0]; bx = np.asarray(inp['lru_bx'], f32)[0]; lam = np.asarray(inp['lru_lambda'], f32)[0]
    shared = {
        'gains': np.ascontiguousarray(np.stack([inp['norm_mix'][0], inp['norm_ffn'][0], inp['norm_ple'][0], inp['final_norm']]).astype(f32)),
        'w_in': np.ascontiguousarray(inp['w_in'][0], f32), 'w_a': np.ascontiguousarray(inp['w_branch_a'][0], f32),
        'w_r': np.ascontiguousarray(inp['w_branch_r'][0], f32), 'w_out': np.ascontiguousarray(inp['w_out'][0], f32),
        'w_q': np.ascontiguousarray(inp['peer_wq'][0], f32), 'w_pg': np.ascontiguousarray(inp['ple_gate_w'][0], f32),
        'w_pp': np.ascontiguousarray(inp['ple_proj_w'][0], f32),
        'skT': np.ascontiguousarray(np.asarray(inp['peer_subkeys'], f32)[0].reshape(16, 128, 128).transpose(0, 2, 1)),
        'peer_u': np.ascontiguousarray(inp['peer_u'][0], f32), 'peer_v': np.ascontiguousarray(inp['peer_v'][0], f32),
        'ident': np.eye(128, dtype=f32),
    }
    zero_tap = np.zeros((1, 1024), f32)
    taps_f = np.concatenate([conv_w, zero_tap], 0)
    taps_b = np.concatenate([zero_tap, conv_w[::-1]], 0)

    def pack_cwb(taps):
        t = np.concatenate([taps, conv_b[None]], 0)
        return t.reshape(6, 8, 128).transpose(2, 1, 0)

    def pack_sm(d):
        t = np.stack([ba[d], bx[d], lam[d]], 0)
        return t.reshape(3, 8, 128).transpose(2, 1, 0)

    maps = []
    for core in range(8):
        m = dict(shared)
        if core < 4:
            c = core; seq = xp; pseq = pp; start = c * NTOK; rows = 256; base_row = 64 * c
            has_prev, has_next = c > 0, c < 3
        else:
            c = None; seq = xs[core - 4]; pseq = psm[core - 4]; start = 0; rows = 64; base_row = 0
            has_prev = has_next = False
        own = seq[start:start + NTOK]
        m['x_own'] = np.ascontiguousarray(own)
        halo = np.zeros((512, D), f32)
        if has_prev:
            halo[0:256] = seq[start - 256:start]
        else:
            halo[0:128] = own[384:512]
        if has_next:
            halo[256:512] = seq[start + NTOK:start + NTOK + 256]
        else:
            halo[384:512] = own[3584:3712]
        m['x_halo'] = halo
        m['pT'] = np.ascontiguousarray(pseq[start:start + NTOK].T)
        m['natab'] = _na_tables(rpb, base_row, rows, has_prev, has_next)
        ext = np.zeros((3, 4100, D), f32)
        fl = np.zeros((128, 16), f32)
        dirs = [0, 0, 0]
        if c is not None:
            padded = np.concatenate([np.zeros((2, D), f32), seq, np.zeros((2, D), f32)], 0)
            slots = [(j, 0) for j in range(c)] + [(j, 1) for j in range(3, c, -1)]
            for s, (j, d) in enumerate(slots):
                a = padded[j * NTOK: j * NTOK + 4100]
                ext[s] = a if d == 0 else a[::-1]
                dirs[s] = d
                fl[:, 3 + s] = 1.0
                if s > 0 and slots[s - 1][1] == d:
                    fl[:, s] = 1.0
            if c > 0:
                fl[:, 6 + c - 1] = 1.0
            if c < 3:
                fl[:, 9 + 2] = 1.0
        m['x_ext'] = ext
        m['flags'] = fl
        udirs = dirs + [0, 1]
        m['cwb'] = np.ascontiguousarray(np.stack([pack_cwb(taps_b if dirs[s] else taps_f) for s in range(3)] + [pack_cwb(taps_f)]))
        m['lwa'] = np.ascontiguousarray(np.stack([wa[d] for d in udirs]))
        m['lwx'] = np.ascontiguousarray(np.stack([wx[d] for d in udirs]))
        m['lsm'] = np.ascontiguousarray(np.stack([pack_sm(d) for d in udirs]))
        maps.append(m)
    return maps


def kernel(**inputs):
    maps = prep_inputs(inputs)
    nc = build_nc()
    res = run_bass_kernel_spmd(nc, maps, core_ids=list(range(8)))
    outs = [np.asarray(r['y_own'], np.float32) for r in res.results]
    y_prompt = np.concatenate(outs[0:4], 0)[None]
    y_sample = np.stack(outs[4:8], 0)
    return (y_prompt, y_sample)
```

```python
import numpy as np
import ml_dtypes
from contextlib import ExitStack
import concourse.bass as bass
import concourse.mybir as mybir
from concourse.bass_utils import run_bass_kernel_spmd

F32 = mybir.dt.float32
BF16 = mybir.dt.bfloat16
U32 = mybir.dt.uint32
I32 = mybir.dt.int32
AF = mybir.ActivationFunctionType
ALU = mybir.AluOpType
AX = mybir.AxisListType


class Sched:
    ENG = ['pe', 'act', 'dve', 'pool', 'sp']

    def __init__(self, nc, es, n_dsem=40):
        self.nc = nc
        self.q = {e: [] for e in self.ENG}
        self.csem = {e: es.enter_context(nc.semaphore(f"c_{e}")) for e in ['pe', 'act', 'dve', 'pool']}
        self.cnt = {e: 0 for e in self.csem}
        self.dsem = [es.enter_context(nc.semaphore(f"d_{i}")) for i in range(n_dsem)]
        self.dval = [0] * n_dsem
        self.dnext = 0
        self.waited = {e: {} for e in self.ENG}
        self.lastw = {}
        self.readers = {}
        self.ninstr = 0

    def _semobj(self, semkey):
        return self.csem[semkey] if isinstance(semkey, str) else self.dsem[semkey]

    def _wait(self, eng, tok):
        semkey, val = tok
        if eng == 'pe' and semkey == 'pe':
            return
        if self.waited[eng].get(semkey, 0) >= val:
            return
        self.waited[eng][semkey] = val
        sem = self._semobj(semkey)
        self.q[eng].append(lambda e, sem=sem, val=val: e.wait_ge(sem, val))
        self.ninstr += 1

    def _deps(self, eng, reads, writes):
        toks = []
        for k in list(reads) + list(writes):
            if k in self.lastw:
                toks.append(self.lastw[k])
        for k in writes:
            for sk, v in self.readers.get(k, {}).items():
                toks.append((sk, v))
        for t in toks:
            self._wait(eng, t)

    def _commit(self, tok, reads, writes):
        for k in writes:
            self.lastw[k] = tok
            self.readers[k] = {}
        for k in reads:
            d = self.readers.setdefault(k, {})
            d[tok[0]] = max(d.get(tok[0], 0), tok[1])

    def op(self, eng, fn, reads=(), writes=()):
        self._deps(eng, reads, writes)
        self.cnt[eng] += 1
        tok = (eng, self.cnt[eng])
        sem = self.csem[eng]
        self.q[eng].append(lambda e, fn=fn, sem=sem: fn(e).then_inc(sem, 1))
        self.ninstr += 1
        self._commit(tok, reads, writes)

    def dma(self, eng, fn, reads=(), writes=()):
        self._deps(eng, reads, writes)
        k = self.dnext
        self.dnext = (self.dnext + 1) % len(self.dsem)
        if self.dval[k] > 0:
            self._wait(eng, (k, self.dval[k]))
        self.dval[k] += 16
        tok = (k, self.dval[k])
        sem = self.dsem[k]
        self.q[eng].append(lambda e, fn=fn, sem=sem: fn(e).then_inc(sem, 16))
        self.ninstr += 1
        self._commit(tok, reads, writes)

    def barrier(self):
        for e in self.ENG:
            for e2 in self.csem:
                if self.cnt[e2] > 0:
                    self._wait(e, (e2, self.cnt[e2]))
            for k in range(len(self.dsem)):
                if self.dval[k] > 0:
                    self._wait(e, (k, self.dval[k]))
        self.lastw = {}
        self.readers = {}

    def flush(self):
        nc = self.nc
        q = self.q
        with nc.Block() as block:
            @block.tensor
            def _(e):
                for f in q['pe']:
                    f(e)

            @block.scalar
            def _(e):
                for f in q['act']:
                    f(e)

            @block.vector
            def _(e):
                for f in q['dve']:
                    f(e)

            @block.gpsimd
            def _(e):
                for f in q['pool']:
                    f(e)

            @block.sync
            def _(e):
                for f in q['sp']:
                    f(e)
        self.q = {e: [] for e in self.ENG}


D = 2048
NTOK = 4096
TS = 512
NT = NTOK // TS
EPS = 1e-6
QSCALE = 128 ** -0.5


def build_nc(debug=False):
    nc = bass.Bass("TRN2", target_bir_lowering=False)

    def din(name, shape, dt=F32):
        return nc.dram_tensor(name, shape, dt, kind="ExternalInput").ap()

    def dscr(name, shape, dt=F32):
        kind = "ExternalOutput" if debug in (1, 2, 3) else "Internal"
        return nc.dram_tensor(name, shape, dt, kind=kind).ap()

    x_own = din("x_own", [NTOK, D])
    x_halo = din("x_halo", [512, D])
    x_ext = din("x_ext", [3, 4100, D])
    pT = din("pT", [256, NTOK])
    gains = din("gains", [4, D])
    w_in = din("w_in", [D, 9216])
    w_a = din("w_a", [1024, D])
    w_r = din("w_r", [1024, D])
    w_out = din("w_out", [D, D])
    w_q = din("w_q", [D, D])
    w_pg = din("w_pg", [D, D])
    w_pp = din("w_pp", [256, D])
    skT = din("skT", [16, 128, 128])
    peer_u = din("peer_u", [16384, D])
    peer_v = din("peer_v", [16384, D])
    natab = din("natab", [5, 8, 128, 640])
    cwb = din("cwb", [4, 128, 8, 6])
    lwa = din("lwa", [5, 8, 128, 128])
    lwx = din("lwx", [5, 8, 128, 128])
    lsm = din("lsm", [5, 128, 8, 3])
    flags = din("flags", [128, 16])
    ident = din("ident", [128, 128])
    y_own = nc.dram_tensor("y_own", [NTOK, D], F32, kind="ExternalOutput").ap()

    QT = dscr("QT", [1024, NTOK], BF16)
    KT = dscr("KT", [1024, NTOK + 512], BF16)
    VV = dscr("VV", [NTOK + 512, 1024], BF16)
    XR = dscr("XR", [1024, 4100])
    XRE = dscr("XRE", [3, 1024, 4100])
    GG = dscr("GG", [1024, NTOK])
    HF = dscr("HF", [1024, NTOK])
    YA = dscr("YA", [1024, NTOK], BF16)
    YR = dscr("YR", [1024, NTOK], BF16)
    UBF = nc.dram_tensor("UBF", [16384, D], BF16).ap()
    WIN1 = nc.dram_tensor("WIN1", [10, 128, 16, 512], BF16).ap()
    WIN2 = nc.dram_tensor("WIN2", [16, 128, 16, 256], BF16).ap()
    WAT = nc.dram_tensor("WAT", [8, 128, 8, 256], BF16).ap()
    WRT = nc.dram_tensor("WRT", [8, 128, 8, 256], BF16).ap()
    WOT = nc.dram_tensor("WOT", [8, 128, 16, 256], BF16).ap()
    WQT = nc.dram_tensor("WQT", [8, 128, 16, 256], BF16).ap()
    WPGT = nc.dram_tensor("WPGT", [8, 128, 16, 256], BF16).ap()
    WPPT = nc.dram_tensor("WPPT", [8, 128, 2, 256], BF16).ap()
    VBF = nc.dram_tensor("VBF", [16384, D], BF16).ap()

    with ExitStack() as es:
        S = Sched(nc, es)

        def sbuf(st, name, shape, dt=F32):
            return st.enter_context(nc.sbuf_tensor(name, shape, dt))

        def psum(st, name, shape, dt=F32):
            return st.enter_context(nc.psum_tensor(name, shape, dt))

        identf = sbuf(es, "identf", [128, 128])
        identb = sbuf(es, "identb", [128, 128], BF16)
        S.dma('sp', lambda e: e.dma_start(out=identf[:], in_=ident), writes=['identf'])
        S.op('dve', lambda e: e.tensor_copy(out=identb[:], in_=identf[:]), reads=['identf'], writes=['identb'])

        cnt = {'evac': 0}

        def evac_copy(out_ap, in_ap, reads, writes, scale=None, func=None):
            cnt['evac'] += 1
            if func is not None:
                S.op('act', lambda e: e.activation(out=out_ap, in_=in_ap, func=func), reads=reads, writes=writes)
            elif scale is not None:
                if cnt['evac'] % 2:
                    S.op('act', lambda e: e.mul(out=out_ap, in_=in_ap, mul=scale) if False else e.activation(out=out_ap, in_=in_ap, func=AF.Copy, scale=scale), reads=reads, writes=writes)
                else:
                    S.op('dve', lambda e: e.tensor_scalar(out=out_ap, in0=in_ap, scalar1=scale, scalar2=None, op0=ALU.mult), reads=reads, writes=writes)
            else:
                if cnt['evac'] % 2:
                    S.op('act', lambda e: e.copy(out=out_ap, in_=in_ap), reads=reads, writes=writes)
                else:
                    S.op('dve', lambda e: e.tensor_copy(out=out_ap, in_=in_ap), reads=reads, writes=writes)

        def load_gain(G, k):
            S.dma('sp', lambda e: e.dma_start(out=G[:], in_=gains[k:k + 1, :].partition_broadcast(128)), writes=['G', 'J0', 'J1'])

        def rms_norm_blocks(XT, XB, G, SM, nblk=4, stats_only=False, xbkey='XB'):
            for b in range(nblk):
                S.op('act', lambda e, b=b: e.activation(out=XB[:, b, :], in_=XT[:, b, :], func=AF.Square, accum_out=SM[:, b:b + 1]),
                     reads=['XT'], writes=[xbkey, 'SM'])
            S.op('dve', lambda e: e.tensor_scalar(out=SM[:, 4:4 + nblk], in0=SM[:, 0:nblk], scalar1=1.0 / D, scalar2=EPS, op0=ALU.mult, op1=ALU.add),
                 reads=['SM'], writes=['SM'])
            S.op('act', lambda e: e.activation(out=SM[:, 8:8 + nblk], in_=SM[:, 4:4 + nblk], func=AF.Sqrt), reads=['SM'], writes=['SM'])
            S.op('dve', lambda e: e.reciprocal(out=SM[:, 12:12 + nblk], in_=SM[:, 8:8 + nblk]), reads=['SM'], writes=['SM'])
            for b in range(0 if stats_only else nblk):
                S.op('dve', lambda e, b=b: e.scalar_tensor_tensor(out=XB[:, b, :], in0=XT[:, b, :], scalar=SM[:, 12 + b:13 + b], in1=G[:],
                                                                  op0=ALU.mult, op1=ALU.mult),
                     reads=['XT', 'SM', 'G'], writes=[xbkey])

        def transpose_blocks(XB, XNT, PTs, nblk=4, xbkey='XB'):
            for c in range(16):
                pt = PTs[c % 2]
                key = f'pt{c % 2}'
                for b in range(nblk):
                    S.op('pe', lambda e, b=b, c=c, pt=pt: e.transpose(pt[:, b * 128:(b + 1) * 128], XB[:, b, c * 128:(c + 1) * 128], identb[:]),
                         reads=[xbkey, 'identb'], writes=[key])
                evac_copy(XNT[:, c, 0:nblk * 128], pt[:, 0:nblk * 128], reads=[key], writes=['XNT'])


        with ExitStack() as ph:
            CV0 = [sbuf(ph, f"p0_CV{i}", [128, 8192], BF16) for i in range(2)]
            p0 = {'k': 0}

            def conv_w(w, c_lo, nr, cchunk, T, cw, g_off=0):
                K = w.shape[0] // 128
                ncols = T.shape[0] * cw - g_off * cw
                wv = w.rearrange("(r p) c -> p r c", p=128)
                Tv = T.rearrange("g p kc j -> p kc g j")
                for r0 in range(0, K, nr):
                    for c0 in range(0, ncols, cchunk):
                        j = p0['k'] % 2
                        p0['k'] += 1
                        ng = cchunk // cw
                        stg = CV0[j][:, 0:nr * cchunk]
                        S.dma('pool', lambda e, r0=r0, c0=c0, stg=stg: e.dma_start(out=stg.rearrange("p (r c) -> p r c", r=nr),
                                                                                 in_=wv[:, r0:r0 + nr, c_lo + c0:c_lo + c0 + cchunk]), writes=[f'cv{j}'])
                        g0 = g_off + c0 // cw
                        for gi in range(ng):
                            S.dma('sp', lambda e, r0=r0, g0=g0, gi=gi, ng=ng, stg=stg: e.dma_start(out=Tv[:, r0:r0 + nr, g0 + gi, :],
                                                                                               in_=stg.rearrange("p (r g j) -> p r g j", r=nr, g=ng)[:, :, gi, :]), reads=[f'cv{j}'])

            conv_w(w_in, 0, 8, 1024, WIN1, 512)
            conv_w(w_in, 5120, 8, 1024, WIN2, 256)
            conv_w(w_a, 0, 8, 1024, WAT, 256)
            conv_w(w_r, 0, 8, 1024, WRT, 256)
            conv_w(w_out, 0, 8, 1024, WOT, 256)
            conv_w(w_q, 0, 8, 1024, WQT, 256)
            conv_w(w_pg, 0, 8, 1024, WPGT, 256)
            conv_w(w_pp, 0, 2, 2048, WPPT, 256)
            S.barrier()
            S.flush()

        with ExitStack() as ph:
            XT = sbuf(ph, "p1_XT", [128, 4, D])
            XB = sbuf(ph, "p1_XB", [128, 4, D], BF16)
            XNT = sbuf(ph, "p1_XNT", [128, 16, TS], BF16)
            WG = [sbuf(ph, f"p1_WG{i}", [128, 16, 512], BF16) for i in range(2)]
            WGX = sbuf(ph, "p1_WGX", [128, 16, 1024], BF16)
            STB = [sbuf(ph, f"p1_STB{i}", [128, 4, 512], BF16) for i in range(2)]
            STF = [sbuf(ph, f"p1_STF{i}", [128, 4, 512]) for i in range(2)]
            G = sbuf(ph, "p1_G", [128, D])
            SM = sbuf(ph, "p1_SM", [128, 16])
            PSB = [psum(ph, f"p1_ps{i}", [128, 512]) for i in range(4)]
            PTf = [psum(ph, f"p1_pt{i}", [128, 512]) for i in range(2)]
            PTs = [p[:, :].bitcast(BF16) for p in PTf]
            load_gain(G, 0)
            st = {'wg': 0, 'ps': 0, 'stb': 0, 'stf': 0, 'cv': 0}
            CVB = [sbuf(ph, f"p1_CV{i}", [128, 2, D], BF16) for i in range(2)]

            def conv_steps(n):
                for _ in range(n):
                    k = st['cv']
                    if k >= 128:
                        return
                    st['cv'] += 1
                    src, dst = (peer_u, UBF) if k < 64 else (peer_v, VBF)
                    kk = k % 64
                    j = k % 2
                    sv = src.rearrange("(r p) d -> p r d", p=128)
                    dv = dst.rearrange("(r p) d -> p r d", p=128)
                    S.dma('pool', lambda e, sv=sv, kk=kk, j=j: e.dma_start(out=CVB[j][:, :, :], in_=sv[:, kk * 2:(kk + 1) * 2, :]), writes=[f'cv{j}'])
                    S.dma('pool', lambda e, dv=dv, kk=kk, j=j: e.dma_start(out=dv[:, kk * 2:(kk + 1) * 2, :], in_=CVB[j][:, :, :]), reads=[f'cv{j}'])

            def load_x(src_ap, nrows):
                nb = (nrows + 127) // 128
                if nrows % 128 == 0:
                    S.dma('sp', lambda e: e.dma_start(out=XT[:, 0:nb, :], in_=src_ap.rearrange("(b p) d -> p b d", p=128)), writes=['XT'])
                else:
                    S.op('dve', lambda e: e.memset(XT[:, 0, :], 0.0), writes=['XT'])
                    S.dma('sp', lambda e: e.dma_start(out=XT[0:nrows, 0, :], in_=src_ap), writes=['XT'])
                return nb

            def fm_group(wt, wkey, wc0, nb, kind, dst_fn):
                ntok = nb * 128
                if kind in ('xr', 'gg'):
                    stg = STF[st['stf'] % 2]; skey = f"stf{st['stf'] % 2}"; st['stf'] += 1
                else:
                    stg = STB[st['stb'] % 2]; skey = f"stb{st['stb'] % 2}"; st['stb'] += 1
                for oc in range(4):
                    ps = PSB[st['ps'] % 4]; pkey = f"ps{st['ps'] % 4}"; st['ps'] += 1
                    for kc in range(16):
                        S.op('pe', lambda e, ps=ps, kc=kc, oc=oc: e.matmul(ps[:, 0:ntok], wt[:, kc, wc0 + oc * 128: wc0 + (oc + 1) * 128], XNT[:, kc, 0:ntok],
                                                                         start=(kc == 0), stop=(kc == 15)),
                             reads=[wkey, 'XNT'], writes=[pkey])
                    if kind == 'q':
                        evac_copy(stg[:, oc, 0:ntok], ps[:, 0:ntok], [pkey], [skey], scale=QSCALE)
                    elif kind == 'gg':
                        evac_copy(stg[:, oc, 0:ntok], ps[:, 0:ntok], [pkey], [skey], func=AF.Gelu)
                    else:
                        evac_copy(stg[:, oc, 0:ntok], ps[:, 0:ntok], [pkey], [skey])
                dst_fn(stg, skey)

            def tm_group(wt, wkey, nb, dst_fn):
                stg = STB[st['stb'] % 2]; skey = f"stb{st['stb'] % 2}"; st['stb'] += 1
                for b in range(nb):
                    ps = PSB[st['ps'] % 4]; pkey = f"ps{st['ps'] % 4}"; st['ps'] += 1
                    for kc in range(16):
                        S.op('pe', lambda e, ps=ps, kc=kc, b=b: e.matmul(ps[:, :], XNT[:, kc, b * 128:(b + 1) * 128], wt[:, kc, 0:512],
                                                                       start=(kc == 0), stop=(kc == 15)),
                             reads=[wkey, 'XNT'], writes=[pkey])
                    evac_copy(stg[:, b, :], ps[:, :], [pkey], [skey])
                dst_fn(stg, skey)

            def load_wg(col0):
                i = st['wg'] % 2; st['wg'] += 1
                S.dma('sp', lambda e: e.dma_start(out=WG[i][:, :, :], in_=WIN1[col0 // 512]), writes=[f'wg{i}'])
                return WG[i], f'wg{i}'

            def fm_store(dst, r0, t0, ntok):
                def f(stg, skey):
                    S.dma('act', lambda e: e.dma_start(out=dst[r0:r0 + 512, t0:t0 + ntok].rearrange("(c p) t -> p c t", p=128), in_=stg[:, :, 0:ntok]),
                          reads=[skey])
                return f

            def tm_store(dst, t0, c0, nb):
                def f(stg, skey):
                    S.dma('act', lambda e: e.dma_start(out=dst[t0:t0 + nb * 128, c0:c0 + 512].rearrange("(b p) c -> p b c", p=128), in_=stg[:, 0:nb, :]),
                          reads=[skey])
                return f

            def halo_k_store(half):
                def f(stg, skey):
                    S.dma('act', lambda e: e.dma_start(out=KT[half:half + 512, 0:256].rearrange("(c p) t -> p c t", p=128), in_=stg[:, :, 0:256]),
                          reads=[skey])
                    S.dma('act', lambda e: e.dma_start(out=KT[half:half + 512, 256 + NTOK:512 + NTOK].rearrange("(c p) t -> p c t", p=128), in_=stg[:, :, 256:512]),
                          reads=[skey])
                return f

            def halo_v_store(half):
                def f(stg, skey):
                    S.dma('act', lambda e: e.dma_start(out=VV[0:256, half:half + 512].rearrange("(b p) c -> p b c", p=128), in_=stg[:, 0:2, :]),
                          reads=[skey])
                    S.dma('act', lambda e: e.dma_start(out=VV[256 + NTOK:512 + NTOK, half:half + 512].rearrange("(b p) c -> p b c", p=128), in_=stg[:, 2:4, :]),
                          reads=[skey])
                return f

            def halo_xr_store(half):
                def f(stg, skey):
                    S.dma('act', lambda e: e.dma_start(out=XR[half:half + 512, 0:2].rearrange("(c p) t -> p c t", p=128), in_=stg[:, :, 254:256]),
                          reads=[skey])
                    S.dma('act', lambda e: e.dma_start(out=XR[half:half + 512, 4098:4100].rearrange("(c p) t -> p c t", p=128), in_=stg[:, :, 256:258]),
                          reads=[skey])
                return f


            XB2 = [XB, sbuf(ph, "p1_XBb", [128, 4, D], BF16)]
            tiles = [('own', i) for i in range(NT)] + [('halo', 0)] + [('ext', s, i) for s in range(3) for i in range(9)]

            def tile_src(tl):
                if tl[0] == 'own':
                    return x_own[tl[1] * TS:(tl[1] + 1) * TS, :], TS
                if tl[0] == 'halo':
                    return x_halo[:, :], 512
                s_, i_ = tl[1], tl[2]
                nrows = TS if i_ < 8 else 4
                return x_ext[s_, i_ * TS:i_ * TS + nrows, :], nrows

            def prep(k):
                src, nrows = tile_src(tiles[k])
                conv_steps(4)
                nb = load_x(src, nrows)
                rms_norm_blocks(XT, XB2[k % 2], G, SM, nblk=nb, xbkey=f'XB{k % 2}')
                return nb

            nbs = {0: prep(0)}
            for k, tl in enumerate(tiles):
                nb = nbs[k]
                transpose_blocks(XB2[k % 2], XNT, PTs, nblk=nb, xbkey=f'XB{k % 2}')
                if k + 1 < len(tiles):
                    nbs[k + 1] = prep(k + 1)
                if tl[0] == 'own':
                    t0 = tl[1] * TS
                    for g in range(10):
                        wt, wkey = load_wg(g * 512)
                        half = (g % 2) * 512
                        if g < 2:
                            fm_group(wt, wkey, 0, 4, 'q', fm_store(QT, half, t0, TS))
                        elif g < 4:
                            fm_group(wt, wkey, 0, 4, 'k', fm_store(KT, half, 256 + t0, TS))
                        elif g < 6:
                            tm_group(wt, wkey, 4, tm_store(VV, 256 + t0, half, 4))
                        elif g < 8:
                            fm_group(wt, wkey, 0, 4, 'xr', fm_store(XR, half, 2 + t0, TS))
                        else:
                            fm_group(wt, wkey, 0, 4, 'gg', fm_store(GG, half, t0, TS))
                elif tl[0] == 'halo':
                    for g in (2, 3, 4, 5, 6, 7):
                        wt, wkey = load_wg(g * 512)
                        half = (g % 2) * 512
                        if g < 4:
                            fm_group(wt, wkey, 0, 4, 'k', halo_k_store(half))
                        elif g < 6:
                            tm_group(wt, wkey, 4, halo_v_store(half))
                        else:
                            fm_group(wt, wkey, 0, 4, 'xr', halo_xr_store(half))
                    S.dma('sp', lambda e: e.dma_start(out=WGX[:, :, 0:512], in_=WIN1[6]), writes=['wgx'])
                    S.dma('sp', lambda e: e.dma_start(out=WGX[:, :, 512:1024], in_=WIN1[7]), writes=['wgx'])
                else:
                    s_, i_ = tl[1], tl[2]
                    t0 = i_ * TS
                    for hh in range(2):
                        if i_ < 8:
                            fm_group(WGX, 'wgx', hh * 512, 4, 'xr', fm_store(XRE[s_], hh * 512, t0, TS))
                        else:
                            fm_group(WGX, 'wgx', hh * 512, 1, 'xr', fm_store(XRE[s_], hh * 512, t0, 4))
            conv_steps(128)
            S.barrier()
            S.flush()
        if debug == 1:
            return nc
        with ExitStack() as ph:
            TAB = sbuf(ph, "na_TAB", [128, 40, 640], BF16)
            for ty in range(5):
                S.dma('pool', lambda e, ty=ty: e.dma_start(out=TAB[:, ty * 8:(ty + 1) * 8, :], in_=natab[ty].rearrange("h q k -> q h k")), writes=['TAB'])
            Qt = [sbuf(ph, f"na_Q{i}", [128, 8, 128], BF16) for i in range(2)]
            Kt = [sbuf(ph, f"na_K{i}", [128, 8, 640], BF16) for i in range(2)]
            Vt = [sbuf(ph, f"na_V{i}", [128, 5, 1024], BF16) for i in range(2)]
            YAo = [sbuf(ph, f"na_YA{i}", [128, 8, 128], BF16) for i in range(2)]
            Pf = [sbuf(ph, f"na_Pf{i}", [128, 640], BF16) for i in range(2)]
            Pn = [sbuf(ph, f"na_Pn{i}", [128, 640], BF16) for i in range(2)]
            PTs_ = [sbuf(ph, f"na_PT{i}", [128, 5, 128], BF16) for i in range(2)]
            NM = sbuf(ph, "na_NM", [128, 8])
            psS = [psum(ph, f"na_psS{i}", [128, 1024]) for i in range(2)]
            psPTf = [psum(ph, f"na_psPT{i}", [128, 512]) for i in range(2)]
            psPT = [p[:, :].bitcast(BF16) for p in psPTf]
            psO = [psum(ph, f"na_psO{i}", [128, 512]) for i in range(2)]
            nau = []
            for P in range(32):
                for h in range(8):
                    nau.append((P, h))

            def naA(k):
                P, h = nau[k]
                j, i = P % 2, k % 2
                ty = {0: 1, 1: 2, 30: 3, 31: 4}.get(P, 0)
                if h == 0:
                    S.dma('sp', lambda e: e.dma_start(out=Qt[j][:, :, :], in_=QT[:, 128 * P:128 * P + 128].rearrange("(h d) t -> d h t", d=128)), writes=[f'Q{j}'])
                    S.dma('sp', lambda e: e.dma_start(out=Kt[j][:, :, :], in_=KT[:, 128 * P:128 * P + 640].rearrange("(h d) t -> d h t", d=128)), writes=[f'K{j}'])
                    S.dma('sp', lambda e: e.dma_start(out=Vt[j][:, :, :], in_=VV[128 * P:128 * P + 640, :].rearrange("(c p) f -> p c f", p=128)), writes=[f'V{j}'])
                sp_, pk = psS[i], f'psS{i}'
                S.op('pe', lambda e: e.matmul(sp_[:, 0:512], Qt[j][:, h, :], Kt[j][:, h, 0:512], start=True, stop=False), reads=[f'Q{j}', f'K{j}'], writes=[pk])
                S.op('pe', lambda e: e.matmul(sp_[:, 0:512], identb[:], TAB[:, ty * 8 + h, 0:512], start=False, stop=True), reads=['TAB', 'identb'], writes=[pk])
                S.op('pe', lambda e: e.matmul(sp_[:, 512:640], Qt[j][:, h, :], Kt[j][:, h, 512:640], start=True, stop=False), reads=[f'Q{j}', f'K{j}'], writes=[pk])
                S.op('pe', lambda e: e.matmul(sp_[:, 512:640], identb[:], TAB[:, ty * 8 + h, 512:640], start=False, stop=True), reads=['TAB', 'identb'], writes=[pk])

            def naB(k):
                i = k % 2
                sp_, pk, nk = psS[i], f'psS{i}', f'NM{i}'
                S.op('dve', lambda e: e.reduce_max(out=NM[:, i:i + 1], in_=sp_[:, 0:640], axis=AX.X, negate=True), reads=[pk], writes=[nk])
                S.op('act', lambda e: e.activation(out=Pf[i][:], in_=sp_[:, 0:640], func=AF.Exp, bias=NM[:, i:i + 1], scale=1.0, accum_out=NM[:, 2 + i:3 + i]),
                     reads=[pk, nk], writes=[f'Pf{i}', f'NS{i}'])
                S.op('dve', lambda e: e.reciprocal(out=NM[:, 4 + i:5 + i], in_=NM[:, 2 + i:3 + i]), reads=[f'NS{i}'], writes=[f'NR{i}'])
                S.op('dve', lambda e: e.tensor_scalar(out=Pn[i][:], in0=Pf[i][:], scalar1=NM[:, 4 + i:5 + i], scalar2=None, op0=ALU.mult),
                     reads=[f'Pf{i}', f'NR{i}'], writes=[f'Pn{i}'])

            def naC(k):
                i = k % 2
                for c in range(5):
                    S.op('pe', lambda e, c=c: e.transpose(psPT[i][:, c * 128:(c + 1) * 128], Pn[i][:, c * 128:(c + 1) * 128], identb[:]),
                         reads=[f'Pn{i}', 'identb'], writes=[f'psPT{i}'])
                evac_copy(PTs_[i][:, :, :].rearrange("p c q -> p (c q)"), psPT[i][:, 0:640], [f'psPT{i}'], [f'PT{i}'])

            def naD(k):
                P, h = nau[k]
                j, i = P % 2, k % 2
                for c in range(5):
                    S.op('pe', lambda e, c=c: e.matmul(psO[i][:, 0:128], Vt[j][:, c, h * 128:(h + 1) * 128], PTs_[i][:, c, :], start=(c == 0), stop=(c == 4)),
                         reads=[f'V{j}', f'PT{i}'], writes=[f'psO{i}'])
                evac_copy(YAo[j][:, h, :], psO[i][:, 0:128], [f'psO{i}'], [f'YA{j}'])
                if h == 7:
                    S.dma('act', lambda e: e.dma_start(out=YA[:, 128 * P:128 * P + 128].rearrange("(h d) t -> d h t", d=128), in_=YAo[j][:, :, :]), reads=[f'YA{j}'])

            nn = len(nau)
            for kk in range(nn + 3):
                if kk < nn:
                    naA(kk)
                if 0 <= kk - 1 < nn:
                    naB(kk - 1)
                if 0 <= kk - 2 < nn:
                    naC(kk - 2)
                if 0 <= kk - 3 < nn:
                    naD(kk - 3)
            S.barrier()
            S.flush()
        if debug == 2:
            return nc
        with ExitStack() as ph:
            WA = sbuf(ph, "l_WA", [128, 5, 8, 128], BF16)
            WX = sbuf(ph, "l_WX", [128, 5, 8, 128], BF16)
            S.dma('pool', lambda e: e.dma_start(out=WA[:, :, :, :], in_=lwa.rearrange("u n c d -> c u n d")), writes=['WA'])
            S.dma('pool', lambda e: e.dma_start(out=WX[:, :, :, :], in_=lwx.rearrange("u n c d -> c u n d")), writes=['WX'])
            LSM = sbuf(ph, "l_LSM", [128, 5, 8, 3])
            S.dma('sp', lambda e: e.dma_start(out=LSM[:, :, :, :], in_=lsm.rearrange("u p n k -> p u n k")), writes=['LSM'])
            CWB = sbuf(ph, "l_CWB", [128, 4, 8, 6])
            S.dma('sp', lambda e: e.dma_start(out=CWB[:, :, :, :], in_=cwb.rearrange("u p n k -> p u n k")), writes=['CWB'])
            FL = sbuf(ph, "l_FL", [128, 16])
            S.dma('sp', lambda e: e.dma_start(out=FL[:, :], in_=flags), writes=['FL'])
            C8 = sbuf(ph, "l_C8", [128, 5, 8])
            E1 = sbuf(ph, "l_E1", [128, 5, 8])
            S.op('act', lambda e: e.activation(out=E1[:, :, :], in_=LSM[:, :, :, 2], func=AF.Exp, scale=-1.0), reads=['LSM'], writes=['E1'])
            S.op('act', lambda e: e.activation(out=E1[:, :, :], in_=E1[:, :, :], func=AF.Ln, bias=1.0, scale=1.0), reads=['E1'], writes=['E1'])
            S.op('dve', lambda e: e.tensor_scalar(out=C8[:, :, :], in0=E1[:, :, :], scalar1=-8.0, scalar2=None, op0=ALU.mult), reads=['E1'], writes=['C8'])
            STt = sbuf(ph, "l_ST", [128, 8])
            ACCF = sbuf(ph, "l_ACCF", [128, 8])
            ACCB = sbuf(ph, "l_ACCB", [128, 8])
            for t_, k_ in ((STt, 'ST'), (ACCF, 'ACCF'), (ACCB, 'ACCB')):
                S.op('dve', lambda e, t_=t_: e.memset(t_[:, :], 0.0), writes=[k_])
            NB = 6
            XRt = [sbuf(ph, f"l_XR{i}", [128, 516]) for i in range(NB)]
            XC = [sbuf(ph, f"l_XC{i}", [128, 512]) for i in range(NB)]
            XCB = [sbuf(ph, f"l_XCB{i}", [128, 512], BF16) for i in range(NB)]
            REC = [sbuf(ph, f"l_REC{i}", [128, 512]) for i in range(NB)]
            INP = [sbuf(ph, f"l_INP{i}", [128, 512]) for i in range(NB)]
            AA = [sbuf(ph, f"l_A{i}", [128, 512]) for i in range(NB)]
            T1 = [sbuf(ph, f"l_T1{i}", [128, 512]) for i in range(NB)]
            UU = [sbuf(ph, f"l_U{i}", [128, 512]) for i in range(NB)]
            HH = [sbuf(ph, f"l_H{i}", [128, 512]) for i in range(NB)]
            HFt = [sbuf(ph, f"l_HF{i}", [128, 512]) for i in range(NB)]
            GGt = [sbuf(ph, f"l_GG{i}", [128, 512]) for i in range(NB)]
            YRt = [sbuf(ph, f"l_YR{i}", [128, 512], BF16) for i in range(NB)]
            psA = [psum(ph, f"l_psA{i}", [128, 512]) for i in range(4)]
            psX = [psum(ph, f"l_psX{i}", [128, 512]) for i in range(4)]
            units = []

            def mk(uc, ug, scr, i, n, state, skey, reverse, valid_ap, vkeys, kind, pre=None, fin=None):
                k = len(units)
                units.append(dict(uc=uc, ug=ug, scr=scr, i=i, n=n, state=state, skey=skey, reverse=reverse, valid_ap=valid_ap, vkeys=vkeys,
                                  kind=kind, pre=pre, fin=fin, j=k % NB, pj=k % 4))

            def stA(u):
                j, pj, n, i, uc, ug = u['j'], u['pj'], u['n'], u['i'], u['uc'], u['ug']
                k = lambda s_: f'{s_}{j}'
                scr = u['scr']
                S.dma('sp', lambda e: e.dma_start(out=XRt[j][:, :], in_=scr[n * 128:(n + 1) * 128, 512 * i:512 * i + 516]), writes=[k('XR')])
                if u['kind'] == 'bwd':
                    S.dma('sp', lambda e: e.dma_start(out=HFt[j][:, :], in_=HF[n * 128:(n + 1) * 128, 512 * i:512 * i + 512]), reads=[f'HF_{i}_{n}'], writes=[k('HFt')])
                    S.dma('sp', lambda e: e.dma_start(out=GGt[j][:, :], in_=GG[n * 128:(n + 1) * 128, 512 * i:512 * i + 512]), writes=[k('GGt')])
                S.op('dve', lambda e: e.tensor_scalar(out=XC[j][:, :], in0=XRt[j][:, 0:512], scalar1=CWB[:, uc, n, 0:1], scalar2=CWB[:, uc, n, 5:6],
                                                      op0=ALU.mult, op1=ALU.add), reads=[k('XR'), 'CWB'], writes=[k('XC')])
                for t in range(1, 5):
                    S.op('dve', lambda e, t=t: e.scalar_tensor_tensor(out=XC[j][:, :], in0=XRt[j][:, t:t + 512], scalar=CWB[:, uc, n, t:t + 1], in1=XC[j][:, :],
                                                                      op0=ALU.mult, op1=ALU.add), reads=[k('XR'), k('XC'), 'CWB'], writes=[k('XC')])
                S.op('act', lambda e: e.copy(out=XCB[j][:, :], in_=XC[j][:, :]), reads=[k('XC')], writes=[k('XCB')])
                S.op('pe', lambda e: e.matmul(psA[pj][:, :], WA[:, ug, n, :], XCB[j][:, :], start=True, stop=True), reads=['WA', k('XCB')], writes=[f'psA{pj}'])
                S.op('pe', lambda e: e.matmul(psX[pj][:, :], WX[:, ug, n, :], XCB[j][:, :], start=True, stop=True), reads=['WX', k('XCB')], writes=[f'psX{pj}'])

            def stB1(u):
                j, pj, n, ug = u['j'], u['pj'], u['n'], u['ug']
                k = lambda s_: f'{s_}{j}'
                S.op('act', lambda e: e.activation(out=REC[j][:, :], in_=psA[pj][:, :], func=AF.Sigmoid, bias=LSM[:, ug, n, 0:1], scale=1.0),
                     reads=[f'psA{pj}', 'LSM'], writes=[k('REC')])
                S.op('act', lambda e: e.activation(out=INP[j][:, :], in_=psX[pj][:, :], func=AF.Sigmoid, bias=LSM[:, ug, n, 1:2], scale=1.0),
                     reads=[f'psX{pj}', 'LSM'], writes=[k('INP')])
                S.op('act', lambda e: e.activation(out=AA[j][:, :], in_=REC[j][:, :], func=AF.Exp, scale=C8[:, ug, n:n + 1]),
                     reads=[k('REC'), 'C8'], writes=[k('A')])
                S.op('pool', lambda e: e.tensor_tensor(out=T1[j][:, :], in0=AA[j][:, :], in1=AA[j][:, :], op=ALU.mult), reads=[k('A')], writes=[k('T1')])
                S.op('pool', lambda e: e.tensor_tensor(out=UU[j][:, :], in0=INP[j][:, :], in1=XC[j][:, :], op=ALU.mult), reads=[k('INP'), k('XC')], writes=[k('U')])

            def stB2(u):
                j = u['j']
                k = lambda s_: f'{s_}{j}'
                S.op('act', lambda e: e.activation(out=T1[j][:, :], in_=T1[j][:, :], func=AF.Sqrt, scale=-1.0, bias=1.0), reads=[k('T1')], writes=[k('T1')])
                S.op('pool', lambda e: e.tensor_tensor(out=UU[j][:, :], in0=T1[j][:, :], in1=UU[j][:, :], op=ALU.mult), reads=[k('T1'), k('U')], writes=[k('U')])

            def stC(u):
                j, n, i = u['j'], u['n'], u['i']
                k = lambda s_: f'{s_}{j}'
                state, skey = u['state'], u['skey']
                if u['pre'] is not None:
                    u['pre']()
                sk = f'{skey}{n}'
                if not u['reverse']:
                    S.op('dve', lambda e: e.tensor_tensor_scan(out=HH[j][:, :], data0=AA[j][:, :], data1=UU[j][:, :], initial=state[:, n:n + 1],
                                                               op0=ALU.mult, op1=ALU.add), reads=[k('A'), k('U'), sk, skey], writes=[k('H')])
                    S.op('act', lambda e: e.copy(out=state[:, n:n + 1], in_=HH[j][:, 511:512]), reads=[k('H')], writes=[sk])
                else:
                    S.op('dve', lambda e: e.tensor_tensor_scan(out=HH[j][:, ::-1], data0=AA[j][:, ::-1], data1=UU[j][:, ::-1], initial=state[:, n:n + 1],
                                                               op0=ALU.mult, op1=ALU.add), reads=[k('A'), k('U'), sk, skey], writes=[k('H')])
                    S.op('act', lambda e: e.copy(out=state[:, n:n + 1], in_=HH[j][:, 0:1]), reads=[k('H')], writes=[sk])
                if u['kind'] == 'fwd':
                    S.dma('act', lambda e: e.dma_start(out=HF[n * 128:(n + 1) * 128, 512 * i:512 * i + 512], in_=HH[j][:, :]), reads=[k('H')], writes=[f'HF_{i}_{n}'])
                elif u['kind'] == 'bwd':
                    S.op('pool', lambda e: e.tensor_tensor(out=HFt[j][:, :], in0=HFt[j][:, :], in1=HH[j][:, :], op=ALU.add), reads=[k('HFt'), k('H')], writes=[k('HFt')])
                    S.op('pool', lambda e: e.tensor_tensor(out=YRt[j][:, :], in0=HFt[j][:, :], in1=GGt[j][:, :], op=ALU.mult), reads=[k('HFt'), k('GGt')], writes=[k('YR')])
                    S.dma('pool', lambda e: e.dma_start(out=YR[n * 128:(n + 1) * 128, 512 * i:512 * i + 512], in_=YRt[j][:, :]), reads=[k('YR')])
                if u['fin'] is not None:
                    u['fin']()

            ST_ALL = ['ST'] + [f'ST{n}' for n in range(8)]

            def slot_pre(s):
                def f():
                    S.op('dve', lambda e: e.tensor_scalar(out=STt[:, :], in0=STt[:, :], scalar1=FL[:, s:s + 1], scalar2=None, op0=ALU.mult),
                         reads=ST_ALL + ['FL'], writes=ST_ALL)
                return f

            def slot_fin(s):
                def f():
                    S.op('dve', lambda e: e.scalar_tensor_tensor(out=ACCF[:, :], in0=STt[:, :], scalar=FL[:, 6 + s:7 + s], in1=ACCF[:, :], op0=ALU.mult, op1=ALU.add),
                         reads=ST_ALL + ['FL', 'ACCF'], writes=['ACCF'])
                    S.op('dve', lambda e: e.scalar_tensor_tensor(out=ACCB[:, :], in0=STt[:, :], scalar=FL[:, 9 + s:10 + s], in1=ACCB[:, :], op0=ALU.mult, op1=ALU.add),
                         reads=ST_ALL + ['FL', 'ACCB'], writes=['ACCB'])
                return f

            for s in range(3):
                for i in range(8):
                    for n in range(8):
                        mk(s, s, XRE[s], i, n, STt, 'ST', False, FL[:, 3 + s:4 + s], ['FL'], 'ext',
                           pre=slot_pre(s) if (i == 0 and n == 0) else None, fin=slot_fin(s) if (i == 7 and n == 7) else None)
            for i in range(8):
                for n in range(8):
                    mk(3, 3, XR, i, n, ACCF, 'ACCF', False, 1.0, [], 'fwd')
            for i in range(7, -1, -1):
                for n in range(8):
                    mk(3, 4, XR, i, n, ACCB, 'ACCB', True, 1.0, [], 'bwd')
            nu = len(units)
            for kk in range(nu + 3):
                if kk < nu:
                    stA(units[kk])
                if 0 <= kk - 1 < nu:
                    stB1(units[kk - 1])
                if 0 <= kk - 2 < nu:
                    stB2(units[kk - 2])
                if 0 <= kk - 3 < nu:
                    stC(units[kk - 3])
            S.barrier()
            S.flush()
        if debug == 3:
            return nc
        with ExitStack() as ph:
            XT = sbuf(ph, "f_XT", [128, 4, D])
            XB = sbuf(ph, "f_XB", [128, 4, D], BF16)
            XNT = sbuf(ph, "f_XNT", [128, 16, TS], BF16)
            WG = [sbuf(ph, f"f_WG{i}", [128, 16, 256], BF16) for i in range(2)]
            WGA = sbuf(ph, "f_WGA", [128, 8, 256], BF16)
            WGR = sbuf(ph, "f_WGR", [128, 8, 256], BF16)
            WPP = sbuf(ph, "f_WPP", [128, 2, 256], BF16)
            MT = sbuf(ph, "f_MT", [128, 16, TS], BF16)
            NG = 6
            BIG = sbuf(ph, "f_BIG", [128, 2 * NG * D], BF16)
            UB = [BIG[:, j * D:(j + 1) * D] for j in range(NG)]
            VB = [BIG[:, (NG + j) * D:(NG + 1 + j) * D] for j in range(NG)]
            YAT = BIG[:, 0:2 * D].rearrange("p (c t) -> p c t", c=8)
            YRT = BIG[:, 2 * D:4 * D].rearrange("p (c t) -> p c t", c=8)
            OUTB = [BIG[:, (NG + 2 * j) * D:(NG + 2 + 2 * j) * D].bitcast(F32) for j in range(2)]
            G = sbuf(ph, "f_G", [128, D])
            JUNK = G[:, 0:1024].bitcast(BF16)
            SGA = sbuf(ph, "f_SGA", [128, 512])
            SGB = sbuf(ph, "f_SGB", [128, 512])
            TT = sbuf(ph, "f_TT", [128, 512])
            PTt = sbuf(ph, "f_PTt", [128, 2, TS], BF16)
            SM = sbuf(ph, "f_SM", [128, 16])
            SKT = sbuf(ph, "f_SKT", [128, 16, 128], BF16)
            S.dma('pool', lambda e: e.dma_start(out=SKT[:, :, :], in_=skT.rearrange("c d k -> d c k")), writes=['SKT'])
            S_ALL = sbuf(ph, "f_SALL", [128, 16, 128])
            S_TMP = sbuf(ph, "f_STMP", [128, 256])
            SV = sbuf(ph, "f_SV", [128, 16, 16])
            SI = sbuf(ph, "f_SI", [128, 16, 16], U32)
            SIF = sbuf(ph, "f_SIF", [128, 16, 16])
            CAND = sbuf(ph, "f_CAND", [128, 16, 16])
            CV = sbuf(ph, "f_CV", [128, 8, 16])
            CI = sbuf(ph, "f_CI", [128, 8, 16], U32)
            HI = sbuf(ph, "f_HI", [128, 128], U32)
            LO = sbuf(ph, "f_LO", [128, 128], U32)
            HIF = sbuf(ph, "f_HIF", [128, 128])
            LOF = sbuf(ph, "f_LOF", [128, 128])
            EQ = sbuf(ph, "f_EQ", [128, 128, 16])
            I1 = sbuf(ph, "f_I1", [128, 128])
            I2 = sbuf(ph, "f_I2", [128, 128])
            IDS = sbuf(ph, "f_IDS", [128, 128])
            NEG = sbuf(ph, "f_NEG", [128, 8])
            GE = sbuf(ph, "f_GE", [128, 8, 16])
            GS = sbuf(ph, "f_GS", [128, 8])
            RS = sbuf(ph, "f_RS", [128, 8])
            GM = sbuf(ph, "f_GM", [128, 8, 16])
            IDST = sbuf(ph, "f_IDST", [128, 128], U32)
            GT = sbuf(ph, "f_GT", [128, 128])
            IOT = sbuf(ph, "f_IOT", [128, 16])
            S.op('pool', lambda e: e.iota(IOT[:], pattern=[[1, 16]], base=0, channel_multiplier=0, allow_small_or_imprecise_dtypes=True), writes=['IOT'])
            HUH = [sbuf(ph, f"f_HU{i}", [128, 4]) for i in range(2)]
            GL = sbuf(ph, "f_GL", [128, 4])
            ZB = [sbuf(ph, f"f_Z{i}", [128, 256], BF16) for i in range(4)]
            for zi in range(4):
                S.op('dve', lambda e, zi=zi: e.memset(ZB[zi][:, :], 0.0), writes=[f'Z{zi}'])
            Xps = psum(ph, "f_Xps", [128, D])
            Yps = psum(ph, "f_Yps", [128, D])
            BANK = [(Xps[:, j * 512:(j + 1) * 512], f'bX{j}') for j in range(4)] + [(Yps[:, j * 512:(j + 1) * 512], f'bY{j}') for j in range(4)]
            XK = [f'bX{j}' for j in range(4)]
            YK = [f'bY{j}' for j in range(4)]
            PTs4 = [Xps[:, 0:512].bitcast(BF16), Xps[:, 512:1024].bitcast(BF16)]
            rot = {'b': 0, 'wg': 0}

            def nbank():
                r = BANK[rot['b'] % 8]
                rot['b'] += 1
                return r

            def load_w(dst, key, T, g):
                S.dma('sp', lambda e: e.dma_start(out=dst, in_=T[g]), writes=[key])

            def load_wg(T, g):
                i = rot['wg'] % 2
                rot['wg'] += 1
                load_w(WG[i][:, :, :], f'wg{i}', T, g)
                return WG[i], f'wg{i}'

            def transpose4(XB_, XNT_):
                for c in range(16):
                    pt = PTs4[c % 2]; key = f'bX{c % 2}'
                    for b in range(4):
                        S.op('pe', lambda e, b=b, c=c, pt=pt: e.transpose(pt[:, b * 128:(b + 1) * 128], XB_[:, b, c * 128:(c + 1) * 128], identb[:]),
                             reads=['XB', 'identb'], writes=[key])
                    evac_copy(XNT_[:, c, :], pt[:, 0:512], reads=[key], writes=['XNT'])

            def top16(src, skeys, sv, si, okeys, tmp):
                S.op('dve', lambda e: e.max(out=sv[:, 0:8], in_=src), reads=skeys, writes=okeys[:1])
                S.op('dve', lambda e: e.max_index(out=si[:, 0:8], in_max=sv[:, 0:8], in_values=src), reads=skeys + okeys[:1], writes=okeys[1:])
                S.op('dve', lambda e: e.match_replace(out=tmp, in_to_replace=sv[:, 0:8], in_values=src, imm_value=-1e30), reads=skeys + okeys[:1], writes=['STMP'])
                S.op('dve', lambda e: e.max(out=sv[:, 8:16], in_=tmp), reads=['STMP'], writes=okeys[:1])
                S.op('dve', lambda e: e.max_index(out=si[:, 8:16], in_max=sv[:, 8:16], in_values=tmp), reads=['STMP'] + okeys[:1], writes=okeys[1:])

            def peer_block(b):
                for c in range(16):
                    S.op('pe', lambda e, c=c: e.matmul(Xps[:, c * 128:(c + 1) * 128], MT[:, c, b * 128:(b + 1) * 128], SKT[:, c, :], start=True, stop=True),
                         reads=['MT', 'SKT'], writes=[f'bX{c // 4}'])
                S.op('act', lambda e: e.copy(out=S_ALL[:, :, :].rearrange("p c k -> p (c k)"), in_=Xps[:, :]), reads=XK, writes=['SALL'])
                for c in range(16):
                    top16(S_ALL[:, c, :], ['SALL'], SV[:, c, :], SI[:, c, :], ['SV', 'SI'], S_TMP[:, 0:128])
                for h in range(8):
                    S.op('dve', lambda e, h=h: e.tensor_tensor(out=CAND[:, :, :], in0=SV[:, 2 * h, :].unsqueeze(2).to_broadcast([128, 16, 16]),
                                                               in1=SV[:, 2 * h + 1, :].unsqueeze(1).to_broadcast([128, 16, 16]), op=ALU.add),
                         reads=['SV'], writes=['CAND'])
                    top16(CAND[:, :, :].rearrange("p a b -> p (a b)"), ['CAND'], CV[:, h, :], CI[:, h, :], ['CV', 'CI'], S_TMP[:, 0:256])
                CIf = CI[:, :, :].rearrange("p h k -> p (h k)")
                S.op('dve', lambda e: e.tensor_single_scalar(out=HI[:, :], in_=CIf, scalar=4, op=ALU.logical_shift_right), reads=['CI'], writes=['HI'])
                S.op('dve', lambda e: e.tensor_single_scalar(out=LO[:, :], in_=CIf, scalar=15, op=ALU.bitwise_and), reads=['CI'], writes=['LO'])
                S.op('dve', lambda e: e.tensor_copy(out=HIF[:, :], in_=HI[:, :]), reads=['HI'], writes=['HIF'])
                S.op('dve', lambda e: e.tensor_copy(out=LOF[:, :], in_=LO[:, :]), reads=['LO'], writes=['LOF'])
                S.op('dve', lambda e: e.tensor_copy(out=SIF[:, :, :], in_=SI[:, :, :]), reads=['SI'], writes=['SIF'])
                SIF4 = SIF[:, :, :].rearrange("p (h two) a -> p h two a", two=2)
                EQ4 = EQ[:, :, :].rearrange("p (h k) a -> p h k a", h=8)
                for which, (XF, IX, ikey) in enumerate(((HIF, I1, 'I1'), (LOF, I2, 'I2'))):
                    xkey = 'HIF' if which == 0 else 'LOF'
                    S.op('dve', lambda e, XF=XF: e.tensor_tensor(out=EQ[:, :, :], in0=XF[:, :].unsqueeze(2).to_broadcast([128, 128, 16]),
                                                                 in1=IOT[:, :].unsqueeze(1).to_broadcast([128, 128, 16]), op=ALU.is_equal),
                         reads=[xkey, 'IOT'], writes=['EQ'])
                    S.op('dve', lambda e, which=which: e.tensor_tensor(out=EQ4, in0=EQ4, in1=SIF4[:, :, which, :].unsqueeze(2).to_broadcast([128, 8, 16, 16]), op=ALU.mult),
                         reads=['EQ', 'SIF'], writes=['EQ'])
                    S.op('dve', lambda e, IX=IX: e.reduce_sum(out=IX[:, :], in_=EQ[:, :, :], axis=AX.X), reads=['EQ'], writes=[ikey])
                S.op('dve', lambda e: e.scalar_tensor_tensor(out=IDS[:, :], in0=I1[:, :], scalar=128.0, in1=I2[:, :], op0=ALU.mult, op1=ALU.add),
                     reads=['I1', 'I2'], writes=['IDS'])
                S.op('dve', lambda e: e.tensor_scalar(out=NEG[:, :], in0=CV[:, :, 0], scalar1=-1.0, scalar2=None, op0=ALU.mult), reads=['CV'], writes=['NEG'])
                for h in range(8):
                    S.op('act', lambda e, h=h: e.activation(out=GE[:, h, :], in_=CV[:, h, :], func=AF.Exp, bias=NEG[:, h:h + 1], scale=1.0, accum_out=GS[:, h:h + 1]),
                         reads=['CV', 'NEG'], writes=['GE', 'GS'])
                S.op('dve', lambda e: e.reciprocal(out=RS[:, :], in_=GS[:, :]), reads=['GS'], writes=['RS'])
                S.op('dve', lambda e: e.tensor_tensor(out=GM[:, :, :], in0=GE[:, :, :], in1=RS[:, :].unsqueeze(2).to_broadcast([128, 8, 16]), op=ALU.mult),
                     reads=['GE', 'RS'], writes=['GM'])
                S.op('pe', lambda e: e.transpose(Xps[:, 0:128], IDS[:, :], identf[:]), reads=['IDS', 'identf'], writes=['bX0'])
                S.op('pe', lambda e: e.transpose(Xps[:, 512:640], GM[:, :, :].rearrange("p h k -> p (h k)"), identf[:]), reads=['GM', 'identf'], writes=['bX1'])
                S.op('dve', lambda e: e.tensor_copy(out=IDST[:, :], in_=Xps[:, 0:128]), reads=['bX0'], writes=['IDST'])
                S.op('act', lambda e: e.copy(out=GT[:, :], in_=Xps[:, 512:640]), reads=['bX1'], writes=['GT'])
                def gather(tl):
                    ui = tl % NG
                    S.dma('pool', lambda e: e.indirect_dma_start(out=UB[ui], out_offset=None, in_=UBF,
                                                                 in_offset=bass.IndirectOffsetOnAxis(ap=IDST[:, tl:tl + 1], axis=0)),
                          reads=['IDST'], writes=[f'UB{ui}'])
                    S.dma('pool', lambda e: e.indirect_dma_start(out=VB[ui], out_offset=None, in_=VBF,
                                                                 in_offset=bass.IndirectOffsetOnAxis(ap=IDST[:, tl:tl + 1], axis=0)),
                          reads=['IDST'], writes=[f'VB{ui}'])

                def bcast_half(tl, hf):
                    for c in (2 * hf, 2 * hf + 1):
                        S.op('pe', lambda e, c=c: e.matmul(Xps[:, c * 512:(c + 1) * 512], identb[:, tl:tl + 1].to_broadcast([128, 128]),
                                                           XB[:, b, c * 512:(c + 1) * 512], start=True, stop=True),
                             reads=['XB', 'identb'], writes=[f'bX{c}'])

                for tl in range(NG - 1):
                    gather(tl)
                bcast_half(0, 0)
                bcast_half(0, 1)
                for tl in range(128):
                    ui = tl % NG
                    zi = tl % 4
                    if tl + NG - 1 < 128:
                        gather(tl + NG - 1)
                    for hf in range(2):
                        hs = slice(hf * 1024, (hf + 1) * 1024)
                        S.op('dve', lambda e, ui=ui, zi=zi, hf=hf, hs=hs: e.scalar_tensor_tensor(out=JUNK[:, hs], in0=UB[ui][:, hs], scalar=1.0, in1=Xps[:, hs],
                                                                                             op0=ALU.mult, op1=ALU.mult, accum_out=HUH[hf][:, zi:zi + 1]),
                             reads=[f'UB{ui}', f'bX{2 * hf}', f'bX{2 * hf + 1}'], writes=[f'J{hf}', f'HU{hf}{zi}'])
                        if tl + 1 < 128:
                            bcast_half(tl + 1, hf)
                    S.op('act', lambda e, zi=zi: e.activation(out=GL[:, zi:zi + 1], in_=HUH[0][:, zi:zi + 1], func=AF.Gelu, bias=HUH[1][:, zi:zi + 1], scale=1.0),
                         reads=[f'HU0{zi}', f'HU1{zi}'], writes=[f'GL{zi}'])
                    S.op('act', lambda e, zi=zi, tl=tl: e.activation(out=ZB[zi][:, 128:129], in_=GL[:, zi:zi + 1], func=AF.Copy, scale=GT[:, tl:tl + 1]),
                         reads=[f'GL{zi}', 'GT'], writes=[f'Z{zi}'])
                    for c in range(4):
                        S.op('pe', lambda e, tl=tl, c=c, zi=zi, ui=ui: e.matmul(Yps[:, c * 512:(c + 1) * 512], ZB[zi][:, 128 - tl:256 - tl], VB[ui][:, c * 512:(c + 1) * 512],
                                                                             start=(tl == 0), stop=(tl == 127)),
                             reads=[f'Z{zi}', f'VB{ui}'], writes=[f'bY{c}'])
                S.op('dve', lambda e: e.tensor_tensor(out=XT[:, b, :], in0=XT[:, b, :], in1=Yps[:, :], op=ALU.add), reads=['XT'] + YK, writes=['XT'])

            ntiles = NT if debug != 4 else 1
            for i in range(ntiles):
                t0 = i * TS
                S.dma('sp', lambda e, t0=t0: e.dma_start(out=XT[:, :, :], in_=x_own[t0:t0 + TS, :].rearrange("(b p) d -> p b d", p=128)), writes=['XT'])
                load_gain(G, 0)
                rms_norm_blocks(XT, XB, G, SM)
                transpose4(XB, XNT)
                S.dma('sp', lambda e, t0=t0: e.dma_start(out=YAT, in_=YA[:, t0:t0 + TS].rearrange("(c p) t -> p c t", p=128)), writes=['UB0', 'UB1'])
                S.dma('sp', lambda e, t0=t0: e.dma_start(out=YRT, in_=YR[:, t0:t0 + TS].rearrange("(c p) t -> p c t", p=128)), writes=['UB2', 'UB3'])
                for fg in range(8):
                    load_w(WG[0][:, :, :], 'wg0', WIN2, fg)
                    load_w(WG[1][:, :, :], 'wg1', WIN2, 8 + fg)
                    load_w(WGA[:, :, :], 'wga', WAT, fg)
                    load_w(WGR[:, :, :], 'wgr', WRT, fg)
                    for f2 in range(2):
                        f = fg * 2 + f2
                        cs = slice(f2 * 128, (f2 + 1) * 128)
                        (pa, ka), (pb, kb), (pA, kA), (pR, kR) = nbank(), nbank(), nbank(), nbank()
                        for kc in range(16):
                            S.op('pe', lambda e, kc=kc, pa=pa, cs=cs: e.matmul(pa, WG[0][:, kc, cs], XNT[:, kc, :], start=(kc == 0), stop=(kc == 15)),
                                 reads=['wg0', 'XNT'], writes=[ka])
                        for kc in range(16):
                            S.op('pe', lambda e, kc=kc, pb=pb, cs=cs: e.matmul(pb, WG[1][:, kc, cs], XNT[:, kc, :], start=(kc == 0), stop=(kc == 15)),
                                 reads=['wg1', 'XNT'], writes=[kb])
                        for kc in range(8):
                            S.op('pe', lambda e, kc=kc, pA=pA, cs=cs: e.matmul(pA, WGA[:, kc, cs], YAT[:, kc, :], start=(kc == 0), stop=(kc == 7)),
                                 reads=['wga', 'UB0', 'UB1'], writes=[kA])
                        for kc in range(8):
                            S.op('pe', lambda e, kc=kc, pR=pR, cs=cs: e.matmul(pR, WGR[:, kc, cs], YRT[:, kc, :], start=(kc == 0), stop=(kc == 7)),
                                 reads=['wgr', 'UB2', 'UB3'], writes=[kR])
                        S.op('act', lambda e, pa=pa: e.activation(out=SGA[:, :], in_=pa, func=AF.Sigmoid), reads=[ka], writes=['SGA'])
                        S.op('act', lambda e, pb=pb: e.activation(out=SGB[:, :], in_=pb, func=AF.Sigmoid), reads=[kb], writes=['SGB'])
                        S.op('dve', lambda e, pA=pA: e.tensor_tensor(out=TT[:, :], in0=SGA[:, :], in1=pA, op=ALU.mult), reads=['SGA', kA], writes=['TT'])
                        S.op('dve', lambda e, pR=pR: e.tensor_tensor(out=SGB[:, :], in0=SGB[:, :], in1=pR, op=ALU.mult), reads=['SGB', kR], writes=['SGB'])
                        S.op('pool', lambda e, f=f: e.tensor_tensor(out=MT[:, f, :], in0=TT[:, :], in1=SGB[:, :], op=ALU.add), reads=['TT', 'SGB'], writes=['MT'])
                for cg in range(8):
                    wt, wkey = load_wg(WOT, cg)
                    for b in range(4):
                        pb_, kb_ = nbank()
                        for kc in range(16):
                            S.op('pe', lambda e, kc=kc, b=b, pb_=pb_, wt=wt: e.matmul(pb_[:, 0:256], MT[:, kc, b * 128:(b + 1) * 128], wt[:, kc, :], start=(kc == 0), stop=(kc == 15)),
                                 reads=['MT', wkey], writes=[kb_])
                        S.op('dve', lambda e, b=b, cg=cg, pb_=pb_: e.tensor_tensor(out=XT[:, b, cg * 256:(cg + 1) * 256], in0=XT[:, b, cg * 256:(cg + 1) * 256], in1=pb_[:, 0:256], op=ALU.add),
                             reads=['XT', kb_], writes=['XT'])
                load_gain(G, 1)
                rms_norm_blocks(XT, XB, G, SM)
                transpose4(XB, XNT)
                for cg in range(8):
                    wt, wkey = load_wg(WQT, cg)
                    for c2 in range(2):
                        pb_, kb_ = nbank()
                        for kc in range(16):
                            S.op('pe', lambda e, kc=kc, c2=c2, pb_=pb_, wt=wt: e.matmul(pb_, wt[:, kc, c2 * 128:(c2 + 1) * 128], XNT[:, kc, :], start=(kc == 0), stop=(kc == 15)),
                                 reads=['XNT', wkey], writes=[kb_])
                        evac_copy(MT[:, cg * 2 + c2, :], pb_, [kb_], ['MT'])
                for b in range(4):
                    peer_block(b)
                load_gain(G, 2)
                rms_norm_blocks(XT, XB, G, SM)
                transpose4(XB, XNT)
                S.dma('pool', lambda e, t0=t0: e.dma_start(out=PTt[:, :, :], in_=pT[:, t0:t0 + TS].rearrange("(c p) t -> p c t", p=128)), writes=['PTt'])
                for cg in range(8):
                    wt, wkey = load_wg(WPGT, cg)
                    load_w(WPP[:, :, :], 'wpp', WPPT, cg)
                    for b in range(4):
                        (pg, kg), (pp_, kp) = nbank(), nbank()
                        for kc in range(16):
                            S.op('pe', lambda e, kc=kc, b=b, pg=pg, wt=wt: e.matmul(pg[:, 0:256], XNT[:, kc, b * 128:(b + 1) * 128], wt[:, kc, :], start=(kc == 0), stop=(kc == 15)),
                                 reads=['XNT', wkey], writes=[kg])
                        for kc in range(2):
                            S.op('pe', lambda e, kc=kc, b=b, pp_=pp_: e.matmul(pp_[:, 0:256], PTt[:, kc, b * 128:(b + 1) * 128], WPP[:, kc, :], start=(kc == 0), stop=(kc == 1)),
                                 reads=['PTt', 'wpp'], writes=[kp])
                        S.op('act', lambda e, pg=pg: e.activation(out=SGA[:, 0:256], in_=pg[:, 0:256], func=AF.Sigmoid), reads=[kg], writes=['SGA'])
                        S.op('dve', lambda e, pp_=pp_: e.tensor_tensor(out=TT[:, 0:256], in0=SGA[:, 0:256], in1=pp_[:, 0:256], op=ALU.mult), reads=['SGA', kp], writes=['TT'])
                        S.op('pool', lambda e, b=b, cg=cg: e.tensor_tensor(out=XT[:, b, cg * 256:(cg + 1) * 256], in0=XT[:, b, cg * 256:(cg + 1) * 256], in1=TT[:, 0:256], op=ALU.add),
                             reads=['XT', 'TT'], writes=['XT'])
                load_gain(G, 3)
                rms_norm_blocks(XT, XB, G, SM, stats_only=True)
                for b in range(4):
                    ob, okey = OUTB[b % 2], [f'VB{2 * (b % 2)}', f'VB{2 * (b % 2) + 1}']
                    S.op('dve', lambda e, b=b, ob=ob: e.scalar_tensor_tensor(out=ob, in0=XT[:, b, :], scalar=SM[:, 12 + b:13 + b], in1=G[:, :], op0=ALU.mult, op1=ALU.mult),
                         reads=['XT', 'SM', 'G'], writes=okey)
                    S.dma('act', lambda e, b=b, ob=ob, t0=t0: e.dma_start(out=y_own[t0 + b * 128:t0 + (b + 1) * 128, :], in_=ob), reads=okey)
            S.barrier()
            S.flush()
    return nc


def _na_tables(rpb, base_row, rows, has_prev, has_next):
    H = rpb.shape[0]
    out = np.full((5, H, 128, 640), -1e30, np.float32)
    qc = np.arange(64)
    kc = np.arange(64)
    col_start = np.clip(qc - 8, 0, 48)
    col_ok = (kc[None, :] >= col_start[:, None]) & (kc[None, :] < col_start[:, None] + 16)
    dc_idx = np.clip(kc[None, :] - qc[:, None], -15, 15) + 15
    for ty, P in enumerate((15, 0, 1, 30, 31)):
        for ri in range(2):
            r_seq = base_row + 2 * P + ri
            row_start = int(np.clip(r_seq - 4, 0, rows - 8))
            for c in range(5):
                pair = P - 2 + c
                for rj in range(2):
                    if pair < 0:
                        if has_prev:
                            k_seq = base_row + 2 * pair + rj
                        elif pair == -2:
                            k_seq = base_row + 6 + rj
                        else:
                            continue
                    elif pair > 31:
                        if has_next:
                            k_seq = base_row + 2 * pair + rj
                        elif pair == 33:
                            k_seq = base_row + 56 + rj
                        else:
                            continue
                    else:
                        k_seq = base_row + 2 * pair + rj
                    if not (row_start <= k_seq < row_start + 8):
                        continue
                    dr = k_seq - r_seq + 7
                    blk = rpb[:, dr, :][:, dc_idx]
                    blk = np.where(col_ok[None], blk, np.float32(-1e30))
                    out[ty, :, ri * 64:(ri + 1) * 64, c * 128 + rj * 64: c * 128 + (rj + 1) * 64] = blk
    return out


def prep_inputs(inp):
    f32 = np.float32
    xp = np.asarray(inp['x_prompt'], f32)[0]
    xs = np.asarray(inp['x_sample'], f32)
    pp = np.asarray(inp['p_prompt'], f32)[0, 0]
    psm = np.asarray(inp['p_sample'], f32)[0]
    rpb = np.asarray(inp['na_rpb'], f32)[0]
    conv_w = np.asarray(inp['conv_w'], f32)[0]
    conv_b = np.asarray(inp['conv_b'], f32)[0]
    wa = np.asarray(inp['lru_wa'], f32)[0]; wx = np.asarray(inp['lru_wx'], f32)[0]
    ba = np.asarray(inp['lru_ba'], f32)[0]; bx = np.asarray(inp['lru_bx'], f32)[0]; lam = np.asarray(inp['lru_lambda'], f32)[0]
    shared = {
        'gains': np.ascontiguousarray(np.stack([inp['norm_mix'][0], inp['norm_ffn'][0], inp['norm_ple'][0], inp['final_norm']]).astype(f32)),
        'w_in': np.ascontiguousarray(inp['w_in'][0], f32), 'w_a': np.ascontiguousarray(inp['w_branch_a'][0], f32),
        'w_r': np.ascontiguousarray(inp['w_branch_r'][0], f32), 'w_out': np.ascontiguousarray(inp['w_out'][0], f32),
        'w_q': np.ascontiguousarray(inp['peer_wq'][0], f32), 'w_pg': np.ascontiguousarray(inp['ple_gate_w'][0], f32),
        'w_pp': np.ascontiguousarray(inp['ple_proj_w'][0], f32),
        'skT': np.ascontiguousarray(np.asarray(inp['peer_subkeys'], f32)[0].reshape(16, 128, 128).transpose(0, 2, 1)),
        'peer_u': np.ascontiguousarray(inp['peer_u'][0], f32), 'peer_v': np.ascontiguousarray(inp['peer_v'][0], f32),
        'ident': np.eye(128, dtype=f32),
    }
    zero_tap = np.zeros((1, 1024), f32)
    taps_f = np.concatenate([conv_w, zero_tap], 0)
    taps_b = np.concatenate([zero_tap, conv_w[::-1]], 0)

    def pack_cwb(taps):
        t = np.concatenate([taps, conv_b[None]], 0)
        return t.reshape(6, 8, 128).transpose(2, 1, 0)

    def pack_sm(d):
        t = np.stack([ba[d], bx[d], lam[d]], 0)
        return t.reshape(3, 8, 128).transpose(2, 1, 0)

    maps = []
    for core in range(8):
        m = dict(shared)
        if core < 4:
            c = core; seq = xp; pseq = pp; start = c * NTOK; rows = 256; base_row = 64 * c
            has_prev, has_next = c > 0, c < 3
        else:
            c = None; seq = xs[core - 4]; pseq = psm[core - 4]; start = 0; rows = 64; base_row = 0
            has_prev = has_next = False
        own = seq[start:start + NTOK]
        m['x_own'] = np.ascontiguousarray(own)
        halo = np.zeros((512, D), f32)
        if has_prev:
            halo[0:256] = seq[start - 256:start]
        else:
            halo[0:128] = own[384:512]
        if has_next:
            halo[256:512] = seq[start + NTOK:start + NTOK + 256]
        else:
            halo[384:512] = own[3584:3712]
        m['x_halo'] = halo
        m['pT'] = np.ascontiguousarray(pseq[start:start + NTOK].T)
        m['natab'] = _na_tables(rpb, base_row, rows, has_prev, has_next)
        ext = np.zeros((3, 4100, D), f32)
        fl = np.zeros((128, 16), f32)
        dirs = [0, 0, 0]
        if c is not None:
            padded = np.concatenate([np.zeros((2, D), f32), seq, np.zeros((2, D), f32)], 0)
            slots = [(j, 0) for j in range(c)] + [(j, 1) for j in range(3, c, -1)]
            for s, (j, d) in enumerate(slots):
                a = padded[j * NTOK: j * NTOK + 4100]
                ext[s] = a if d == 0 else a[::-1]
                dirs[s] = d
                fl[:, 3 + s] = 1.0
                if s > 0 and slots[s - 1][1] == d:
                    fl[:, s] = 1.0
            if c > 0:
                fl[:, 6 + c - 1] = 1.0
            if c < 3:
                fl[:, 9 + 2] = 1.0
        m['x_ext'] = ext
        m['flags'] = fl
        udirs = dirs + [0, 1]
        m['cwb'] = np.ascontiguousarray(np.stack([pack_cwb(taps_b if dirs[s] else taps_f) for s in range(3)] + [pack_cwb(taps_f)]))
        m['lwa'] = np.ascontiguousarray(np.stack([wa[d] for d in udirs]))
        m['lwx'] = np.ascontiguousarray(np.stack([wx[d] for d in udirs]))
        m['lsm'] = np.ascontiguousarray(np.stack([pack_sm(d) for d in udirs]))
        maps.append(m)
    return maps


def kernel(**inputs):
    maps = prep_inputs(inputs)
    nc = build_nc()
    res = run_bass_kernel_spmd(nc, maps, core_ids=list(range(8)))
    outs = [np.asarray(r['y_own'], np.float32) for r in res.results]
    y_prompt = np.concatenate(outs[0:4], 0)[None]
    y_sample = np.stack(outs[4:8], 0)
    return (y_prompt, y_sample)
```
